# Optimizing a Trainium2 kernel written in Bass

```python
import jax, jax.numpy as jnp
from jax import lax
import numpy as np

D_MODEL = 1024
BATCH = 2
SEQ = 8192
DEPTH = 1

D_CONV = D_MODEL // 2
CONV_W = 3
D_RWKV = D_MODEL // 2
HEAD_SIZE = 64
N_HEADS = D_RWKV // HEAD_SIZE
DECAY_LORA = 64
ICLR_LORA = 64
GATE_LORA = 128
GN_EPS = 64e-5
D_FF = 2816
RMS_EPS = 1e-6

COLS_A = 3 * D_CONV
COLS_B = 3 * D_RWKV + DECAY_LORA + ICLR_LORA + GATE_LORA
COLS_G = 2 * D_MODEL
COLS_IN = COLS_A + COLS_B + COLS_G
SPLIT_B = [D_RWKV, 2 * D_RWKV, 3 * D_RWKV, 3 * D_RWKV + DECAY_LORA, 3 * D_RWKV + DECAY_LORA + ICLR_LORA]

kernel_name = "hybrid_shortconv_rwkv7_macaron"


def rms_norm(x, g):
    xf = x.astype(jnp.float32)
    y = xf * lax.rsqrt(jnp.mean(xf * xf, axis=-1, keepdims=True) + RMS_EPS)
    return (y * g.astype(jnp.float32)).astype(x.dtype)


def swiglu(x, w_gate, w_up, w_down):
    return (jax.nn.silu(x @ w_gate) * (x @ w_up)) @ w_down


def token_shift(p):
    return jnp.pad(p, ((0, 0), (1, 0), (0, 0)))[:, :-1]


def causal_dwconv(u, w):
    return lax.conv_general_dilated(
        u, w[:, None, :].astype(u.dtype), window_strides=(1,),
        padding=[(CONV_W - 1, 0)], dimension_numbers=("NWC", "WIO", "NWC"),
        feature_group_count=u.shape[-1])


def wkv7_scan(r, w, k, v, a, b):
    bsz, _, h, n = r.shape

    def step(S, inp):
        r_t, w_t, k_t, v_t, a_t, b_t = inp
        sa = jnp.einsum("bhvk,bhk->bhv", S, a_t)
        S = S * w_t[:, :, None, :] + sa[..., None] * b_t[:, :, None, :] + v_t[..., None] * k_t[:, :, None, :]
        y = jnp.einsum("bhvk,bhk->bhv", S, r_t)
        return S, y

    xs = tuple(jnp.moveaxis(t, 1, 0) for t in (r, w, k, v, a, b))
    S0 = jnp.zeros((bsz, h, n, n), jnp.float32)
    _, ys = lax.scan(step, S0, xs)
    return jnp.moveaxis(ys, 0, 1)


def setup_inputs(seed: int = 0) -> dict:
    key = jax.random.key(seed)
    ks = iter(jax.random.split(key, 40))
    f32 = jnp.float32

    def nrm(shape, scale):
        return jax.random.normal(next(ks), shape, f32) * scale

    def gain(shape):
        return 1.0 + 0.01 * jax.random.normal(next(ks), shape, f32)

    L = DEPTH
    return {
        "x": jax.random.normal(next(ks), (BATCH, SEQ, D_MODEL), f32),
        "ffn1_norm": gain((L, D_MODEL)),
        "ffn1_w_gate": nrm((L, D_MODEL, D_FF), D_MODEL ** -0.5),
        "ffn1_w_up": nrm((L, D_MODEL, D_FF), D_MODEL ** -0.5),
        "ffn1_w_down": nrm((L, D_FF, D_MODEL), D_FF ** -0.5),
        "mix_norm": gain((L, D_MODEL)),
        "w_in": nrm((L, D_MODEL, COLS_IN), D_MODEL ** -0.5),
        "conv_w": nrm((L, CONV_W, D_CONV), CONV_W ** -0.5),
        "w_out_a": nrm((L, D_CONV, D_MODEL), D_CONV ** -0.5),
        "mu_b": jax.random.uniform(next(ks), (L, COLS_B), f32),
        "w0": nrm((L, D_RWKV), 0.5) - 0.5,
        "w_decay_up": nrm((L, DECAY_LORA, D_RWKV), 0.5 * DECAY_LORA ** -0.5),
        "a0": nrm((L, D_RWKV), 0.1),
        "w_iclr_up": nrm((L, ICLR_LORA, D_RWKV), ICLR_LORA ** -0.5),
        "w_gate_up": nrm((L, GATE_LORA, D_RWKV), GATE_LORA ** -0.5),
        "k_k": 0.85 + nrm((L, D_RWKV), 0.05),
        "k_a": 1.0 + nrm((L, D_RWKV), 0.05),
        "r_k": nrm((L, N_HEADS, HEAD_SIZE), 0.1),
        "ln_x_w": gain((L, D_RWKV)),
        "ln_x_b": nrm((L, D_RWKV), 0.01),
        "w_out_b": nrm((L, D_RWKV, D_MODEL), D_RWKV ** -0.5),
        "w_o": nrm((L, D_MODEL, D_MODEL), D_MODEL ** -0.5),
        "ffn2_norm": gain((L, D_MODEL)),
        "ffn2_w_gate": nrm((L, D_MODEL, D_FF), D_MODEL ** -0.5),
        "ffn2_w_up": nrm((L, D_MODEL, D_FF), D_MODEL ** -0.5),
        "ffn2_w_down": nrm((L, D_FF, D_MODEL), D_FF ** -0.5),
        "final_norm": gain((D_MODEL,)),
    }


def token_mixer(h, w_in, conv_w, w_out_a, mu_b, w0, w_decay_up, a0, w_iclr_up,
                w_gate_up, k_k, k_a, r_k, ln_x_w, ln_x_b, w_out_b, w_o):
    bsz, t, _ = h.shape
    f32 = jnp.float32
    p = h @ w_in
    pa, pb, pg = jnp.split(p, [COLS_A, COLS_A + COLS_B], axis=-1)

    b_gate, c_gate, u = jnp.split(pa, 3, axis=-1)
    y_a = (b_gate * causal_dwconv(c_gate * u, conv_w)) @ w_out_a

    pb = pb + (token_shift(pb) - pb) * mu_b
    r, k, v, xw, xa, xg = jnp.split(pb, SPLIT_B, axis=-1)
    w_log = -jax.nn.softplus(-(w0 + jnp.tanh(xw) @ w_decay_up)) - 0.5
    decay = jnp.exp(-jnp.exp(w_log.astype(f32)))
    iclr = jax.nn.sigmoid(a0 + xa @ w_iclr_up)
    g = jax.nn.sigmoid(xg) @ w_gate_up

    def heads(z):
        return z.astype(f32).reshape(bsz, t, N_HEADS, HEAD_SIZE)

    kk = heads(k * k_k)
    kk = kk / jnp.maximum(jnp.sqrt(jnp.sum(kk * kk, axis=-1, keepdims=True)), 1e-12)
    k = k * (1.0 + (iclr - 1.0) * k_a)
    rh, kh, vh, ah = heads(r), heads(k), heads(v), heads(iclr)
    y = wkv7_scan(rh, heads(decay), kh, vh, -kk, kk * ah)

    mu = jnp.mean(y, axis=-1, keepdims=True)
    var = jnp.mean(jnp.square(y - mu), axis=-1, keepdims=True)
    y = (y - mu) * lax.rsqrt(var + GN_EPS)
    y = (y * ln_x_w.astype(f32).reshape(N_HEADS, HEAD_SIZE)
         + ln_x_b.astype(f32).reshape(N_HEADS, HEAD_SIZE))
    y = y + jnp.sum(rh * kh * r_k.astype(f32), axis=-1, keepdims=True) * vh
    y = y.reshape(bsz, t, D_RWKV).astype(h.dtype)
    y_b = (y * g) @ w_out_b

    g_a, g_b = jnp.split(pg, 2, axis=-1)
    merged = jax.nn.sigmoid(g_a) * y_a + jax.nn.sigmoid(g_b) * y_b
    return merged @ w_o


def reference(x, ffn1_norm, ffn1_w_gate, ffn1_w_up, ffn1_w_down, mix_norm, w_in,
              conv_w, w_out_a, mu_b, w0, w_decay_up, a0, w_iclr_up, w_gate_up,
              k_k, k_a, r_k, ln_x_w, ln_x_b, w_out_b, w_o, ffn2_norm,
              ffn2_w_gate, ffn2_w_up, ffn2_w_down, final_norm):
    for l in range(DEPTH):
        x = x + 0.5 * swiglu(rms_norm(x, ffn1_norm[l]), ffn1_w_gate[l], ffn1_w_up[l], ffn1_w_down[l])
        x = x + token_mixer(rms_norm(x, mix_norm[l]), w_in[l], conv_w[l], w_out_a[l], mu_b[l],
                            w0[l], w_decay_up[l], a0[l], w_iclr_up[l], w_gate_up[l],
                            k_k[l], k_a[l], r_k[l], ln_x_w[l], ln_x_b[l], w_out_b[l], w_o[l])
        x = x + 0.5 * swiglu(rms_norm(x, ffn2_norm[l]), ffn2_w_gate[l], ffn2_w_up[l], ffn2_w_down[l])
    return rms_norm(x, final_norm)
```

```python
import numpy as np
from contextlib import ExitStack
import concourse.bass as bass
import concourse.mybir as mybir
from concourse.bass_utils import run_bass_kernel_spmd

F32 = mybir.dt.float32
BF16 = mybir.dt.bfloat16
AF = mybir.ActivationFunctionType
ALU = mybir.AluOpType

NT = 2048
NTH = 2050
D = 1024
FF = 2816
NSEG = 4
TILES = [(0, 512), (512, 512), (1024, 512), (1536, 512), (2048, 2)]
RMS_EPS = 1e-6
GN_EPS = 64e-5
DECAY_C = -0.6065306597126334

PC_F1N, PC_MXN, PC_F2N, PC_FIN, PC_CONV, PC_MU, PC_W0, PC_A0, PC_KK, PC_KA, PC_RK, PC_LNW, PC_LNB, PC_N = \
    0, 8, 16, 24, 32, 44, 58, 62, 66, 70, 74, 78, 82, 86


DBG = [0]
DUMP = [False]
OPEN = []
SREF = []


class _Stop(Exception):
    pass


DBGN = [0]


def ck(n):
    if DBG[0] == n and SREF:
        if DBGN[0] > 0:
            DBGN[0] -= 1
            return
        SREF[0].enabled = False


class Sched:
    ENG = ['pe', 'act', 'dve', 'pool', 'sp']

    def __init__(self, nc, es):
        self.nc = nc
        self.es = es
        self.ops = {e: [] for e in self.ENG}
        self.sig = {e: 0 for e in self.ENG}
        self.sem = {e: es.enter_context(nc.semaphore("s_" + e)) for e in self.ENG}
        self.seen = {e: {} for e in self.ENG}
        self.last_w = {}
        self.readers = {}
        self.dma_sem = {}
        self.dma_cnt = {}
        self.enabled = True
        SREF.clear()
        SREF.append(self)

    def _deps(self, eng, reads, writes):
        need = {}

        def add(k, v):
            if need.get(k, 0) < v:
                need[k] = v
        for r in reads:
            t = self.last_w.get(r)
            if t is not None:
                add(*t)
        for w in writes:
            t = self.last_w.get(w)
            if t is not None:
                add(*t)
            for k, v in self.readers.get(w, {}).items():
                add(k, v)
        waits = []
        for k, v in need.items():
            if k == 'pe' and eng == 'pe':
                continue
            if self.seen[eng].get(k, 0) >= v:
                continue
            self.seen[eng][k] = v
            waits.append((k, v))
        return waits

    def _commit(self, tok, reads, writes):
        for w in writes:
            self.last_w[w] = tok
            self.readers[w] = {}
        for r in reads:
            d = self.readers.setdefault(r, {})
            if d.get(tok[0], 0) < tok[1]:
                d[tok[0]] = tok[1]

    @staticmethod
    def _is_psum(r):
        return r == 'pst' or (isinstance(r, tuple) and r[0] == 'ps')

    def op(self, eng, fn, reads=(), writes=(), signal=True):
        if not self.enabled:
            return None
        writes = list(writes) + [r for r in reads if self._is_psum(r)]
        waits = self._deps(eng, reads, writes)
        if signal:
            self.sig[eng] += 1
            tok = (eng, self.sig[eng])
        else:
            tok = (eng, self.sig[eng] + 1)
        self.ops[eng].append([waits, fn, (eng, 1) if signal else None])
        self._commit(tok, reads, writes)
        return tok

    def signal_last(self, eng):
        if not self.enabled:
            return
        o = self.ops[eng][-1]
        if o[2] is None:
            o[2] = (eng, 1)
            self.sig[eng] += 1

    def dma(self, eng, fn, reads=(), writes=(), sem=None):
        if not self.enabled:
            return None
        if sem not in self.dma_sem:
            self.dma_sem[sem] = self.es.enter_context(self.nc.semaphore("d_" + str(sem)))
            self.dma_cnt[sem] = 0
        waits = self._deps(eng, reads, writes)
        self.dma_cnt[sem] += 16
        key = ('dma', sem)
        tok = (key, self.dma_cnt[sem])
        self.ops[eng].append([waits, fn, (key, 16)])
        self._commit(tok, reads, writes)
        return tok

    def wait_all(self, eng, toks):
        if not self.enabled:
            return
        waits = []
        for k, v in toks:
            if self.seen[eng].get(k, 0) >= v:
                continue
            self.seen[eng][k] = v
            waits.append((k, v))
        self.ops[eng].append([waits, None, None])

    def barrier(self):
        toks = [(e, self.sig[e]) for e in self.ENG if self.sig[e] > 0]
        toks += [(('dma', s), c) for s, c in self.dma_cnt.items() if c > 0]
        for e in self.ENG:
            self.wait_all(e, toks)
        self.last_w = {}
        self.readers = {}

    def _semh(self, k):
        if isinstance(k, tuple):
            return self.dma_sem[k[1]]
        return self.sem[k]

    def emit(self):
        nc = self.nc
        with nc.Block() as block:
            def replay(name):
                def f(eng):
                    for waits, fn, inc in self.ops[name]:
                        for k, v in waits:
                            eng.wait_ge(self._semh(k), v)
                        if fn is None:
                            continue
                        ins = fn(eng)
                        if inc is not None:
                            ins.then_inc(self._semh(inc[0]), inc[1])
                return f
            block.tensor(replay('pe'))
            block.scalar(replay('act'))
            block.vector(replay('dve'))
            block.gpsimd(replay('pool'))
            block.sync(replay('sp'))


def build_nc(nseg=NSEG, do_ffn1=True, do_mix=True, do_ffn2=True):
    nc = bass.Bass("TRN2", target_bir_lowering=False)

    def din(name, shape):
        return nc.dram_tensor(name, list(shape), F32, kind="ExternalInput").ap()
    x4 = din("x4", [NSEG, NTH, D])
    prm_d = din("prm", [128, PC_N])
    cst_d = din("cst", [128, 128 * 3 + 64 + 256 * 4 + 512])
    w1g, w1u, w1d = din("w1g", [D, FF]), din("w1u", [D, FF]), din("w1d", [FF, D])
    w2g, w2u, w2d = din("w2g", [D, FF]), din("w2u", [D, FF]), din("w2d", [FF, D])
    win = din("win", [D, 5376])
    woa, wob, wo = din("woa", [512, D]), din("wob", [512, D]), din("wo", [D, D])
    wdu, wiu, wgu = din("wdu", [64, 512]), din("wiu", [64, 512]), din("wgu", [128, 512])
    out_d = nc.dram_tensor("out", [NT, D], F32, kind="ExternalOutput").ap()
    dbg_d = nc.dram_tensor("dbg", [16, 128, 512], F32, kind="ExternalOutput").ap() if DUMP[0] else None

    with ExitStack() as es:
        S = Sched(nc, es)

        uid = [0]

        def sb(stack, n, s, d):
            uid[0] += 1
            return stack.enter_context(nc.sbuf_tensor("%s_%d" % (n, uid[0]), s, d))
        xT = sb(es, "xT", [128, 8, NTH], F32)
        xn = sb(es, "xn", [128, 8, NTH], BF16)
        prm = sb(es, "prm_sb", [128, PC_N], F32)
        omka = sb(es, "omka", [128, 4], F32)
        XH = sb(es, "XH", [128, 8, 2], F32)
        epsc = sb(es, "epsc", [128, 2], F32)
        ident = sb(es, "ident", [128, 128], F32)
        ones_b = sb(es, "ones_b", [128, 128], BF16)
        blk_b = sb(es, "blk_b", [128, 128], BF16)
        id2 = sb(es, "id2", [128, 64], BF16)
        id2x4 = sb(es, "id2x4", [128, 256], BF16)
        msu = sb(es, "msu", [128, 256], BF16)
        miu = sb(es, "miu", [128, 256], BF16)
        msl = sb(es, "msl", [128, 256], BF16)
        rmask = sb(es, "rmask", [128, 512], F32)
        ST = [sb(es, "ST%d" % i, [128, 64], F32) for i in range(4)]
        STB = [sb(es, "STB%d" % i, [128, 64], BF16) for i in range(4)]
        PS = [es.enter_context(nc.psum_tensor("ps%d" % i, [128, 512], F32)) for i in range(8)]
        psi = [0]
        nrot = [7]

        def bank():
            b = psi[0] % nrot[0]
            psi[0] += 1
            return PS[b], ('ps', b)

        S.dma('sp', lambda e: e.dma_start(out=prm[:], in_=prm_d), writes=['prm'], sem='c_prm')
        S.dma('sp', lambda e: e.dma_start(out=ident[:], in_=cst_d[:, 0:128]), writes=['ident'], sem='c_ident')
        S.dma('sp', lambda e: e.dma_start(out=rmask[:], in_=cst_d[:, 1472:1984]), writes=['rmask'], sem='c_rmask')
        o = 128
        for t, n in ((ones_b, 128), (blk_b, 128), (id2, 64), (id2x4, 256), (msu, 256), (miu, 256), (msl, 256)):
            S.dma('pool', lambda e, t=t, o=o, n=n: e.dma_start(out=t[:], in_=cst_d[:, o:o + n]), writes=['cst'], sem='c1')
            o += n
        S.op('dve', lambda e: e.tensor_scalar(out=omka[:], in0=prm[:, PC_KA:PC_KA + 4], scalar1=-1.0, scalar2=1.0,
                                              op0=ALU.mult, op1=ALU.add), reads=['prm'], writes=['omka'])
        S.op('dve', lambda e: e.memset(epsc[:, 0:1], RMS_EPS), writes=['epsc'])
        S.op('dve', lambda e: e.tensor_scalar(out=XH[:].rearrange("p a b -> p (a b)"), in0=ident[:, 0:16], scalar1=0.0, scalar2=None, op0=ALU.mult),
             reads=['ident'], writes=['XH'])
        S.op('dve', lambda e: e.memset(epsc[:, 1:2], GN_EPS), writes=['epsc'])
        for i in range(4):
            S.op('dve', lambda e, i=i: e.tensor_scalar(out=ST[i][:], in0=ident[:, 0:64], scalar1=0.0, scalar2=None, op0=ALU.mult),
                 reads=['ident'], writes=[('ST', i)])
            S.op('dve', lambda e, i=i: e.tensor_scalar(out=STB[i][:], in0=ident[:, 0:64], scalar1=0.0, scalar2=None, op0=ALU.mult),
                 reads=['ident'], writes=[('STB', i)])

        def dump(idx, tile_, res, ap=None):
            if DUMP[0]:
                src = ap if ap is not None else tile_[:]
                S.dma('sp', lambda e: e.dma_start(out=dbg_d[idx], in_=src), reads=[res], sem='dbg')

        def rmsnorm_to(dst_fn, gcol, sqs, rss, tiles):
            nb_ = len(sqs)
            st = {}

            def stage_a(tix):
                c0, n = tiles[tix]
                kk_ = tix % nb_
                tmp_sq, nsq = sqs[kk_], ('nsq', kk_)
                for dc in range(8):
                    S.op('act', lambda e, dc=dc, c0=c0, n=n, tmp_sq=tmp_sq: e.activation(out=tmp_sq[:, dc, 0:n], in_=xT[:, dc, c0:c0 + n], func=AF.Square),
                         reads=[('xT', dc, c0)], writes=[nsq])
                pb, pr = bank()
                for dc in range(8):
                    S.op('pe', lambda e, dc=dc, n=n, pb=pb, tmp_sq=tmp_sq: e.matmul(pb[:, 0:n], lhsT=ones_b[:], rhs=tmp_sq[:, dc, 0:n], start=(dc == 0), stop=(dc == 7)),
                         reads=[nsq, 'cst'], writes=[pr], signal=(dc == 7))
                st[tix] = (pb, pr)

            def stage_b(tix):
                c0, n = tiles[tix]
                kk_ = tix % nb_
                tmp_rs, nrs = rss[kk_], ('nrs', kk_)
                pb, pr = st.pop(tix)
                S.op('act', lambda e, n=n, pb=pb, tmp_rs=tmp_rs: e.activation(out=tmp_rs[:, 0:n], in_=pb[:, 0:n], func=AF.Ln, bias=epsc[:, 0:1], scale=1.0 / D),
                     reads=[pr, 'epsc'], writes=[nrs])
                S.op('act', lambda e, n=n, tmp_rs=tmp_rs: e.activation(out=tmp_rs[:, 0:n], in_=tmp_rs[:, 0:n], func=AF.Exp, scale=-0.5), reads=[nrs], writes=[nrs])
                for dc in range(8):
                    oap, ores = dst_fn(dc, c0, n)
                    S.op('dve', lambda e, dc=dc, c0=c0, n=n, oap=oap, tmp_rs=tmp_rs: e.scalar_tensor_tensor(
                        out=oap, in0=xT[:, dc, c0:c0 + n], scalar=prm[:, gcol + dc:gcol + dc + 1], in1=tmp_rs[:, 0:n],
                        op0=ALU.mult, op1=ALU.mult), reads=[('xT', dc, c0), nrs, 'prm'], writes=[ores])

            ahead = 1 if nb_ > 1 else 0
            for tix in range(min(ahead, len(tiles))):
                stage_a(tix)
            for tix in range(len(tiles)):
                if tix + ahead < len(tiles) and ahead:
                    stage_a(tix + ahead)
                elif not ahead:
                    stage_a(tix)
                stage_b(tix)

        def xn_dst(dc, c0, n):
            return xn[:, dc, c0:c0 + n], ('xn', c0)

        def ffn(tag, gcol, wg, wu, wd, tiles):
            with ExitStack() as ph:
                sq = [sb(ph, tag + "sq%d" % i, [128, 8, 512], BF16) for i in range(2)]
                rs = [sb(ph, tag + "rs%d" % i, [128, 512], F32) for i in range(2)]
                sg = [sb(ph, tag + "sg%d" % i, [128, 512], F32) for i in range(2)]
                act = [sb(ph, tag + "act%d" % i, [128, 4, NTH], BF16) for i in range(2)]
                wgb = [sb(ph, tag + "wg%d" % i, [128, 8, 512], BF16) for i in range(2)]
                wub = [sb(ph, tag + "wu%d" % i, [128, 8, 512], BF16) for i in range(2)]
                wdb = [sb(ph, tag + "wd%d" % i, [128, 4, D], BF16) for i in range(2)]
                rmsnorm_to(xn_dst, gcol, sq, rs, tiles)
                wgv = wg.rearrange("(dc p) f -> p dc f", p=128)
                wuv = wu.rearrange("(dc p) f -> p dc f", p=128)
                wdv = wd.rearrange("(j p) d -> p j d", p=128)
                ngr = 6
                sgi = 0
                for g in range(ngr):
                    s = g % 2
                    nf = 4 if g < 5 else 2
                    f0 = g * 512
                    S.dma('pool', lambda e, s=s, f0=f0, nf=nf: e.dma_start(out=wgb[s][:, :, 0:nf * 128], in_=wgv[:, :, f0:f0 + nf * 128]),
                          writes=[(tag, 'wg', s)], sem=tag + 'wg%d' % s)
                    S.dma('pool', lambda e, s=s, f0=f0, nf=nf: e.dma_start(out=wub[s][:, :, 0:nf * 128], in_=wuv[:, :, f0:f0 + nf * 128]),
                          writes=[(tag, 'wu', s)], sem=tag + 'wu%d' % s)
                    S.dma('pool', lambda e, s=s, g=g, nf=nf: e.dma_start(out=wdb[s][:, 0:nf, :], in_=wdv[:, g * 4:g * 4 + nf, :]),
                          writes=[(tag, 'wd', s)], sem=tag + 'wd%d' % s)
                    for j in range(nf):
                        for (c0, n) in tiles:
                            pg, rg = bank()
                            pu, ru = bank()
                            for dc in range(8):
                                S.op('pe', lambda e, s=s, j=j, dc=dc, c0=c0, n=n, pg=pg: e.matmul(
                                    pg[:, 0:n], lhsT=wgb[s][:, dc, j * 128:(j + 1) * 128], rhs=xn[:, dc, c0:c0 + n], start=(dc == 0), stop=(dc == 7)),
                                    reads=[(tag, 'wg', s), ('xn', c0)], writes=[rg], signal=(dc == 7))
                            for dc in range(8):
                                S.op('pe', lambda e, s=s, j=j, dc=dc, c0=c0, n=n, pu=pu: e.matmul(
                                    pu[:, 0:n], lhsT=wub[s][:, dc, j * 128:(j + 1) * 128], rhs=xn[:, dc, c0:c0 + n], start=(dc == 0), stop=(dc == 7)),
                                    reads=[(tag, 'wu', s), ('xn', c0)], writes=[ru], signal=(dc == 7))
                            k = sgi % 2
                            sgi += 1
                            S.op('act', lambda e, k=k, n=n, pg=pg: e.activation(out=sg[k][:, 0:n], in_=pg[:, 0:n], func=AF.Silu),
                                 reads=[rg], writes=[(tag, 'sg', k)])
                            S.op('dve', lambda e, k=k, s=s, j=j, c0=c0, n=n, pu=pu: e.tensor_tensor(
                                out=act[s][:, j, c0:c0 + n], in0=sg[k][:, 0:n], in1=pu[:, 0:n], op=ALU.mult),
                                reads=[(tag, 'sg', k), ru], writes=[(tag, 'act', s, c0)])
                    for dc in range(8):
                        for (c0, n) in tiles:
                            pd, rd = bank()
                            for j in range(nf):
                                S.op('pe', lambda e, s=s, j=j, dc=dc, c0=c0, n=n, pd=pd, nf=nf: e.matmul(
                                    pd[:, 0:n], lhsT=wdb[s][:, j, dc * 128:(dc + 1) * 128], rhs=act[s][:, j, c0:c0 + n], start=(j == 0), stop=(j == nf - 1)),
                                    reads=[(tag, 'wd', s), (tag, 'act', s, c0)], writes=[rd], signal=(j == nf - 1))
                            S.op('dve', lambda e, dc=dc, c0=c0, n=n, pd=pd: e.scalar_tensor_tensor(
                                out=xT[:, dc, c0:c0 + n], in0=pd[:, 0:n], scalar=0.5, in1=xT[:, dc, c0:c0 + n], op0=ALU.mult, op1=ALU.add),
                                reads=[rd, ('xT', dc, c0)], writes=[('xT', dc, c0)])
                S.barrier()

        def load_segment(seg):
            with ExitStack() as ph:
                xtok = [sb(ph, "xtok%d" % i, [128, 4, D], F32) for i in range(2)]
                for ti in range(4):
                    s = ti % 2
                    S.dma('sp', lambda e, s=s, ti=ti: e.dma_start(out=xtok[s][:], in_=x4[seg, ti * 512:(ti + 1) * 512, :].rearrange("(n p) d -> p n d", p=128)),
                          writes=[('xtok', s)], sem='xt%d' % s)
                    for dc in range(8):
                        pb, pr = bank()
                        for n4 in range(4):
                            S.op('pe', lambda e, s=s, n4=n4, dc=dc, pb=pb: e.transpose(pb[:, n4 * 128:(n4 + 1) * 128], xtok[s][:, n4, dc * 128:(dc + 1) * 128], ident[:]),
                                 reads=[('xtok', s), 'ident'], writes=[pr], signal=(n4 == 3))
                        eng = 'dve' if dc % 2 == 0 else 'act'
                        if eng == 'dve':
                            S.op('dve', lambda e, dc=dc, ti=ti, pb=pb: e.tensor_copy(out=xT[:, dc, ti * 512:(ti + 1) * 512], in_=pb[:, :]),
                                 reads=[pr], writes=[('xT', dc, ti * 512)])
                        else:
                            S.op('act', lambda e, dc=dc, ti=ti, pb=pb: e.activation(out=xT[:, dc, ti * 512:(ti + 1) * 512], in_=pb[:, :], func=AF.Copy),
                                 reads=[pr], writes=[('xT', dc, ti * 512)])
                S.op('dve', lambda e: e.tensor_copy(out=xT[:, :, NT:NTH], in_=XH[:]), reads=['XH'], writes=[('xT', dc, NT) for dc in range(8)])
                S.barrier()

        def store_output():
            with ExitStack() as ph:
                sq = [sb(ph, "osq%d" % i, [128, 8, 512], BF16) for i in range(2)]
                rs = [sb(ph, "ors%d" % i, [128, 512], F32) for i in range(2)]
                otok = [sb(ph, "otok%d" % i, [128, D], F32) for i in range(2)]

                def dst(dc, c0, n):
                    return xT[:, dc, c0:c0 + n], ('xT', dc, c0)
                rmsnorm_to(dst, PC_FIN, sq, rs, TILES[:4])
                toks = []
                for tk in range(16):
                    s = tk % 2
                    for half in range(2):
                        pb, pr = bank()
                        for q in range(4):
                            dc = half * 4 + q
                            S.op('pe', lambda e, dc=dc, q=q, tk=tk, pb=pb: e.transpose(pb[:, q * 128:(q + 1) * 128], xT[:, dc, tk * 128:(tk + 1) * 128], ident[:]),
                                 reads=[('xT', dc, (tk // 4) * 512), 'ident'], writes=[pr], signal=(q == 3))
                        if half == 0:
                            S.op('dve', lambda e, s=s, pb=pb: e.tensor_copy(out=otok[s][:, 0:512], in_=pb[:, :]), reads=[pr], writes=[('otok', s)])
                        else:
                            S.op('act', lambda e, s=s, pb=pb: e.activation(out=otok[s][:, 512:1024], in_=pb[:, :], func=AF.Copy), reads=[pr], writes=[('otok', s)])
                    toks.append(S.dma('sp', lambda e, s=s, tk=tk: e.dma_start(out=out_d[tk * 128:(tk + 1) * 128, :], in_=otok[s][:]),
                                      reads=[('otok', s)], sem='out%d' % s))
                S.wait_all('sp', toks[-2:])
                S.barrier()

        winv = win.rearrange("(dc p) f -> p dc f", p=128)

        def mixer(full):
            if full:
                dump(14, None, ('xT', 0, 0), ap=xT[:, 0, 0:512])
            with ExitStack() as ph:
                sq = [sb(ph, "msq%d" % i, [128, 8, 512], BF16) for i in range(2)]
                rs = [sb(ph, "mrs%d" % i, [128, 512], F32) for i in range(2)]
                rmsnorm_to(xn_dst, PC_MXN, sq, rs, TILES)
                S.barrier()
            outer = ExitStack()
            OPEN.append(outer)
            if full:
                YG = sb(outer, "YG", [128, 4, NT], BF16)
            with ExitStack() as ph:
                LU = sb(ph, "LU", [128, 512], BF16)
                S.dma('pool', lambda e: e.dma_start(out=LU[0:64, :], in_=wdu), writes=['LU'], sem='lu')
                S.dma('pool', lambda e: e.dma_start(out=LU[64:128, :], in_=wiu), writes=['LU'], sem='lu')
                wrkv = [sb(ph, "wrkv%d" % i, [128, 8, 3, 128], BF16) for i in range(1 if full else 2)]
                LW = sb(ph, "LW", [128, NT], BF16)
                Pb = [sb(ph, "Pb%d" % i, [128, 516], F32) for i in range(1 if full else 2)]
                dtmp = sb(ph, "dtmp", [128, 512], F32)
                if full:
                    WGU = sb(ph, "WGU", [128, 512], BF16)
                    S.dma('pool', lambda e: e.dma_start(out=WGU[:], in_=wgu), writes=['WGU'], sem='wgu')
                    SG = sb(ph, "SG", [128, NT], BF16)
                pbi = [0]

                def proj_mix(wfn, wres, mucol, out_ap, out_res, ti, key=None):
                    c0 = ti * 512
                    cur = Pb[pbi[0] % len(Pb)]
                    cr = ('Pb', pbi[0] % len(Pb))
                    pbi[0] += 1
                    pb, pr = bank()
                    for dc in range(8):
                        S.op('pe', lambda e, dc=dc, c0=c0, pb=pb: e.matmul(pb[:, :], lhsT=wfn(dc), rhs=xn[:, dc, c0:c0 + 512], start=(dc == 0), stop=(dc == 7)),
                             reads=[wres, ('xn', c0)], writes=[pr], signal=(dc == 7))
                    S.op('act', lambda e, cur=cur, pb=pb: e.activation(out=cur[:, 4:516], in_=pb[:, :], func=AF.Copy), reads=[pr], writes=[cr])
                    h0 = NT if ti == 0 else c0 - 2
                    hres = ('xn', NT) if ti == 0 else ('xn', c0 - 512)
                    ph_, phr = bank()
                    for dc in range(8):
                        S.op('pe', lambda e, dc=dc, ph_=ph_, h0=h0: e.matmul(ph_[:, 0:2], lhsT=wfn(dc), rhs=xn[:, dc, h0:h0 + 2], start=(dc == 0), stop=(dc == 7)),
                             reads=[wres, hres], writes=[phr], signal=(dc == 7))
                    S.op('dve', lambda e, cur=cur, ph_=ph_: e.tensor_copy(out=cur[:, 2:4], in_=ph_[:, 0:2]), reads=[phr], writes=[cr])
                    S.op('dve', lambda e, cur=cur: e.tensor_tensor(out=dtmp[:], in0=cur[:, 3:515], in1=cur[:, 4:516], op=ALU.subtract),
                         reads=[cr], writes=['dtmp'])
                    S.op('dve', lambda e, cur=cur: e.scalar_tensor_tensor(out=out_ap, in0=dtmp[:], scalar=prm[:, mucol:mucol + 1], in1=cur[:, 4:516],
                                                                           op0=ALU.mult, op1=ALU.add), reads=['dtmp', cr, 'prm'], writes=[out_res])

                lora_ph = ExitStack()
                wl = sb(lora_ph, "wl", [128, 8, 256], BF16)
                S.dma('pool', lambda e: e.dma_start(out=wl[:], in_=winv[:, :, 3072:3328]), writes=['wl'], sem='wl')
                ltmp = sb(lora_ph, "ltmp", [128, 512], F32)
                for ti in range(4):
                    proj_mix(lambda dc: wl[:, dc, 0:128], 'wl', PC_MU + 12, ltmp[:], 'ltmp', ti, key='l0')
                    S.op('act', lambda e, ti=ti: e.activation(out=LW[0:64, ti * 512:(ti + 1) * 512], in_=ltmp[0:64, :], func=AF.Tanh),
                         reads=['ltmp'], writes=[('LW', ti)])
                    S.op('dve', lambda e, ti=ti: e.tensor_copy(out=LW[64:128, ti * 512:(ti + 1) * 512], in_=ltmp[64:128, :]),
                         reads=['ltmp'], writes=[('LW', ti)])
                if full:
                    for ti in range(4):
                        proj_mix(lambda dc: wl[:, dc, 128:256], 'wl', PC_MU + 13, ltmp[:], 'ltmp', ti, key='l1')
                        S.op('act', lambda e, ti=ti: e.activation(out=SG[:, ti * 512:(ti + 1) * 512], in_=ltmp[:], func=AF.Sigmoid),
                             reads=['ltmp'], writes=[('SG', ti)])

                ck(2)
                S.barrier()
                lora_ph.close()
                def f32t(n):
                    return sb(ph, n, [128, 512], F32)
                Rm, Km, Vm = f32t("Rm"), f32t("Km"), f32t("Vm")
                LG, IC, KKt, K2, Bt = f32t("LG"), f32t("IC"), f32t("KKt"), f32t("K2"), f32t("Bt")
                Li, E1, E2, T1 = f32t("Li"), f32t("E1"), f32t("E2"), f32t("T1")
                sqb = sb(ph, "sqb", [128, 512], BF16)
                nslot = 2
                nrot[0] = 7 if full else 8
                OPS = [(sb(ph, "OP1s%d" % i, [128, 1024], BF16),
                        sb(ph, "OP2s%d" % i, [128, 1024], BF16),
                        sb(ph, "TSs%d" % i, [128, 4, 512], BF16),
                        sb(ph, "PCts%d" % i, [128, 8], F32),
                        sb(ph, "RKPs%d" % i, [128, 512], BF16) if full else None) for i in range(nslot)]
                def pair(n, shape, dt):
                    return [sb(ph, "%s_g%d" % (n, g), shape, dt) for g in range(2)]
                G_RH1 = pair("RH1", [128, 512], BF16)
                G_RH2 = pair("RH2", [128, 512], BF16)
                G_KH = pair("KH", [128, 256], BF16)
                G_VT = pair("VT", [128, 256], BF16)
                G_NRK = pair("NRK", [128, 256], BF16)
                G_XA = [[sb(ph, "XA%d_g%d" % (i, g), [128, 512], BF16) for i in range(2)] for g in range(2)]
                G_XT = [[sb(ph, "XT%d_g%d" % (i, g), [128, 256], BF16) for i in range(2)] for g in range(2)]
                G_TT = pair("TT", [128, 256], BF16)
                G_AW = pair("AW", [128, 512], BF16)
                G_PHI = pair("PHI", [128, 256], BF16)
                G_QQ = pair("QQ", [128, 256], BF16)
                G_HH = pair("HH", [128, 256], BF16)
                G_GG = pair("GG", [128, 256], BF16)
                if full:
                    YT = f32t("YT")
                    YTb = sb(ph, "YTb", [128, 512], BF16)
                    Ysq = sb(ph, "Ysq", [128, 512], BF16)
                    MU_ = IC
                    VAR = LG

                if DBG[0] == -1:
                    print('SBUF remaining after RWKV allocs (full=%s): %d B' % (full, nc.sbuf_bytes_remaining))
                def v3(ap, k=64):
                    return ap.rearrange("p (c k) -> p c k", k=k)

                def opv(t, which):
                    return t[:, :].rearrange("p (c w k) -> p c w k", w=2, k=64)[:, :, which, :]

                def prep_gen(hp, ti, slot):
                    ws = hp % len(wrkv)
                    hc = hp
                    tc0 = ti * 512
                    OP1, OP2, TS, PCt, RKP = OPS[slot]
                    if ti == 0:
                        for q in range(3):
                            col = 1536 + q * 512 + hp * 128
                            S.dma('pool', lambda e, ws=ws, q=q, col=col: e.dma_start(out=wrkv[ws][:, :, q, :], in_=winv[:, :, col:col + 128]),
                                  writes=[('wrkv', ws)], sem='wrkv%d' % ws)
                    if full:
                        proj_mix(lambda dc, ws=ws: wrkv[ws][:, dc, 0, :], ('wrkv', ws), PC_MU + 0 + hp, Rm[:], 'Rm', ti, key='r')
                        yield
                    proj_mix(lambda dc, ws=ws: wrkv[ws][:, dc, 1, :], ('wrkv', ws), PC_MU + 4 + hp, Km[:], 'Km', ti, key='k')
                    yield
                    proj_mix(lambda dc, ws=ws: wrkv[ws][:, dc, 2, :], ('wrkv', ws), PC_MU + 8 + hp, Vm[:], 'Vm', ti, key='v')
                    yield
                    if full and hp == 0 and ti == 0:
                        dump(0, Rm, 'Rm'); dump(1, Km, 'Km'); dump(2, Vm, 'Vm')
                    yield
                    pz, pzr = bank()
                    S.op('pe', lambda e, pz=pz, hp=hp, tc0=tc0: e.matmul(pz[:, :], lhsT=LU[0:64, hp * 128:(hp + 1) * 128], rhs=LW[0:64, tc0:tc0 + 512], start=True, stop=True),
                         reads=['LU', ('LW', ti)], writes=[pzr])
                    S.op('act', lambda e, pz=pz, hc=hc: e.activation(out=LG[:], in_=pz[:, :], func=AF.Sigmoid, bias=prm[:, PC_W0 + hc:PC_W0 + hc + 1], scale=1.0),
                         reads=[pzr, 'prm'], writes=['LG'])
                    pi, pir = bank()
                    S.op('pe', lambda e, pi=pi, hp=hp, tc0=tc0: e.matmul(pi[:, :], lhsT=LU[64:128, hp * 128:(hp + 1) * 128], rhs=LW[64:128, tc0:tc0 + 512], start=True, stop=True),
                         reads=['LU', ('LW', ti)], writes=[pir])
                    S.op('act', lambda e, pi=pi, hc=hc: e.activation(out=IC[:], in_=pi[:, :], func=AF.Sigmoid, bias=prm[:, PC_A0 + hc:PC_A0 + hc + 1], scale=1.0),
                         reads=[pir, 'prm'], writes=['IC'])
                    S.op('dve', lambda e: e.tensor_scalar(out=LG[:], in0=LG[:], scalar1=DECAY_C, scalar2=None, op0=ALU.mult), reads=['LG'], writes=['LG'])
                    yield
                    S.op('dve', lambda e, hc=hc: e.tensor_scalar(out=KKt[:], in0=Km[:], scalar1=prm[:, PC_KK + hc:PC_KK + hc + 1], scalar2=None, op0=ALU.mult),
                         reads=['Km', 'prm'], writes=['KKt'])
                    S.op('act', lambda e: e.activation(out=sqb[:], in_=KKt[:], func=AF.Square), reads=['KKt'], writes=['sqb'])
                    pn, pnr = bank()
                    S.op('pe', lambda e, pn=pn: e.matmul(pn[:, :], lhsT=blk_b[:], rhs=sqb[:], start=True, stop=True), reads=['cst', 'sqb'], writes=[pnr])
                    S.op('dve', lambda e, pn=pn: e.tensor_scalar(out=T1[:], in0=pn[:, :], scalar1=1e-18, scalar2=None, op0=ALU.max), reads=[pnr], writes=['T1'])
                    S.op('act', lambda e: e.activation(out=T1[:], in_=T1[:], func=AF.Ln), reads=['T1'], writes=['T1'])
                    S.op('act', lambda e: e.activation(out=T1[:], in_=T1[:], func=AF.Exp, scale=-0.5), reads=['T1'], writes=['T1'])
                    S.op('dve', lambda e: e.tensor_tensor(out=KKt[:], in0=KKt[:], in1=T1[:], op=ALU.mult), reads=['KKt', 'T1'], writes=['KKt'])
                    yield
                    S.op('dve', lambda e, hc=hc: e.tensor_scalar(out=T1[:], in0=IC[:], scalar1=prm[:, PC_KA + hc:PC_KA + hc + 1], scalar2=omka[:, hc:hc + 1],
                                                                 op0=ALU.mult, op1=ALU.add), reads=['IC', 'prm', 'omka'], writes=['T1'])
                    S.op('dve', lambda e: e.tensor_tensor(out=K2[:], in0=Km[:], in1=T1[:], op=ALU.mult), reads=['Km', 'T1'], writes=['K2'])
                    S.op('dve', lambda e: e.tensor_tensor(out=Bt[:], in0=KKt[:], in1=IC[:], op=ALU.mult), reads=['KKt', 'IC'], writes=['Bt'])
                    if full and hp == 0 and ti == 0:
                        dump(3, LG, 'LG'); dump(4, IC, 'IC'); dump(5, KKt, 'KKt'); dump(6, K2, 'K2')
                    yield
                    S.op('dve', lambda e: e.tensor_tensor_scan(out=Li[:], data0=rmask[:], data1=LG[:], initial=0.0, op0=ALU.mult, op1=ALU.add),
                         reads=['rmask', 'LG'], writes=['Li'])
                    yield
                    if full:
                        S.op('act', lambda e: e.activation(out=E1[:], in_=Li[:], func=AF.Exp), reads=['Li'], writes=['E1'])
                        S.op('dve', lambda e: e.tensor_tensor(out=opv(OP1, 1), in0=v3(Rm[:]), in1=v3(E1[:]), op=ALU.mult), reads=['Rm', 'E1'], writes=[('OP1', slot)])
                    yield
                    S.op('act', lambda e: e.activation(out=E2[:], in_=Li[:], func=AF.Exp, scale=-1.0), reads=['Li'], writes=['E2'])
                    S.op('dve', lambda e: e.tensor_tensor(out=opv(OP2, 0), in0=v3(Bt[:]), in1=v3(E2[:]), op=ALU.mult), reads=['Bt', 'E2'], writes=[('OP2', slot)])
                    S.op('dve', lambda e: e.tensor_tensor(out=opv(OP2, 1), in0=v3(K2[:]), in1=v3(E2[:]), op=ALU.mult), reads=['K2', 'E2'], writes=[('OP2', slot)])
                    yield
                    S.op('dve', lambda e: e.tensor_tensor(out=T1[:], in0=Li[:], in1=LG[:], op=ALU.subtract), reads=['Li', 'LG'], writes=['T1'])
                    S.op('act', lambda e: e.activation(out=E1[:], in_=T1[:], func=AF.Exp), reads=['T1'], writes=['E1'])
                    S.op('dve', lambda e: e.scalar_tensor_tensor(out=TS[:, 0, :], in0=KKt[:], scalar=-1.0, in1=E1[:], op0=ALU.mult, op1=ALU.mult),
                         reads=['KKt', 'E1'], writes=[('TS', slot)])
                    S.op('dve', lambda e: e.tensor_copy(out=opv(OP1, 0), in_=v3(TS[:, 0, :])), reads=[('TS', slot)], writes=[('OP1', slot)])
                    yield
                    S.op('dve', lambda e: e.tensor_tensor(out=v3(T1[:]), in0=v3(Li[:])[:, :, 63:64].to_broadcast([128, 8, 64]), in1=v3(Li[:]), op=ALU.subtract),
                         reads=['Li'], writes=['T1'])
                    S.op('act', lambda e: e.activation(out=E2[:], in_=T1[:], func=AF.Exp), reads=['T1'], writes=['E2'])
                    S.op('act', lambda e: e.activation(out=PCt[:], in_=v3(Li[:])[:, :, 63], func=AF.Exp), reads=['Li'], writes=[('PCt', slot)])
                    S.op('dve', lambda e: e.tensor_tensor(out=TS[:, 1, :], in0=Bt[:], in1=E2[:], op=ALU.mult), reads=['Bt', 'E2'], writes=[('TS', slot)])
                    S.op('dve', lambda e: e.tensor_tensor(out=TS[:, 2, :], in0=K2[:], in1=E2[:], op=ALU.mult), reads=['K2', 'E2'], writes=[('TS', slot)])
                    S.op('act', lambda e: e.activation(out=TS[:, 3, :], in_=Vm[:], func=AF.Copy), reads=['Vm'], writes=[('TS', slot)])
                    if full:
                        S.op('dve', lambda e, hc=hc: e.scalar_tensor_tensor(out=RKP[:], in0=Rm[:], scalar=prm[:, PC_RK + hc:PC_RK + hc + 1], in1=K2[:], op0=ALU.mult, op1=ALU.mult),
                             reads=['Rm', 'K2', 'prm'], writes=[('RKP', slot)])

                def grp_gen(grp, slot, hp, py, pyr):
                    OP1, OP2, TS, PCt, RKP = OPS[slot]
                    RH1, RH2, KH, VT, NRK = G_RH1[grp], G_RH2[grp], G_KH[grp], G_VT[grp], G_NRK[grp]
                    XA, XTt, TT, AW = G_XA[grp], G_XT[grp], G_TT[grp], G_AW[grp]
                    PHI, QQ, HH, GG = G_PHI[grp], G_QQ[grp], G_HH[grp], G_GG[grp]
                    pt = [bank(), bank()]
                    for u in range(4):
                        c = grp * 4 + u
                        for src in range(4):
                            for h in range(2):
                                hs = slice(h * 64, (h + 1) * 64)
                                S.op('pe', lambda e, hs=hs, u=u, src=src, c=c, ptb=pt[src // 2][0]: e.matmul(
                                    ptb[hs, ((src % 2) * 4 + u) * 64:((src % 2) * 4 + u + 1) * 64], lhsT=TS[hs, src, c * 64:(c + 1) * 64], rhs=id2[hs, :], start=True, stop=True),
                                    reads=[('TS', slot), 'cst'], writes=[pt[src // 2][1]], signal=False)
                    S.signal_last('pe')
                    pv0 = pt[0][0][:, :].rearrange("p (s u k) -> p s u k", u=4, k=64)
                    pv1 = pt[1][0][:, :].rearrange("p (s u k) -> p s u k", u=4, k=64)
                    S.op('act', lambda e: e.activation(out=RH1[:, :].rearrange("p (u w k) -> p u w k", w=2, k=64)[:, :, 0, :], in_=pv0[:, 0, :, :], func=AF.Copy),
                         reads=[pt[0][1]], writes=[('RH1', grp)])
                    S.op('act', lambda e: e.activation(out=RH2[:, :].rearrange("p (u w k) -> p u w k", w=2, k=64)[:, :, 0, :], in_=pv0[:, 1, :, :], func=AF.Copy),
                         reads=[pt[0][1]], writes=[('RH2', grp)])
                    S.op('act', lambda e: e.activation(out=v3(KH[:]), in_=pv1[:, 0, :, :], func=AF.Copy), reads=[pt[1][1]], writes=[('KH', grp)])
                    S.op('act', lambda e: e.activation(out=v3(VT[:]), in_=pv1[:, 1, :, :], func=AF.Copy), reads=[pt[1][1]], writes=[('VT', grp)])
                    yield
                    wq = 128 if full else 64
                    px, pxr = bank()
                    pyy, pyyr = bank()
                    if full:
                        pzz, pzzr = bank()
                    for u in range(4):
                        c = grp * 4 + u
                        for h in range(2):
                            hs = slice(h * 64, (h + 1) * 64)
                            S.op('pe', lambda e, hs=hs, u=u, c=c, px=px: e.matmul(px[hs, u * 128:u * 128 + wq], lhsT=OP2[hs, c * 128:c * 128 + 64],
                                                                                 rhs=OP1[hs, c * 128:c * 128 + wq], start=True, stop=True),
                                 reads=[('OP1', slot), ('OP2', slot)], writes=[pxr], signal=False)
                            S.op('pe', lambda e, hs=hs, u=u, c=c, pyy=pyy: e.matmul(pyy[hs, u * 128:(u + 1) * 128], lhsT=OP1[hs, c * 128:c * 128 + 64],
                                                                                   rhs=OP2[hs, c * 128:(c + 1) * 128], start=True, stop=True),
                                 reads=[('OP1', slot), ('OP2', slot)], writes=[pyyr], signal=False)
                            if full:
                                S.op('pe', lambda e, hs=hs, u=u, c=c, pzz=pzz: e.matmul(pzz[hs, u * 64:(u + 1) * 64], lhsT=OP2[hs, c * 128 + 64:(c + 1) * 128],
                                                                                       rhs=OP1[hs, c * 128 + 64:(c + 1) * 128], start=True, stop=True),
                                     reads=[('OP1', slot), ('OP2', slot)], writes=[pzzr], signal=False)
                    S.signal_last('pe')

                    def xv(t, which):
                        return t[:, :].rearrange("p (u w k) -> p u w k", w=2, k=64)[:, :, which, :]
                    xa, xb = XA[0], XA[1]
                    xta, xtb = XTt[0], XTt[1]
                    S.op('dve', lambda e, px=px, xa=xa: e.tensor_tensor(out=xv(xa, 0), in0=xv(px, 0), in1=v3(msu[:]), op=ALU.mult),
                         reads=[pxr, 'cst'], writes=[('XA', grp, 0)])
                    S.op('dve', lambda e, xa=xa: e.tensor_copy(out=xv(xa, 1), in_=v3(id2x4[:])), reads=['cst'], writes=[('XA', grp, 0)])
                    if full:
                        S.op('dve', lambda e, px=px: e.tensor_tensor(out=xv(RH2, 1), in0=xv(px, 1), in1=v3(miu[:]), op=ALU.mult),
                             reads=[pxr, 'cst'], writes=[('RH2', grp)])
                    S.op('dve', lambda e, pyy=pyy, xta=xta: e.tensor_tensor(out=v3(xta[:]), in0=xv(pyy, 0), in1=v3(msl[:]), op=ALU.mult),
                         reads=[pyyr, 'cst'], writes=[('XT', grp, 0)])
                    S.op('dve', lambda e, pyy=pyy: e.tensor_tensor(out=xv(RH1, 1), in0=xv(pyy, 1), in1=v3(msl[:]), op=ALU.mult),
                         reads=[pyyr, 'cst'], writes=[('RH1', grp)])
                    if full:
                        S.op('dve', lambda e, pzz=pzz: e.tensor_tensor(out=v3(NRK[:]), in0=v3(pzz[:, 0:256]), in1=v3(miu[:]), op=ALU.mult),
                             reads=[pzzr, 'cst'], writes=[('NRK', grp)])
                    yield
                    cur = 0
                    for lvl in range(5):
                        xc, xn_ = XA[cur], XA[1 - cur]
                        tcur, tn_ = XTt[cur], XTt[1 - cur]
                        pa, par = bank()
                        pbb, pbr = bank()
                        for u in range(4):
                            for h in range(2):
                                hs = slice(h * 64, (h + 1) * 64)
                                S.op('pe', lambda e, hs=hs, u=u, pa=pa, xc=xc, tcur=tcur: e.matmul(
                                    pa[hs, u * 128:(u + 1) * 128], lhsT=tcur[hs, u * 64:(u + 1) * 64], rhs=xc[hs, u * 128:(u + 1) * 128], start=True, stop=True),
                                    reads=[('XA', grp, cur), ('XT', grp, cur)], writes=[par], signal=False)
                                S.op('pe', lambda e, hs=hs, u=u, pbb=pbb, xc=xc, tcur=tcur: e.matmul(
                                    pbb[hs, u * 64:(u + 1) * 64], lhsT=xc[hs, u * 128:u * 128 + 64], rhs=tcur[hs, u * 64:(u + 1) * 64], start=True, stop=True),
                                    reads=[('XA', grp, cur), ('XT', grp, cur)], writes=[pbr], signal=False)
                        S.signal_last('pe')
                        S.op('act', lambda e, pa=pa, xn_=xn_: e.activation(out=xv(xn_, 0), in_=xv(pa, 0), func=AF.Copy), reads=[par], writes=[('XA', grp, 1 - cur)])
                        S.op('dve', lambda e, pa=pa, xn_=xn_, xc=xc: e.tensor_tensor(out=xv(xn_, 1), in0=xv(pa, 1), in1=xv(xc, 1), op=ALU.add),
                             reads=[par, ('XA', grp, cur)], writes=[('XA', grp, 1 - cur)])
                        S.op('act', lambda e, pbb=pbb, tn_=tn_: e.activation(out=tn_[:, :], in_=pbb[:, 0:256], func=AF.Copy), reads=[pbr], writes=[('XT', grp, 1 - cur)])
                        cur = 1 - cur
                        yield
                    xc, tcur = XA[cur], XTt[cur]
                    pa, par = bank()
                    for u in range(4):
                        for h in range(2):
                            hs = slice(h * 64, (h + 1) * 64)
                            S.op('pe', lambda e, hs=hs, u=u, pa=pa, xc=xc, tcur=tcur: e.matmul(
                                pa[hs, u * 64:(u + 1) * 64], lhsT=tcur[hs, u * 64:(u + 1) * 64], rhs=xc[hs, u * 128 + 64:(u + 1) * 128], start=True, stop=True),
                                reads=[('XA', grp, cur), ('XT', grp, cur)], writes=[par], signal=False)
                    S.signal_last('pe')
                    S.op('dve', lambda e, pa=pa, xc=xc: e.tensor_tensor(out=v3(TT[:]), in0=v3(pa[:, 0:256]), in1=xv(xc, 1), op=ALU.add),
                         reads=[par, ('XA', grp, cur)], writes=[('TT', grp)])
                    yield
                    pa, par = bank()
                    for u in range(4):
                        for h in range(2):
                            hs = slice(h * 64, (h + 1) * 64)
                            S.op('pe', lambda e, hs=hs, u=u, pa=pa: e.matmul(pa[hs, u * 128:(u + 1) * 128], lhsT=TT[hs, u * 64:(u + 1) * 64],
                                                                             rhs=RH1[hs, u * 128:(u + 1) * 128], start=True, stop=True),
                                 reads=[('TT', grp), ('RH1', grp)], writes=[par], signal=False)
                    S.signal_last('pe')
                    S.op('act', lambda e, pa=pa: e.activation(out=AW[:, :], in_=pa[:, :], func=AF.Copy), reads=[par], writes=[('AW', grp)])
                    pq, pqr = bank()
                    phh, phr = bank()
                    for u in range(4):
                        for h in range(2):
                            hs = slice(h * 64, (h + 1) * 64)
                            S.op('pe', lambda e, hs=hs, u=u, pq=pq: e.matmul(pq[hs, u * 128:u * 128 + wq], lhsT=AW[hs, u * 128:u * 128 + 64],
                                                                             rhs=RH2[hs, u * 128:u * 128 + wq], start=True, stop=True),
                                 reads=[('AW', grp), ('RH2', grp)], writes=[pqr], signal=False)
                            S.op('pe', lambda e, hs=hs, u=u, phh=phh: e.matmul(phh[hs, u * 128:u * 128 + wq], lhsT=AW[hs, u * 128 + 64:(u + 1) * 128],
                                                                               rhs=RH2[hs, u * 128:u * 128 + wq], start=True, stop=True),
                                 reads=[('AW', grp), ('RH2', grp)], writes=[phr], signal=False)
                    S.signal_last('pe')
                    S.op('act', lambda e, pq=pq: e.activation(out=v3(PHI[:]), in_=xv(pq, 0), func=AF.Copy), reads=[pqr], writes=[('PHI', grp)])
                    S.op('dve', lambda e, phh=phh: e.tensor_tensor(out=v3(HH[:]), in0=xv(phh, 0), in1=v3(KH[:]), op=ALU.add), reads=[phr, ('KH', grp)], writes=[('HH', grp)])
                    if full:
                        S.op('dve', lambda e, pq=pq, grp=grp: e.tensor_tensor(
                            out=v3(QQ[:]), in0=xv(pq, 1), in1=OP1[:, grp * 512:(grp + 1) * 512].rearrange("p (u w k) -> p u w k", w=2, k=64)[:, :, 1, :], op=ALU.add),
                            reads=[pqr, ('OP1', slot)], writes=[('QQ', grp)])
                        S.op('dve', lambda e, phh=phh: e.tensor_tensor(out=v3(GG[:]), in0=xv(phh, 1), in1=v3(NRK[:]), op=ALU.add), reads=[phr, ('NRK', grp)], writes=[('GG', grp)])
                    yield
                    for u in range(4):
                        c = grp * 4 + u
                        if full:
                            for h in range(2):
                                hs = slice(h * 64, (h + 1) * 64)
                                S.op('pe', lambda e, hs=hs, u=u, c=c, py=py, hp=hp: e.matmul(py[hs, c * 64:(c + 1) * 64], lhsT=STB[hp][hs, :], rhs=QQ[hs, u * 64:(u + 1) * 64], start=True, stop=False),
                                     reads=[('STB', hp), ('QQ', grp)], writes=[pyr], signal=False)
                                S.op('pe', lambda e, hs=hs, u=u, c=c, py=py: e.matmul(py[hs, c * 64:(c + 1) * 64], lhsT=VT[hs, u * 64:(u + 1) * 64], rhs=GG[hs, u * 64:(u + 1) * 64], start=False, stop=True),
                                     reads=[('VT', grp), ('GG', grp)], writes=[pyr], signal=False)
                            S.signal_last('pe')
                        pss, psr = bank()
                        for h in range(2):
                            hs = slice(h * 64, (h + 1) * 64)
                            S.op('pe', lambda e, hs=hs, u=u, pss=pss, hp=hp: e.matmul(pss[hs, 0:64], lhsT=PHI[hs, u * 64:(u + 1) * 64], rhs=STB[hp][hs, :], start=True, stop=False),
                                 reads=[('PHI', grp), ('STB', hp)], writes=[psr], signal=False)
                            S.op('pe', lambda e, hs=hs, u=u, pss=pss: e.matmul(pss[hs, 0:64], lhsT=HH[hs, u * 64:(u + 1) * 64], rhs=VT[hs, u * 64:(u + 1) * 64], start=False, stop=True),
                                 reads=[('HH', grp), ('VT', grp)], writes=[psr], signal=False)
                        S.signal_last('pe')
                        S.op('dve', lambda e, c=c, pss=pss, hp=hp: e.scalar_tensor_tensor(out=ST[hp][:], in0=ST[hp][:], scalar=PCt[:, c:c + 1], in1=pss[:, 0:64], op0=ALU.mult, op1=ALU.add),
                             reads=[('ST', hp), ('PCt', slot), psr], writes=[('ST', hp)])
                        S.op('act', lambda e, hp=hp: e.activation(out=STB[hp][:], in_=ST[hp][:], func=AF.Copy), reads=[('ST', hp)], writes=[('STB', hp)])

                def gn_stage(hp, ti, slot, py, pyr):
                    hc = hp
                    tc0 = ti * 512
                    OP1, OP2, TS, PCt, RKP = OPS[slot]
                    S.op('act', lambda e, py=py: e.activation(out=YT[:], in_=py[:, :], func=AF.Copy), reads=[pyr], writes=['YT'])
                    if hp == 0 and ti == 0:
                        dump(7, YT, 'YT')
                    S.op('dve', lambda e: e.tensor_copy(out=YTb[:], in_=YT[:]), reads=['YT'], writes=['YTb'])
                    S.op('act', lambda e: e.activation(out=Ysq[:], in_=YT[:], func=AF.Square), reads=['YT'], writes=['Ysq'])
                    pm, pmr = bank()
                    pv, pvr = bank()
                    S.op('pe', lambda e, pm=pm: e.matmul(pm[:, :], lhsT=blk_b[:], rhs=YTb[:], start=True, stop=True), reads=['cst', 'YTb'], writes=[pmr])
                    S.op('pe', lambda e, pv=pv: e.matmul(pv[:, :], lhsT=blk_b[:], rhs=Ysq[:], start=True, stop=True), reads=['cst', 'Ysq'], writes=[pvr])
                    S.op('act', lambda e, pm=pm: e.activation(out=MU_[:], in_=pm[:, :], func=AF.Copy, scale=1.0 / 64), reads=[pmr], writes=['IC'])
                    S.op('dve', lambda e: e.tensor_tensor(out=VAR[:], in0=MU_[:], in1=MU_[:], op=ALU.mult), reads=['IC'], writes=['LG'])
                    S.op('dve', lambda e, pv=pv: e.scalar_tensor_tensor(out=VAR[:], in0=pv[:, :], scalar=1.0 / 64, in1=VAR[:], op0=ALU.mult, op1=ALU.subtract),
                         reads=[pvr, 'LG'], writes=['LG'])
                    S.op('act', lambda e: e.activation(out=VAR[:], in_=VAR[:], func=AF.Ln, bias=epsc[:, 1:2], scale=1.0), reads=['LG', 'epsc'], writes=['LG'])
                    S.op('act', lambda e: e.activation(out=VAR[:], in_=VAR[:], func=AF.Exp, scale=-0.5), reads=['LG'], writes=['LG'])
                    S.op('dve', lambda e: e.tensor_tensor(out=YT[:], in0=YT[:], in1=MU_[:], op=ALU.subtract), reads=['YT', 'IC'], writes=['YT'])
                    S.op('dve', lambda e: e.tensor_tensor(out=YT[:], in0=YT[:], in1=VAR[:], op=ALU.mult), reads=['YT', 'LG'], writes=['YT'])
                    S.op('dve', lambda e, hc=hc: e.tensor_scalar(out=YT[:], in0=YT[:], scalar1=prm[:, PC_LNW + hc:PC_LNW + hc + 1], scalar2=prm[:, PC_LNB + hc:PC_LNB + hc + 1],
                                                                 op0=ALU.mult, op1=ALU.add), reads=['YT', 'prm'], writes=['YT'])
                    pbn, pbnr = bank()
                    S.op('pe', lambda e, pbn=pbn: e.matmul(pbn[:, :], lhsT=blk_b[:], rhs=RKP[:], start=True, stop=True), reads=['cst', ('RKP', slot)], writes=[pbnr])
                    S.op('dve', lambda e, pbn=pbn: e.tensor_tensor(out=MU_[:], in0=pbn[:, :], in1=TS[:, 3, :], op=ALU.mult), reads=[pbnr, ('TS', slot)], writes=['IC'])
                    S.op('dve', lambda e: e.tensor_tensor(out=YT[:], in0=YT[:], in1=MU_[:], op=ALU.add), reads=['YT', 'IC'], writes=['YT'])
                    if hp == 0 and ti == 0:
                        dump(8, YT, 'YT')
                    pgt, pgr = bank()
                    S.op('pe', lambda e, pgt=pgt, hp=hp, tc0=tc0: e.matmul(pgt[:, :], lhsT=WGU[:, hp * 128:(hp + 1) * 128], rhs=SG[:, tc0:tc0 + 512], start=True, stop=True),
                         reads=['WGU', ('SG', ti)], writes=[pgr])
                    S.op('dve', lambda e, pgt=pgt, hp=hp, tc0=tc0: e.tensor_tensor(out=YG[:, hp, tc0:tc0 + 512], in0=YT[:], in1=pgt[:, :], op=ALU.mult),
                         reads=['YT', pgr], writes=['YG'])

                def drive(gl):
                    gl = list(gl)
                    while gl:
                        for g_ in list(gl):
                            try:
                                next(g_)
                            except StopIteration:
                                gl.remove(g_)

                tiles16 = [(hp_, ti_) for hp_ in range(4) for ti_ in range(4)]
                drive([prep_gen(tiles16[0][0], tiles16[0][1], 0)])
                for k_, (hp, ti) in enumerate(tiles16):
                    py, pyr = (PS[7], ('ps', 7)) if full else (None, None)
                    gl = [grp_gen(0, k_ % 2, hp, py, pyr), grp_gen(1, k_ % 2, hp, py, pyr)]
                    if k_ + 1 < len(tiles16):
                        gl.append(prep_gen(tiles16[k_ + 1][0], tiles16[k_ + 1][1], (k_ + 1) % 2))
                    drive(gl)
                    if full:
                        gn_stage(hp, ti, k_ % 2, py, pyr)
                S.barrier()
            if not full:
                outer.close()
                return
            ck(10)
            ph2 = outer
            YA = sb(ph2, "YA", [128, 4, NT], BF16)
            MG = sb(ph2, "MG", [128, 8, NT], BF16)
            sub = ExitStack()
            OPEN.append(sub)
            wc = [sb(sub, "wc%d" % i, [128, 8, 3, 128], BF16) for i in range(2)]
            Bc, Cc, Uc = f32c = [sb(sub, "cb%d" % i, [128, 512], F32) for i in range(3)]
            CU = [sb(sub, "CU%d" % i, [128, 516], F32) for i in range(2)]
            cacc = sb(sub, "cacc", [128, 512], F32)
            for i in range(4):
                ws = i % 2
                for q in range(3):
                    col = q * 512 + i * 128
                    S.dma('pool', lambda e, ws=ws, q=q, col=col: e.dma_start(out=wc[ws][:, :, q, :], in_=winv[:, :, col:col + 128]),
                          writes=[('wc', ws)], sem='wc%d' % ws)
                cw = PC_CONV + i * 3
                for ti in range(4):
                    c0 = ti * 512
                    cu = CU[ti % 2]
                    cup = CU[(ti + 1) % 2]
                    pbs = []
                    for q in range(3):
                        pb, pr = bank()
                        for dc in range(8):
                            S.op('pe', lambda e, ws=ws, q=q, dc=dc, c0=c0, pb=pb: e.matmul(pb[:, :], lhsT=wc[ws][:, dc, q, :], rhs=xn[:, dc, c0:c0 + 512], start=(dc == 0), stop=(dc == 7)),
                                 reads=[('wc', ws), ('xn', c0)], writes=[pr], signal=(dc == 7))
                        pbs.append((pb, pr))
                    S.op('act', lambda e, pb=pbs[0][0]: e.activation(out=Bc[:], in_=pb[:, :], func=AF.Copy), reads=[pbs[0][1]], writes=['Bc'])
                    S.op('act', lambda e, pb=pbs[1][0]: e.activation(out=Cc[:], in_=pb[:, :], func=AF.Copy), reads=[pbs[1][1]], writes=['Cc'])
                    S.op('dve', lambda e, pb=pbs[2][0], cu=cu: e.tensor_tensor(out=cu[:, 4:516], in0=Cc[:], in1=pb[:, :], op=ALU.mult),
                         reads=['Cc', pbs[2][1]], writes=[('CU', ti % 2)])
                    if ti == 0:
                        hb = []
                        for q in (1, 2):
                            pb, pr = bank()
                            for dc in range(8):
                                S.op('pe', lambda e, ws=ws, q=q, dc=dc, pb=pb: e.matmul(pb[:, 0:2], lhsT=wc[ws][:, dc, q, :], rhs=xn[:, dc, NT:NTH], start=(dc == 0), stop=(dc == 7)),
                                     reads=[('wc', ws), ('xn', NT)], writes=[pr], signal=(dc == 7))
                            hb.append((pb, pr))
                        S.op('act', lambda e, pb=hb[0][0]: e.activation(out=Cc[:, 0:2], in_=pb[:, 0:2], func=AF.Copy), reads=[hb[0][1], ('CU', 0)], writes=['Cc'])
                        S.op('dve', lambda e, pb=hb[1][0], cu=cu: e.tensor_tensor(out=cu[:, 2:4], in0=Cc[:, 0:2], in1=pb[:, 0:2], op=ALU.mult),
                             reads=['Cc', hb[1][1]], writes=[('CU', 0)])
                    else:
                        S.op('dve', lambda e, cu=cu, cup=cup: e.tensor_copy(out=cu[:, 2:4], in_=cup[:, 514:516]), reads=[('CU', (ti + 1) % 2)], writes=[('CU', ti % 2)])
                    S.op('dve', lambda e, cu=cu, cw=cw: e.tensor_scalar(out=cacc[:], in0=cu[:, 2:514], scalar1=prm[:, cw:cw + 1], scalar2=None, op0=ALU.mult),
                         reads=[('CU', ti % 2), 'prm'], writes=['cacc'])
                    S.op('dve', lambda e, cu=cu, cw=cw: e.scalar_tensor_tensor(out=cacc[:], in0=cu[:, 3:515], scalar=prm[:, cw + 1:cw + 2], in1=cacc[:], op0=ALU.mult, op1=ALU.add),
                         reads=[('CU', ti % 2), 'prm', 'cacc'], writes=['cacc'])
                    S.op('dve', lambda e, cu=cu, cw=cw: e.scalar_tensor_tensor(out=cacc[:], in0=cu[:, 4:516], scalar=prm[:, cw + 2:cw + 3], in1=cacc[:], op0=ALU.mult, op1=ALU.add),
                         reads=[('CU', ti % 2), 'prm', 'cacc'], writes=['cacc'])
                    if i == 0 and ti == 0:
                        dump(10, cacc, 'cacc'); dump(11, Bc, 'Bc')
                    S.op('dve', lambda e, i=i, c0=c0: e.tensor_tensor(out=YA[:, i, c0:c0 + 512], in0=Bc[:], in1=cacc[:], op=ALU.mult),
                         reads=['Bc', 'cacc'], writes=['YA'])
            S.barrier()
            sub.close()
            sub = ExitStack()
            OPEN.append(sub)
            WOA = sb(sub, "WOA", [128, 4, D], BF16)
            WOB = sb(sub, "WOB", [128, 4, D], BF16)
            S.dma('pool', lambda e: e.dma_start(out=WOA[:], in_=woa.rearrange("(j p) d -> p j d", p=128)), writes=['WOA'], sem='woa')
            S.dma('pool', lambda e: e.dma_start(out=WOB[:], in_=wob.rearrange("(j p) d -> p j d", p=128)), writes=['WOB'], sem='wob')
            wgc = [sb(sub, "wgc%d" % i, [128, 8, 2, 128], BF16) for i in range(2)]
            ga, gb_, m1 = [sb(sub, "gm%d" % i, [128, 512], F32) for i in range(3)]
            for dcx in range(8):
                ws = dcx % 2
                for q in range(2):
                    col = 3328 + q * 1024 + dcx * 128
                    S.dma('pool', lambda e, ws=ws, q=q, col=col: e.dma_start(out=wgc[ws][:, :, q, :], in_=winv[:, :, col:col + 128]),
                          writes=[('wgc', ws)], sem='wgc%d' % ws)
                for ti in range(4):
                    c0 = ti * 512
                    pgs = []
                    for q in range(2):
                        pb, pr = bank()
                        for dc in range(8):
                            S.op('pe', lambda e, ws=ws, q=q, dc=dc, c0=c0, pb=pb: e.matmul(pb[:, :], lhsT=wgc[ws][:, dc, q, :], rhs=xn[:, dc, c0:c0 + 512], start=(dc == 0), stop=(dc == 7)),
                                 reads=[('wgc', ws), ('xn', c0)], writes=[pr], signal=(dc == 7))
                        pgs.append((pb, pr))
                    S.op('act', lambda e, pb=pgs[0][0]: e.activation(out=ga[:], in_=pb[:, :], func=AF.Sigmoid), reads=[pgs[0][1]], writes=['ga'])
                    S.op('act', lambda e, pb=pgs[1][0]: e.activation(out=gb_[:], in_=pb[:, :], func=AF.Sigmoid), reads=[pgs[1][1]], writes=['gb'])
                    pya, pyar = bank()
                    pyb, pybr = bank()
                    for j in range(4):
                        S.op('pe', lambda e, j=j, dcx=dcx, c0=c0, pya=pya: e.matmul(pya[:, :], lhsT=WOA[:, j, dcx * 128:(dcx + 1) * 128], rhs=YA[:, j, c0:c0 + 512], start=(j == 0), stop=(j == 3)),
                             reads=['WOA', 'YA'], writes=[pyar], signal=(j == 3))
                    for j in range(4):
                        S.op('pe', lambda e, j=j, dcx=dcx, c0=c0, pyb=pyb: e.matmul(pyb[:, :], lhsT=WOB[:, j, dcx * 128:(dcx + 1) * 128], rhs=YG[:, j, c0:c0 + 512], start=(j == 0), stop=(j == 3)),
                             reads=['WOB', 'YG'], writes=[pybr], signal=(j == 3))
                    S.op('dve', lambda e, pya=pya: e.tensor_tensor(out=m1[:], in0=ga[:], in1=pya[:, :], op=ALU.mult), reads=['ga', pyar], writes=['m1'])
                    S.op('dve', lambda e, pyb=pyb: e.tensor_tensor(out=gb_[:], in0=gb_[:], in1=pyb[:, :], op=ALU.mult), reads=['gb', pybr], writes=['gb'])
                    if dcx == 0 and ti == 0:
                        dump(12, m1, 'm1'); dump(13, gb_, 'gb')
                    S.op('dve', lambda e, dcx=dcx, c0=c0: e.tensor_tensor(out=MG[:, dcx, c0:c0 + 512], in0=m1[:], in1=gb_[:], op=ALU.add), reads=['m1', 'gb'], writes=['MG'])
            S.barrier()
            sub.close()
            sub = ExitStack()
            OPEN.append(sub)
            WO = sb(sub, "WO", [128, 8, D], BF16)
            S.dma('pool', lambda e: e.dma_start(out=WO[:], in_=wo.rearrange("(j p) d -> p j d", p=128)), writes=['WO'], sem='wo')
            for dcx in range(8):
                for ti in range(4):
                    c0 = ti * 512
                    pb, pr = bank()
                    for j in range(8):
                        S.op('pe', lambda e, j=j, dcx=dcx, c0=c0, pb=pb: e.matmul(pb[:, :], lhsT=WO[:, j, dcx * 128:(dcx + 1) * 128], rhs=MG[:, j, c0:c0 + 512], start=(j == 0), stop=(j == 7)),
                             reads=['WO', 'MG'], writes=[pr], signal=(j == 7))
                    S.op('dve', lambda e, dcx=dcx, c0=c0, pb=pb: e.tensor_tensor(out=xT[:, dcx, c0:c0 + 512], in0=xT[:, dcx, c0:c0 + 512], in1=pb[:, :], op=ALU.add),
                         reads=[pr, ('xT', dcx, c0)], writes=[('xT', dcx, c0)])
            dump(9, None, ('xT', 0, 0), ap=xT[:, 0, 0:512])
            S.barrier()
            sub.close()
            outer.close()

        try:
            for seg in range(NSEG - nseg, NSEG):
                full = (seg == NSEG - 1)
                load_segment(seg)
                if do_ffn1:
                    ffn("f1", PC_F1N, w1g, w1u, w1d, TILES[:4])
                    S.op('dve', lambda e: e.tensor_copy(out=XH[:], in_=xT[:, :, NT - 2:NT]), reads=[('xT', dc, 1536) for dc in range(8)], writes=['XH'])
                if do_mix:
                    mixer(full)
            if do_ffn2:
                ffn("f2", PC_F2N, w2g, w2u, w2d, TILES[:4])
        except _Stop:
            pass
        S.enabled = True
        S.barrier()
        store_output()
        S.emit()
    return nc


def make_consts():
    c = np.zeros((128, 128 * 3 + 64 + 256 * 4 + 512), np.float32)
    c[:, 0:128] = np.eye(128)
    c[:, 128:256] = 1.0
    blk = np.zeros((128, 128), np.float32)
    blk[0:64, 0:64] = 1.0
    blk[64:128, 64:128] = 1.0
    c[:, 256:384] = blk
    i64 = np.eye(64, dtype=np.float32)
    id2 = np.concatenate([i64, i64], 0)
    c[:, 384:448] = id2
    c[:, 448:704] = np.tile(id2, (1, 4))
    su = np.triu(np.ones((64, 64), np.float32), 1)
    iu = np.triu(np.ones((64, 64), np.float32), 0)
    sl = su.T
    for k, m in enumerate((su, iu, sl)):
        c[:, 704 + k * 256:704 + (k + 1) * 256] = np.tile(np.concatenate([m, m], 0), (1, 4))
    rm = np.ones((128, 512), np.float32)
    rm[:, 0::64] = 0.0
    c[:, 1472:1984] = rm
    return c


def make_prm(i):
    p = np.zeros((128, PC_N), np.float32)

    def cols(v, n):
        return np.ascontiguousarray(np.asarray(v, np.float32).reshape(n, 128).T)
    p[:, PC_F1N:PC_F1N + 8] = cols(i["ffn1_norm"][0], 8)
    p[:, PC_MXN:PC_MXN + 8] = cols(i["mix_norm"][0], 8)
    p[:, PC_F2N:PC_F2N + 8] = cols(i["ffn2_norm"][0], 8)
    p[:, PC_FIN:PC_FIN + 8] = cols(i["final_norm"], 8)
    cw = np.asarray(i["conv_w"][0], np.float32)
    for ch in range(4):
        for j in range(3):
            p[:, PC_CONV + ch * 3 + j] = cw[j, ch * 128:(ch + 1) * 128]
    p[:, PC_MU:PC_MU + 14] = cols(i["mu_b"][0], 14)
    p[:, PC_W0:PC_W0 + 4] = cols(i["w0"][0], 4)
    p[:, PC_A0:PC_A0 + 4] = cols(i["a0"][0], 4)
    p[:, PC_KK:PC_KK + 4] = cols(i["k_k"][0], 4)
    p[:, PC_KA:PC_KA + 4] = cols(i["k_a"][0], 4)
    p[:, PC_RK:PC_RK + 4] = cols(np.asarray(i["r_k"][0]).reshape(512), 4)
    p[:, PC_LNW:PC_LNW + 4] = cols(i["ln_x_w"][0], 4)
    p[:, PC_LNB:PC_LNB + 4] = cols(i["ln_x_b"][0], 4)
    return p


def make_in_maps(inputs):
    i = {k: np.asarray(v) for k, v in inputs.items()}
    x = np.asarray(i["x"], np.float32)
    shared = {
        "prm": make_prm(i), "cst": make_consts(),
        "w1g": np.ascontiguousarray(i["ffn1_w_gate"][0]), "w1u": np.ascontiguousarray(i["ffn1_w_up"][0]), "w1d": np.ascontiguousarray(i["ffn1_w_down"][0]),
        "w2g": np.ascontiguousarray(i["ffn2_w_gate"][0]), "w2u": np.ascontiguousarray(i["ffn2_w_up"][0]), "w2d": np.ascontiguousarray(i["ffn2_w_down"][0]),
        "win": np.ascontiguousarray(i["w_in"][0]), "woa": np.ascontiguousarray(i["w_out_a"][0]), "wob": np.ascontiguousarray(i["w_out_b"][0]),
        "wo": np.ascontiguousarray(i["w_o"][0]), "wdu": np.ascontiguousarray(i["w_decay_up"][0]), "wiu": np.ascontiguousarray(i["w_iclr_up"][0]),
        "wgu": np.ascontiguousarray(i["w_gate_up"][0]),
    }
    maps = []
    for core in range(8):
        b, qq = core // 4, core % 4
        x4 = np.zeros((NSEG, NTH, D), np.float32)
        for s in range(NSEG):
            src = qq - (NSEG - 1 - s)
            if src < 0:
                continue
            x4[s, 0:NT] = x[b, src * NT:(src + 1) * NT]
            if src > 0:
                x4[s, NT:NTH] = x[b, src * NT - 2:src * NT]
        m = dict(shared)
        m["x4"] = x4
        maps.append(m)
    return maps


_NC = {}


def kernel(**inputs):
    if "nc" not in _NC:
        _NC["nc"] = build_nc()
    maps = make_in_maps(inputs)
    res = run_bass_kernel_spmd(_NC["nc"], maps, core_ids=list(range(8)))
    out = np.zeros((2, 8192, D), np.float32)
    for core in range(8):
        b, qq = core // 4, core % 4
        out[b, qq * NT:(qq + 1) * NT] = res.results[core]["out"]
    return out
```

```python
import numpy as np
from contextlib import ExitStack
import concourse.bass as bass
import concourse.mybir as mybir
from concourse.bass_utils import run_bass_kernel_spmd

F32 = mybir.dt.float32
BF16 = mybir.dt.bfloat16
AF = mybir.ActivationFunctionType
ALU = mybir.AluOpType

NT = 2048
NTH = 2050
D = 1024
FF = 2816
NSEG = 4
TILES = [(0, 512), (512, 512), (1024, 512), (1536, 512), (2048, 2)]
RMS_EPS = 1e-6
GN_EPS = 64e-5
DECAY_C = -0.6065306597126334

PC_F1N, PC_MXN, PC_F2N, PC_FIN, PC_CONV, PC_MU, PC_W0, PC_A0, PC_KK, PC_KA, PC_RK, PC_LNW, PC_LNB, PC_N = \
    0, 8, 16, 24, 32, 44, 58, 62, 66, 70, 74, 78, 82, 86


DBG = [0]
DUMP = [False]
OPEN = []
SREF = []


class _Stop(Exception):
    pass


DBGN = [0]


def ck(n):
    if DBG[0] == n and SREF:
        if DBGN[0] > 0:
            DBGN[0] -= 1
            return
        SREF[0].enabled = False


class Sched:
    ENG = ['pe', 'act', 'dve', 'pool', 'sp']

    def __init__(self, nc, es):
        self.nc = nc
        self.es = es
        self.ops = {e: [] for e in self.ENG}
        self.sig = {e: 0 for e in self.ENG}
        self.sem = {e: es.enter_context(nc.semaphore("s_" + e)) for e in self.ENG}
        self.seen = {e: {} for e in self.ENG}
        self.last_w = {}
        self.readers = {}
        self.dma_sem = {}
        self.dma_cnt = {}
        self.enabled = True
        SREF.clear()
        SREF.append(self)

    def _deps(self, eng, reads, writes):
        need = {}

        def add(k, v):
            if need.get(k, 0) < v:
                need[k] = v
        for r in reads:
            t = self.last_w.get(r)
            if t is not None:
                add(*t)
        for w in writes:
            t = self.last_w.get(w)
            if t is not None:
                add(*t)
            for k, v in self.readers.get(w, {}).items():
                add(k, v)
        waits = []
        for k, v in need.items():
            if k == 'pe' and eng == 'pe':
                continue
            if self.seen[eng].get(k, 0) >= v:
                continue
            self.seen[eng][k] = v
            waits.append((k, v))
        return waits

    def _commit(self, tok, reads, writes):
        for w in writes:
            self.last_w[w] = tok
            self.readers[w] = {}
        for r in reads:
            d = self.readers.setdefault(r, {})
            if d.get(tok[0], 0) < tok[1]:
                d[tok[0]] = tok[1]

    @staticmethod
    def _is_psum(r):
        return r == 'pst' or (isinstance(r, tuple) and r[0] == 'ps')

    def op(self, eng, fn, reads=(), writes=(), signal=True):
        if not self.enabled:
            return None
        writes = list(writes) + [r for r in reads if self._is_psum(r)]
        waits = self._deps(eng, reads, writes)
        if signal:
            self.sig[eng] += 1
            tok = (eng, self.sig[eng])
        else:
            tok = (eng, self.sig[eng] + 1)
        self.ops[eng].append([waits, fn, (eng, 1) if signal else None])
        self._commit(tok, reads, writes)
        return tok

    def signal_last(self, eng):
        if not self.enabled:
            return
        o = self.ops[eng][-1]
        if o[2] is None:
            o[2] = (eng, 1)
            self.sig[eng] += 1

    def dma(self, eng, fn, reads=(), writes=(), sem=None):
        if not self.enabled:
            return None
        if sem not in self.dma_sem:
            self.dma_sem[sem] = self.es.enter_context(self.nc.semaphore("d_" + str(sem)))
            self.dma_cnt[sem] = 0
        waits = self._deps(eng, reads, writes)
        self.dma_cnt[sem] += 16
        key = ('dma', sem)
        tok = (key, self.dma_cnt[sem])
        self.ops[eng].append([waits, fn, (key, 16)])
        self._commit(tok, reads, writes)
        return tok

    def wait_all(self, eng, toks):
        if not self.enabled:
            return
        waits = []
        for k, v in toks:
            if self.seen[eng].get(k, 0) >= v:
                continue
            self.seen[eng][k] = v
            waits.append((k, v))
        self.ops[eng].append([waits, None, None])

    def barrier(self):
        toks = [(e, self.sig[e]) for e in self.ENG if self.sig[e] > 0]
        toks += [(('dma', s), c) for s, c in self.dma_cnt.items() if c > 0]
        for e in self.ENG:
            self.wait_all(e, toks)
        self.last_w = {}
        self.readers = {}

    def _semh(self, k):
        if isinstance(k, tuple):
            return self.dma_sem[k[1]]
        return self.sem[k]

    def emit(self):
        nc = self.nc
        with nc.Block() as block:
            def replay(name):
                def f(eng):
                    for waits, fn, inc in self.ops[name]:
                        for k, v in waits:
                            eng.wait_ge(self._semh(k), v)
                        if fn is None:
                            continue
                        ins = fn(eng)
                        if inc is not None:
                            ins.then_inc(self._semh(inc[0]), inc[1])
                return f
            block.tensor(replay('pe'))
            block.scalar(replay('act'))
            block.vector(replay('dve'))
            block.gpsimd(replay('pool'))
            block.sync(replay('sp'))


def build_nc(nseg=NSEG, do_ffn1=True, do_mix=True, do_ffn2=True):
    nc = bass.Bass("TRN2", target_bir_lowering=False)

    def din(name, shape):
        return nc.dram_tensor(name, list(shape), F32, kind="ExternalInput").ap()
    x4 = din("x4", [NSEG, NTH, D])
    prm_d = din("prm", [128, PC_N])
    cst_d = din("cst", [128, 128 * 3 + 64 + 256 * 4 + 512])
    w1g, w1u, w1d = din("w1g", [D, FF]), din("w1u", [D, FF]), din("w1d", [FF, D])
    w2g, w2u, w2d = din("w2g", [D, FF]), din("w2u", [D, FF]), din("w2d", [FF, D])
    win = din("win", [D, 5376])
    woa, wob, wo = din("woa", [512, D]), din("wob", [512, D]), din("wo", [D, D])
    wdu, wiu, wgu = din("wdu", [64, 512]), din("wiu", [64, 512]), din("wgu", [128, 512])
    out_d = nc.dram_tensor("out", [NT, D], F32, kind="ExternalOutput").ap()
    dbg_d = nc.dram_tensor("dbg", [16, 128, 512], F32, kind="ExternalOutput").ap() if DUMP[0] else None

    with ExitStack() as es:
        S = Sched(nc, es)

        uid = [0]

        def sb(stack, n, s, d):
            uid[0] += 1
            return stack.enter_context(nc.sbuf_tensor("%s_%d" % (n, uid[0]), s, d))
        xT = sb(es, "xT", [128, 8, NTH], F32)
        xn = sb(es, "xn", [128, 8, NTH], BF16)
        prm = sb(es, "prm_sb", [128, PC_N], F32)
        omka = sb(es, "omka", [128, 4], F32)
        XH = sb(es, "XH", [128, 8, 2], F32)
        epsc = sb(es, "epsc", [128, 2], F32)
        ident = sb(es, "ident", [128, 128], F32)
        ones_b = sb(es, "ones_b", [128, 128], BF16)
        blk_b = sb(es, "blk_b", [128, 128], BF16)
        id2 = sb(es, "id2", [128, 64], BF16)
        id2x4 = sb(es, "id2x4", [128, 256], BF16)
        msu = sb(es, "msu", [128, 256], BF16)
        miu = sb(es, "miu", [128, 256], BF16)
        msl = sb(es, "msl", [128, 256], BF16)
        rmask = sb(es, "rmask", [128, 512], F32)
        ST = [sb(es, "ST%d" % i, [128, 64], F32) for i in range(4)]
        STB = [sb(es, "STB%d" % i, [128, 64], BF16) for i in range(4)]
        PS = [es.enter_context(nc.psum_tensor("ps%d" % i, [128, 512], F32)) for i in range(8)]
        psi = [0]
        nrot = [7]

        def bank():
            b = psi[0] % nrot[0]
            psi[0] += 1
            return PS[b], ('ps', b)

        S.dma('sp', lambda e: e.dma_start(out=prm[:], in_=prm_d), writes=['prm'], sem='c_prm')
        S.dma('sp', lambda e: e.dma_start(out=ident[:], in_=cst_d[:, 0:128]), writes=['ident'], sem='c_ident')
        S.dma('sp', lambda e: e.dma_start(out=rmask[:], in_=cst_d[:, 1472:1984]), writes=['rmask'], sem='c_rmask')
        o = 128
        for t, n in ((ones_b, 128), (blk_b, 128), (id2, 64), (id2x4, 256), (msu, 256), (miu, 256), (msl, 256)):
            S.dma('pool', lambda e, t=t, o=o, n=n: e.dma_start(out=t[:], in_=cst_d[:, o:o + n]), writes=['cst'], sem='c1')
            o += n
        S.op('dve', lambda e: e.tensor_scalar(out=omka[:], in0=prm[:, PC_KA:PC_KA + 4], scalar1=-1.0, scalar2=1.0,
                                              op0=ALU.mult, op1=ALU.add), reads=['prm'], writes=['omka'])
        S.op('dve', lambda e: e.memset(epsc[:, 0:1], RMS_EPS), writes=['epsc'])
        S.op('dve', lambda e: e.tensor_scalar(out=XH[:].rearrange("p a b -> p (a b)"), in0=ident[:, 0:16], scalar1=0.0, scalar2=None, op0=ALU.mult),
             reads=['ident'], writes=['XH'])
        S.op('dve', lambda e: e.memset(epsc[:, 1:2], GN_EPS), writes=['epsc'])
        for i in range(4):
            S.op('dve', lambda e, i=i: e.tensor_scalar(out=ST[i][:], in0=ident[:, 0:64], scalar1=0.0, scalar2=None, op0=ALU.mult),
                 reads=['ident'], writes=[('ST', i)])
            S.op('dve', lambda e, i=i: e.tensor_scalar(out=STB[i][:], in0=ident[:, 0:64], scalar1=0.0, scalar2=None, op0=ALU.mult),
                 reads=['ident'], writes=[('STB', i)])

        def dump(idx, tile_, res, ap=None):
            if DUMP[0]:
                src = ap if ap is not None else tile_[:]
                S.dma('sp', lambda e: e.dma_start(out=dbg_d[idx], in_=src), reads=[res], sem='dbg')

        def rmsnorm_to(dst_fn, gcol, sqs, rss, tiles):
            nb_ = len(sqs)
            st = {}

            def stage_a(tix):
                c0, n = tiles[tix]
                kk_ = tix % nb_
                tmp_sq, nsq = sqs[kk_], ('nsq', kk_)
                for dc in range(8):
                    S.op('act', lambda e, dc=dc, c0=c0, n=n, tmp_sq=tmp_sq: e.activation(out=tmp_sq[:, dc, 0:n], in_=xT[:, dc, c0:c0 + n], func=AF.Square),
                         reads=[('xT', dc, c0)], writes=[nsq])
                pb, pr = bank()
                for dc in range(8):
                    S.op('pe', lambda e, dc=dc, n=n, pb=pb, tmp_sq=tmp_sq: e.matmul(pb[:, 0:n], lhsT=ones_b[:], rhs=tmp_sq[:, dc, 0:n], start=(dc == 0), stop=(dc == 7)),
                         reads=[nsq, 'cst'], writes=[pr], signal=(dc == 7))
                st[tix] = (pb, pr)

            def stage_b(tix):
                c0, n = tiles[tix]
                kk_ = tix % nb_
                tmp_rs, nrs = rss[kk_], ('nrs', kk_)
                pb, pr = st.pop(tix)
                S.op('act', lambda e, n=n, pb=pb, tmp_rs=tmp_rs: e.activation(out=tmp_rs[:, 0:n], in_=pb[:, 0:n], func=AF.Ln, bias=epsc[:, 0:1], scale=1.0 / D),
                     reads=[pr, 'epsc'], writes=[nrs])
                S.op('act', lambda e, n=n, tmp_rs=tmp_rs: e.activation(out=tmp_rs[:, 0:n], in_=tmp_rs[:, 0:n], func=AF.Exp, scale=-0.5), reads=[nrs], writes=[nrs])
                for dc in range(8):
                    oap, ores = dst_fn(dc, c0, n)
                    S.op('dve', lambda e, dc=dc, c0=c0, n=n, oap=oap, tmp_rs=tmp_rs: e.scalar_tensor_tensor(
                        out=oap, in0=xT[:, dc, c0:c0 + n], scalar=prm[:, gcol + dc:gcol + dc + 1], in1=tmp_rs[:, 0:n],
                        op0=ALU.mult, op1=ALU.mult), reads=[('xT', dc, c0), nrs, 'prm'], writes=[ores])

            ahead = 1 if nb_ > 1 else 0
            for tix in range(min(ahead, len(tiles))):
                stage_a(tix)
            for tix in range(len(tiles)):
                if tix + ahead < len(tiles) and ahead:
                    stage_a(tix + ahead)
                elif not ahead:
                    stage_a(tix)
                stage_b(tix)

        def xn_dst(dc, c0, n):
            return xn[:, dc, c0:c0 + n], ('xn', c0)

        def ffn(tag, gcol, wg, wu, wd, tiles):
            with ExitStack() as ph:
                sq = [sb(ph, tag + "sq%d" % i, [128, 8, 512], BF16) for i in range(2)]
                rs = [sb(ph, tag + "rs%d" % i, [128, 512], F32) for i in range(2)]
                sg = [sb(ph, tag + "sg%d" % i, [128, 512], F32) for i in range(2)]
                act = [sb(ph, tag + "act%d" % i, [128, 4, NTH], BF16) for i in range(2)]
                wgb = [sb(ph, tag + "wg%d" % i, [128, 8, 512], BF16) for i in range(2)]
                wub = [sb(ph, tag + "wu%d" % i, [128, 8, 512], BF16) for i in range(2)]
                wdb = [sb(ph, tag + "wd%d" % i, [128, 4, D], BF16) for i in range(2)]
                rmsnorm_to(xn_dst, gcol, sq, rs, tiles)
                wgv = wg.rearrange("(dc p) f -> p dc f", p=128)
                wuv = wu.rearrange("(dc p) f -> p dc f", p=128)
                wdv = wd.rearrange("(j p) d -> p j d", p=128)
                ngr = 6
                sgi = 0
                for g in range(ngr):
                    s = g % 2
                    nf = 4 if g < 5 else 2
                    f0 = g * 512
                    S.dma('pool', lambda e, s=s, f0=f0, nf=nf: e.dma_start(out=wgb[s][:, :, 0:nf * 128], in_=wgv[:, :, f0:f0 + nf * 128]),
                          writes=[(tag, 'wg', s)], sem=tag + 'wg%d' % s)
                    S.dma('pool', lambda e, s=s, f0=f0, nf=nf: e.dma_start(out=wub[s][:, :, 0:nf * 128], in_=wuv[:, :, f0:f0 + nf * 128]),
                          writes=[(tag, 'wu', s)], sem=tag + 'wu%d' % s)
                    S.dma('pool', lambda e, s=s, g=g, nf=nf: e.dma_start(out=wdb[s][:, 0:nf, :], in_=wdv[:, g * 4:g * 4 + nf, :]),
                          writes=[(tag, 'wd', s)], sem=tag + 'wd%d' % s)
                    for j in range(nf):
                        for (c0, n) in tiles:
                            pg, rg = bank()
                            pu, ru = bank()
                            for dc in range(8):
                                S.op('pe', lambda e, s=s, j=j, dc=dc, c0=c0, n=n, pg=pg: e.matmul(
                                    pg[:, 0:n], lhsT=wgb[s][:, dc, j * 128:(j + 1) * 128], rhs=xn[:, dc, c0:c0 + n], start=(dc == 0), stop=(dc == 7)),
                                    reads=[(tag, 'wg', s), ('xn', c0)], writes=[rg], signal=(dc == 7))
                            for dc in range(8):
                                S.op('pe', lambda e, s=s, j=j, dc=dc, c0=c0, n=n, pu=pu: e.matmul(
                                    pu[:, 0:n], lhsT=wub[s][:, dc, j * 128:(j + 1) * 128], rhs=xn[:, dc, c0:c0 + n], start=(dc == 0), stop=(dc == 7)),
                                    reads=[(tag, 'wu', s), ('xn', c0)], writes=[ru], signal=(dc == 7))
                            k = sgi % 2
                            sgi += 1
                            S.op('act', lambda e, k=k, n=n, pg=pg: e.activation(out=sg[k][:, 0:n], in_=pg[:, 0:n], func=AF.Silu),
                                 reads=[rg], writes=[(tag, 'sg', k)])
                            S.op('dve', lambda e, k=k, s=s, j=j, c0=c0, n=n, pu=pu: e.tensor_tensor(
                                out=act[s][:, j, c0:c0 + n], in0=sg[k][:, 0:n], in1=pu[:, 0:n], op=ALU.mult),
                                reads=[(tag, 'sg', k), ru], writes=[(tag, 'act', s, c0)])
                    for dc in range(8):
                        for (c0, n) in tiles:
                            pd, rd = bank()
                            for j in range(nf):
                                S.op('pe', lambda e, s=s, j=j, dc=dc, c0=c0, n=n, pd=pd, nf=nf: e.matmul(
                                    pd[:, 0:n], lhsT=wdb[s][:, j, dc * 128:(dc + 1) * 128], rhs=act[s][:, j, c0:c0 + n], start=(j == 0), stop=(j == nf - 1)),
                                    reads=[(tag, 'wd', s), (tag, 'act', s, c0)], writes=[rd], signal=(j == nf - 1))
                            S.op('dve', lambda e, dc=dc, c0=c0, n=n, pd=pd: e.scalar_tensor_tensor(
                                out=xT[:, dc, c0:c0 + n], in0=pd[:, 0:n], scalar=0.5, in1=xT[:, dc, c0:c0 + n], op0=ALU.mult, op1=ALU.add),
                                reads=[rd, ('xT', dc, c0)], writes=[('xT', dc, c0)])
                S.barrier()

        def load_segment(seg):
            with ExitStack() as ph:
                xtok = [sb(ph, "xtok%d" % i, [128, 4, D], F32) for i in range(2)]
                for ti in range(4):
                    s = ti % 2
                    S.dma('sp', lambda e, s=s, ti=ti: e.dma_start(out=xtok[s][:], in_=x4[seg, ti * 512:(ti + 1) * 512, :].rearrange("(n p) d -> p n d", p=128)),
                          writes=[('xtok', s)], sem='xt%d' % s)
                    for dc in range(8):
                        pb, pr = bank()
                        for n4 in range(4):
                            S.op('pe', lambda e, s=s, n4=n4, dc=dc, pb=pb: e.transpose(pb[:, n4 * 128:(n4 + 1) * 128], xtok[s][:, n4, dc * 128:(dc + 1) * 128], ident[:]),
                                 reads=[('xtok', s), 'ident'], writes=[pr], signal=(n4 == 3))
                        eng = 'dve' if dc % 2 == 0 else 'act'
                        if eng == 'dve':
                            S.op('dve', lambda e, dc=dc, ti=ti, pb=pb: e.tensor_copy(out=xT[:, dc, ti * 512:(ti + 1) * 512], in_=pb[:, :]),
                                 reads=[pr], writes=[('xT', dc, ti * 512)])
                        else:
                            S.op('act', lambda e, dc=dc, ti=ti, pb=pb: e.activation(out=xT[:, dc, ti * 512:(ti + 1) * 512], in_=pb[:, :], func=AF.Copy),
                                 reads=[pr], writes=[('xT', dc, ti * 512)])
                S.op('dve', lambda e: e.tensor_copy(out=xT[:, :, NT:NTH], in_=XH[:]), reads=['XH'], writes=[('xT', dc, NT) for dc in range(8)])
                S.barrier()

        def store_output():
            with ExitStack() as ph:
                sq = [sb(ph, "osq%d" % i, [128, 8, 512], BF16) for i in range(2)]
                rs = [sb(ph, "ors%d" % i, [128, 512], F32) for i in range(2)]
                otok = [sb(ph, "otok%d" % i, [128, D], F32) for i in range(2)]

                def dst(dc, c0, n):
                    return xT[:, dc, c0:c0 + n], ('xT', dc, c0)
                rmsnorm_to(dst, PC_FIN, sq, rs, TILES[:4])
                toks = []
                for tk in range(16):
                    s = tk % 2
                    for half in range(2):
                        pb, pr = bank()
                        for q in range(4):
                            dc = half * 4 + q
                            S.op('pe', lambda e, dc=dc, q=q, tk=tk, pb=pb: e.transpose(pb[:, q * 128:(q + 1) * 128], xT[:, dc, tk * 128:(tk + 1) * 128], ident[:]),
                                 reads=[('xT', dc, (tk // 4) * 512), 'ident'], writes=[pr], signal=(q == 3))
                        if half == 0:
                            S.op('dve', lambda e, s=s, pb=pb: e.tensor_copy(out=otok[s][:, 0:512], in_=pb[:, :]), reads=[pr], writes=[('otok', s)])
                        else:
                            S.op('act', lambda e, s=s, pb=pb: e.activation(out=otok[s][:, 512:1024], in_=pb[:, :], func=AF.Copy), reads=[pr], writes=[('otok', s)])
                    toks.append(S.dma('sp', lambda e, s=s, tk=tk: e.dma_start(out=out_d[tk * 128:(tk + 1) * 128, :], in_=otok[s][:]),
                                      reads=[('otok', s)], sem='out%d' % s))
                S.wait_all('sp', toks[-2:])
                S.barrier()

        winv = win.rearrange("(dc p) f -> p dc f", p=128)

        def mixer(full):
            if full:
                dump(14, None, ('xT', 0, 0), ap=xT[:, 0, 0:512])
            with ExitStack() as ph:
                sq = [sb(ph, "msq%d" % i, [128, 8, 512], BF16) for i in range(2)]
                rs = [sb(ph, "mrs%d" % i, [128, 512], F32) for i in range(2)]
                rmsnorm_to(xn_dst, PC_MXN, sq, rs, TILES)
                S.barrier()
            outer = ExitStack()
            OPEN.append(outer)
            if full:
                YG = sb(outer, "YG", [128, 4, NT], BF16)
            with ExitStack() as ph:
                LU = sb(ph, "LU", [128, 512], BF16)
                S.dma('pool', lambda e: e.dma_start(out=LU[0:64, :], in_=wdu), writes=['LU'], sem='lu')
                S.dma('pool', lambda e: e.dma_start(out=LU[64:128, :], in_=wiu), writes=['LU'], sem='lu')
                wrkv = [sb(ph, "wrkv%d" % i, [128, 8, 3, 128], BF16) for i in range(1 if full else 2)]
                LW = sb(ph, "LW", [128, NT], BF16)
                Pb = [sb(ph, "Pb%d" % i, [128, 516], F32) for i in range(1 if full else 2)]
                dtmp = sb(ph, "dtmp", [128, 512], F32)
                if full:
                    WGU = sb(ph, "WGU", [128, 512], BF16)
                    S.dma('pool', lambda e: e.dma_start(out=WGU[:], in_=wgu), writes=['WGU'], sem='wgu')
                    SG = sb(ph, "SG", [128, NT], BF16)
                LASTT = sb(ph, "lastt", [128, 16], F32)
                LAST = {k_: LASTT[:, 2 * i_:2 * i_ + 2] for i_, k_ in enumerate(('l0', 'l1', 'r', 'k', 'v'))}
                pbi = [0]

                def proj_mix(wfn, wres, mucol, out_ap, out_res, ti, key=None):
                    c0 = ti * 512
                    cur = Pb[pbi[0] % len(Pb)]
                    cr = ('Pb', pbi[0] % len(Pb))
                    pbi[0] += 1
                    pb, pr = bank()
                    for dc in range(8):
                        S.op('pe', lambda e, dc=dc, c0=c0, pb=pb: e.matmul(pb[:, :], lhsT=wfn(dc), rhs=xn[:, dc, c0:c0 + 512], start=(dc == 0), stop=(dc == 7)),
                             reads=[wres, ('xn', c0)], writes=[pr], signal=(dc == 7))
                    S.op('act', lambda e, cur=cur, pb=pb: e.activation(out=cur[:, 4:516], in_=pb[:, :], func=AF.Copy), reads=[pr], writes=[cr])
                    if ti == 0:
                        ph_, phr = bank()
                        for dc in range(8):
                            S.op('pe', lambda e, dc=dc, ph_=ph_: e.matmul(ph_[:, 0:2], lhsT=wfn(dc), rhs=xn[:, dc, NT:NTH], start=(dc == 0), stop=(dc == 7)),
                                 reads=[wres, ('xn', NT)], writes=[phr], signal=(dc == 7))
                        S.op('dve', lambda e, cur=cur, ph_=ph_: e.tensor_copy(out=cur[:, 2:4], in_=ph_[:, 0:2]), reads=[phr], writes=[cr])
                    else:
                        S.op('dve', lambda e, cur=cur, key=key: e.tensor_copy(out=cur[:, 2:4], in_=LAST[key]), reads=[('last', key)], writes=[cr])
                    S.op('dve', lambda e, cur=cur, key=key: e.tensor_copy(out=LAST[key], in_=cur[:, 514:516]), reads=[cr], writes=[('last', key)])
                    S.op('dve', lambda e, cur=cur: e.tensor_tensor(out=dtmp[:], in0=cur[:, 3:515], in1=cur[:, 4:516], op=ALU.subtract),
                         reads=[cr], writes=['dtmp'])
                    S.op('dve', lambda e, cur=cur: e.scalar_tensor_tensor(out=out_ap, in0=dtmp[:], scalar=prm[:, mucol:mucol + 1], in1=cur[:, 4:516],
                                                                           op0=ALU.mult, op1=ALU.add), reads=['dtmp', cr, 'prm'], writes=[out_res])

                lora_ph = ExitStack()
                wl = sb(lora_ph, "wl", [128, 8, 256], BF16)
                S.dma('pool', lambda e: e.dma_start(out=wl[:], in_=winv[:, :, 3072:3328]), writes=['wl'], sem='wl')
                ltmp = sb(lora_ph, "ltmp", [128, 512], F32)
                for ti in range(4):
                    proj_mix(lambda dc: wl[:, dc, 0:128], 'wl', PC_MU + 12, ltmp[:], 'ltmp', ti, key='l0')
                    S.op('act', lambda e, ti=ti: e.activation(out=LW[0:64, ti * 512:(ti + 1) * 512], in_=ltmp[0:64, :], func=AF.Tanh),
                         reads=['ltmp'], writes=[('LW', ti)])
                    S.op('dve', lambda e, ti=ti: e.tensor_copy(out=LW[64:128, ti * 512:(ti + 1) * 512], in_=ltmp[64:128, :]),
                         reads=['ltmp'], writes=[('LW', ti)])
                if full:
                    for ti in range(4):
                        proj_mix(lambda dc: wl[:, dc, 128:256], 'wl', PC_MU + 13, ltmp[:], 'ltmp', ti, key='l1')
                        S.op('act', lambda e, ti=ti: e.activation(out=SG[:, ti * 512:(ti + 1) * 512], in_=ltmp[:], func=AF.Sigmoid),
                             reads=['ltmp'], writes=[('SG', ti)])

                ck(2)
                S.barrier()
                lora_ph.close()
                def f32t(n):
                    return sb(ph, n, [128, 512], F32)
                Rm, Km, Vm = f32t("Rm"), f32t("Km"), f32t("Vm")
                LG, IC, KKt, K2, Bt = f32t("LG"), f32t("IC"), f32t("KKt"), f32t("K2"), f32t("Bt")
                Li, E1, E2, T1 = f32t("Li"), f32t("E1"), f32t("E2"), f32t("T1")
                sqb = sb(ph, "sqb", [128, 512], BF16)
                nslot = 2
                nrot[0] = 7 if full else 8
                OPS = [(sb(ph, "OP1s%d" % i, [128, 1024], BF16),
                        sb(ph, "OP2s%d" % i, [128, 1024], BF16),
                        sb(ph, "TSs%d" % i, [128, 4, 512], BF16),
                        sb(ph, "PCts%d" % i, [128, 8], F32),
                        sb(ph, "RKPs%d" % i, [128, 512], BF16) if full else None) for i in range(nslot)]
                def pair(n, shape, dt):
                    return [sb(ph, "%s_g%d" % (n, g), shape, dt) for g in range(2)]
                G_RH1 = pair("RH1", [128, 512], BF16)
                G_RH2 = pair("RH2", [128, 512], BF16)
                G_KH = pair("KH", [128, 256], BF16)
                G_VT = pair("VT", [128, 256], BF16)
                G_NRK = pair("NRK", [128, 256], BF16)
                G_XA = [[sb(ph, "XA%d_g%d" % (i, g), [128, 512], BF16) for i in range(2)] for g in range(2)]
                G_XT = [[sb(ph, "XT%d_g%d" % (i, g), [128, 256], BF16) for i in range(2)] for g in range(2)]
                G_TT = pair("TT", [128, 256], BF16)
                G_AW = pair("AW", [128, 512], BF16)
                G_PHI = pair("PHI", [128, 256], BF16)
                G_QQ = pair("QQ", [128, 256], BF16)
                G_HH = pair("HH", [128, 256], BF16)
                G_GG = pair("GG", [128, 256], BF16)
                if full:
                    YT = f32t("YT")
                    YTb = sb(ph, "YTb", [128, 512], BF16)
                    Ysq = sb(ph, "Ysq", [128, 512], BF16)
                    MU_ = IC
                    VAR = LG

                if DBG[0] == -1:
                    print('SBUF remaining after RWKV allocs (full=%s): %d B' % (full, nc.sbuf_bytes_remaining))
                def v3(ap, k=64):
                    return ap.rearrange("p (c k) -> p c k", k=k)

                def opv(t, which):
                    return t[:, :].rearrange("p (c w k) -> p c w k", w=2, k=64)[:, :, which, :]

                def prep_gen(hp, ti, slot):
                    ws = hp % len(wrkv)
                    hc = hp
                    tc0 = ti * 512
                    OP1, OP2, TS, PCt, RKP = OPS[slot]
                    if ti == 0:
                        for q in range(3):
                            col = 1536 + q * 512 + hp * 128
                            S.dma('pool', lambda e, ws=ws, q=q, col=col: e.dma_start(out=wrkv[ws][:, :, q, :], in_=winv[:, :, col:col + 128]),
                                  writes=[('wrkv', ws)], sem='wrkv%d' % ws)
                    if full:
                        proj_mix(lambda dc, ws=ws: wrkv[ws][:, dc, 0, :], ('wrkv', ws), PC_MU + 0 + hp, Rm[:], 'Rm', ti, key='r')
                        yield
                    proj_mix(lambda dc, ws=ws: wrkv[ws][:, dc, 1, :], ('wrkv', ws), PC_MU + 4 + hp, Km[:], 'Km', ti, key='k')
                    yield
                    proj_mix(lambda dc, ws=ws: wrkv[ws][:, dc, 2, :], ('wrkv', ws), PC_MU + 8 + hp, Vm[:], 'Vm', ti, key='v')
                    yield
                    if full and hp == 0 and ti == 0:
                        dump(0, Rm, 'Rm'); dump(1, Km, 'Km'); dump(2, Vm, 'Vm')
                    yield
                    pz, pzr = bank()
                    S.op('pe', lambda e, pz=pz, hp=hp, tc0=tc0: e.matmul(pz[:, :], lhsT=LU[0:64, hp * 128:(hp + 1) * 128], rhs=LW[0:64, tc0:tc0 + 512], start=True, stop=True),
                         reads=['LU', ('LW', ti)], writes=[pzr])
                    S.op('act', lambda e, pz=pz, hc=hc: e.activation(out=LG[:], in_=pz[:, :], func=AF.Sigmoid, bias=prm[:, PC_W0 + hc:PC_W0 + hc + 1], scale=1.0),
                         reads=[pzr, 'prm'], writes=['LG'])
                    pi, pir = bank()
                    S.op('pe', lambda e, pi=pi, hp=hp, tc0=tc0: e.matmul(pi[:, :], lhsT=LU[64:128, hp * 128:(hp + 1) * 128], rhs=LW[64:128, tc0:tc0 + 512], start=True, stop=True),
                         reads=['LU', ('LW', ti)], writes=[pir])
                    S.op('act', lambda e, pi=pi, hc=hc: e.activation(out=IC[:], in_=pi[:, :], func=AF.Sigmoid, bias=prm[:, PC_A0 + hc:PC_A0 + hc + 1], scale=1.0),
                         reads=[pir, 'prm'], writes=['IC'])
                    S.op('dve', lambda e: e.tensor_scalar(out=LG[:], in0=LG[:], scalar1=DECAY_C, scalar2=None, op0=ALU.mult), reads=['LG'], writes=['LG'])
                    yield
                    S.op('dve', lambda e, hc=hc: e.tensor_scalar(out=KKt[:], in0=Km[:], scalar1=prm[:, PC_KK + hc:PC_KK + hc + 1], scalar2=None, op0=ALU.mult),
                         reads=['Km', 'prm'], writes=['KKt'])
                    S.op('act', lambda e: e.activation(out=sqb[:], in_=KKt[:], func=AF.Square), reads=['KKt'], writes=['sqb'])
                    pn, pnr = bank()
                    S.op('pe', lambda e, pn=pn: e.matmul(pn[:, :], lhsT=blk_b[:], rhs=sqb[:], start=True, stop=True), reads=['cst', 'sqb'], writes=[pnr])
                    S.op('dve', lambda e, pn=pn: e.tensor_scalar(out=T1[:], in0=pn[:, :], scalar1=1e-18, scalar2=None, op0=ALU.max), reads=[pnr], writes=['T1'])
                    S.op('act', lambda e: e.activation(out=T1[:], in_=T1[:], func=AF.Ln), reads=['T1'], writes=['T1'])
                    S.op('act', lambda e: e.activation(out=T1[:], in_=T1[:], func=AF.Exp, scale=-0.5), reads=['T1'], writes=['T1'])
                    S.op('dve', lambda e: e.tensor_tensor(out=KKt[:], in0=KKt[:], in1=T1[:], op=ALU.mult), reads=['KKt', 'T1'], writes=['KKt'])
                    yield
                    S.op('dve', lambda e, hc=hc: e.tensor_scalar(out=T1[:], in0=IC[:], scalar1=prm[:, PC_KA + hc:PC_KA + hc + 1], scalar2=omka[:, hc:hc + 1],
                                                                 op0=ALU.mult, op1=ALU.add), reads=['IC', 'prm', 'omka'], writes=['T1'])
                    S.op('dve', lambda e: e.tensor_tensor(out=K2[:], in0=Km[:], in1=T1[:], op=ALU.mult), reads=['Km', 'T1'], writes=['K2'])
                    S.op('dve', lambda e: e.tensor_tensor(out=Bt[:], in0=KKt[:], in1=IC[:], op=ALU.mult), reads=['KKt', 'IC'], writes=['Bt'])
                    if full and hp == 0 and ti == 0:
                        dump(3, LG, 'LG'); dump(4, IC, 'IC'); dump(5, KKt, 'KKt'); dump(6, K2, 'K2')
                    yield
                    S.op('dve', lambda e: e.tensor_tensor_scan(out=Li[:], data0=rmask[:], data1=LG[:], initial=0.0, op0=ALU.mult, op1=ALU.add),
                         reads=['rmask', 'LG'], writes=['Li'])
                    yield
                    if full:
                        S.op('act', lambda e: e.activation(out=E1[:], in_=Li[:], func=AF.Exp), reads=['Li'], writes=['E1'])
                        S.op('dve', lambda e: e.tensor_tensor(out=opv(OP1, 1), in0=v3(Rm[:]), in1=v3(E1[:]), op=ALU.mult), reads=['Rm', 'E1'], writes=[('OP1', slot)])
                    yield
                    S.op('act', lambda e: e.activation(out=E2[:], in_=Li[:], func=AF.Exp, scale=-1.0), reads=['Li'], writes=['E2'])
                    S.op('dve', lambda e: e.tensor_tensor(out=opv(OP2, 0), in0=v3(Bt[:]), in1=v3(E2[:]), op=ALU.mult), reads=['Bt', 'E2'], writes=[('OP2', slot)])
                    S.op('dve', lambda e: e.tensor_tensor(out=opv(OP2, 1), in0=v3(K2[:]), in1=v3(E2[:]), op=ALU.mult), reads=['K2', 'E2'], writes=[('OP2', slot)])
                    yield
                    S.op('dve', lambda e: e.tensor_tensor(out=T1[:], in0=Li[:], in1=LG[:], op=ALU.subtract), reads=['Li', 'LG'], writes=['T1'])
                    S.op('act', lambda e: e.activation(out=E1[:], in_=T1[:], func=AF.Exp), reads=['T1'], writes=['E1'])
                    S.op('dve', lambda e: e.scalar_tensor_tensor(out=TS[:, 0, :], in0=KKt[:], scalar=-1.0, in1=E1[:], op0=ALU.mult, op1=ALU.mult),
                         reads=['KKt', 'E1'], writes=[('TS', slot)])
                    S.op('dve', lambda e: e.tensor_copy(out=opv(OP1, 0), in_=v3(TS[:, 0, :])), reads=[('TS', slot)], writes=[('OP1', slot)])
                    yield
                    S.op('dve', lambda e: e.tensor_tensor(out=v3(T1[:]), in0=v3(Li[:])[:, :, 63:64].to_broadcast([128, 8, 64]), in1=v3(Li[:]), op=ALU.subtract),
                         reads=['Li'], writes=['T1'])
                    S.op('act', lambda e: e.activation(out=E2[:], in_=T1[:], func=AF.Exp), reads=['T1'], writes=['E2'])
                    S.op('act', lambda e: e.activation(out=PCt[:], in_=v3(Li[:])[:, :, 63], func=AF.Exp), reads=['Li'], writes=[('PCt', slot)])
                    S.op('dve', lambda e: e.tensor_tensor(out=TS[:, 1, :], in0=Bt[:], in1=E2[:], op=ALU.mult), reads=['Bt', 'E2'], writes=[('TS', slot)])
                    S.op('dve', lambda e: e.tensor_tensor(out=TS[:, 2, :], in0=K2[:], in1=E2[:], op=ALU.mult), reads=['K2', 'E2'], writes=[('TS', slot)])
                    S.op('act', lambda e: e.activation(out=TS[:, 3, :], in_=Vm[:], func=AF.Copy), reads=['Vm'], writes=[('TS', slot)])
                    if full:
                        S.op('dve', lambda e, hc=hc: e.scalar_tensor_tensor(out=RKP[:], in0=Rm[:], scalar=prm[:, PC_RK + hc:PC_RK + hc + 1], in1=K2[:], op0=ALU.mult, op1=ALU.mult),
                             reads=['Rm', 'K2', 'prm'], writes=[('RKP', slot)])

                def grp_gen(grp, slot, hp, py, pyr):
                    OP1, OP2, TS, PCt, RKP = OPS[slot]
                    RH1, RH2, KH, VT, NRK = G_RH1[grp], G_RH2[grp], G_KH[grp], G_VT[grp], G_NRK[grp]
                    XA, XTt, TT, AW = G_XA[grp], G_XT[grp], G_TT[grp], G_AW[grp]
                    PHI, QQ, HH, GG = G_PHI[grp], G_QQ[grp], G_HH[grp], G_GG[grp]
                    pt = [bank(), bank()]
                    for u in range(4):
                        c = grp * 4 + u
                        for src in range(4):
                            for h in range(2):
                                hs = slice(h * 64, (h + 1) * 64)
                                S.op('pe', lambda e, hs=hs, u=u, src=src, c=c, ptb=pt[src // 2][0]: e.matmul(
                                    ptb[hs, ((src % 2) * 4 + u) * 64:((src % 2) * 4 + u + 1) * 64], lhsT=TS[hs, src, c * 64:(c + 1) * 64], rhs=id2[hs, :], start=True, stop=True),
                                    reads=[('TS', slot), 'cst'], writes=[pt[src // 2][1]], signal=False)
                    S.signal_last('pe')
                    pv0 = pt[0][0][:, :].rearrange("p (s u k) -> p s u k", u=4, k=64)
                    pv1 = pt[1][0][:, :].rearrange("p (s u k) -> p s u k", u=4, k=64)
                    S.op('act', lambda e: e.activation(out=RH1[:, :].rearrange("p (u w k) -> p u w k", w=2, k=64)[:, :, 0, :], in_=pv0[:, 0, :, :], func=AF.Copy),
                         reads=[pt[0][1]], writes=[('RH1', grp)])
                    S.op('act', lambda e: e.activation(out=RH2[:, :].rearrange("p (u w k) -> p u w k", w=2, k=64)[:, :, 0, :], in_=pv0[:, 1, :, :], func=AF.Copy),
                         reads=[pt[0][1]], writes=[('RH2', grp)])
                    S.op('act', lambda e: e.activation(out=v3(KH[:]), in_=pv1[:, 0, :, :], func=AF.Copy), reads=[pt[1][1]], writes=[('KH', grp)])
                    S.op('act', lambda e: e.activation(out=v3(VT[:]), in_=pv1[:, 1, :, :], func=AF.Copy), reads=[pt[1][1]], writes=[('VT', grp)])
                    yield
                    wq = 128 if full else 64
                    px, pxr = bank()
                    pyy, pyyr = bank()
                    if full:
                        pzz, pzzr = bank()
                    for u in range(4):
                        c = grp * 4 + u
                        for h in range(2):
                            hs = slice(h * 64, (h + 1) * 64)
                            S.op('pe', lambda e, hs=hs, u=u, c=c, px=px: e.matmul(px[hs, u * 128:u * 128 + wq], lhsT=OP2[hs, c * 128:c * 128 + 64],
                                                                                 rhs=OP1[hs, c * 128:c * 128 + wq], start=True, stop=True),
                                 reads=[('OP1', slot), ('OP2', slot)], writes=[pxr], signal=False)
                            S.op('pe', lambda e, hs=hs, u=u, c=c, pyy=pyy: e.matmul(pyy[hs, u * 128:(u + 1) * 128], lhsT=OP1[hs, c * 128:c * 128 + 64],
                                                                                   rhs=OP2[hs, c * 128:(c + 1) * 128], start=True, stop=True),
                                 reads=[('OP1', slot), ('OP2', slot)], writes=[pyyr], signal=False)
                            if full:
                                S.op('pe', lambda e, hs=hs, u=u, c=c, pzz=pzz: e.matmul(pzz[hs, u * 64:(u + 1) * 64], lhsT=OP2[hs, c * 128 + 64:(c + 1) * 128],
                                                                                       rhs=OP1[hs, c * 128 + 64:(c + 1) * 128], start=True, stop=True),
                                     reads=[('OP1', slot), ('OP2', slot)], writes=[pzzr], signal=False)
                    S.signal_last('pe')

                    def xv(t, which):
                        return t[:, :].rearrange("p (u w k) -> p u w k", w=2, k=64)[:, :, which, :]
                    xa, xb = XA[0], XA[1]
                    xta, xtb = XTt[0], XTt[1]
                    S.op('dve', lambda e, px=px, xa=xa: e.tensor_tensor(out=xv(xa, 0), in0=xv(px, 0), in1=v3(msu[:]), op=ALU.mult),
                         reads=[pxr, 'cst'], writes=[('XA', grp, 0)])
                    S.op('dve', lambda e, xa=xa: e.tensor_copy(out=xv(xa, 1), in_=v3(id2x4[:])), reads=['cst'], writes=[('XA', grp, 0)])
                    if full:
                        S.op('dve', lambda e, px=px: e.tensor_tensor(out=xv(RH2, 1), in0=xv(px, 1), in1=v3(miu[:]), op=ALU.mult),
                             reads=[pxr, 'cst'], writes=[('RH2', grp)])
                    S.op('dve', lambda e, pyy=pyy, xta=xta: e.tensor_tensor(out=v3(xta[:]), in0=xv(pyy, 0), in1=v3(msl[:]), op=ALU.mult),
                         reads=[pyyr, 'cst'], writes=[('XT', grp, 0)])
                    S.op('dve', lambda e, pyy=pyy: e.tensor_tensor(out=xv(RH1, 1), in0=xv(pyy, 1), in1=v3(msl[:]), op=ALU.mult),
                         reads=[pyyr, 'cst'], writes=[('RH1', grp)])
                    if full:
                        S.op('dve', lambda e, pzz=pzz: e.tensor_tensor(out=v3(NRK[:]), in0=v3(pzz[:, 0:256]), in1=v3(miu[:]), op=ALU.mult),
                             reads=[pzzr, 'cst'], writes=[('NRK', grp)])
                    yield
                    cur = 0
                    for lvl in range(5):
                        xc, xn_ = XA[cur], XA[1 - cur]
                        tcur, tn_ = XTt[cur], XTt[1 - cur]
                        pa, par = bank()
                        pbb, pbr = bank()
                        for u in range(4):
                            for h in range(2):
                                hs = slice(h * 64, (h + 1) * 64)
                                S.op('pe', lambda e, hs=hs, u=u, pa=pa, xc=xc, tcur=tcur: e.matmul(
                                    pa[hs, u * 128:(u + 1) * 128], lhsT=tcur[hs, u * 64:(u + 1) * 64], rhs=xc[hs, u * 128:(u + 1) * 128], start=True, stop=True),
                                    reads=[('XA', grp, cur), ('XT', grp, cur)], writes=[par], signal=False)
                                S.op('pe', lambda e, hs=hs, u=u, pbb=pbb, xc=xc, tcur=tcur: e.matmul(
                                    pbb[hs, u * 64:(u + 1) * 64], lhsT=xc[hs, u * 128:u * 128 + 64], rhs=tcur[hs, u * 64:(u + 1) * 64], start=True, stop=True),
                                    reads=[('XA', grp, cur), ('XT', grp, cur)], writes=[pbr], signal=False)
                        S.signal_last('pe')
                        S.op('act', lambda e, pa=pa, xn_=xn_: e.activation(out=xv(xn_, 0), in_=xv(pa, 0), func=AF.Copy), reads=[par], writes=[('XA', grp, 1 - cur)])
                        S.op('dve', lambda e, pa=pa, xn_=xn_, xc=xc: e.tensor_tensor(out=xv(xn_, 1), in0=xv(pa, 1), in1=xv(xc, 1), op=ALU.add),
                             reads=[par, ('XA', grp, cur)], writes=[('XA', grp, 1 - cur)])
                        S.op('act', lambda e, pbb=pbb, tn_=tn_: e.activation(out=tn_[:, :], in_=pbb[:, 0:256], func=AF.Copy), reads=[pbr], writes=[('XT', grp, 1 - cur)])
                        cur = 1 - cur
                        yield
                    xc, tcur = XA[cur], XTt[cur]
                    pa, par = bank()
                    for u in range(4):
                        for h in range(2):
                            hs = slice(h * 64, (h + 1) * 64)
                            S.op('pe', lambda e, hs=hs, u=u, pa=pa, xc=xc, tcur=tcur: e.matmul(
                                pa[hs, u * 64:(u + 1) * 64], lhsT=tcur[hs, u * 64:(u + 1) * 64], rhs=xc[hs, u * 128 + 64:(u + 1) * 128], start=True, stop=True),
                                reads=[('XA', grp, cur), ('XT', grp, cur)], writes=[par], signal=False)
                    S.signal_last('pe')
                    S.op('dve', lambda e, pa=pa, xc=xc: e.tensor_tensor(out=v3(TT[:]), in0=v3(pa[:, 0:256]), in1=xv(xc, 1), op=ALU.add),
                         reads=[par, ('XA', grp, cur)], writes=[('TT', grp)])
                    yield
                    pa, par = bank()
                    for u in range(4):
                        for h in range(2):
                            hs = slice(h * 64, (h + 1) * 64)
                            S.op('pe', lambda e, hs=hs, u=u, pa=pa: e.matmul(pa[hs, u * 128:(u + 1) * 128], lhsT=TT[hs, u * 64:(u + 1) * 64],
                                                                             rhs=RH1[hs, u * 128:(u + 1) * 128], start=True, stop=True),
                                 reads=[('TT', grp), ('RH1', grp)], writes=[par], signal=False)
                    S.signal_last('pe')
                    S.op('act', lambda e, pa=pa: e.activation(out=AW[:, :], in_=pa[:, :], func=AF.Copy), reads=[par], writes=[('AW', grp)])
                    pq, pqr = bank()
                    phh, phr = bank()
                    for u in range(4):
                        for h in range(2):
                            hs = slice(h * 64, (h + 1) * 64)
                            S.op('pe', lambda e, hs=hs, u=u, pq=pq: e.matmul(pq[hs, u * 128:u * 128 + wq], lhsT=AW[hs, u * 128:u * 128 + 64],
                                                                             rhs=RH2[hs, u * 128:u * 128 + wq], start=True, stop=True),
                                 reads=[('AW', grp), ('RH2', grp)], writes=[pqr], signal=False)
                            S.op('pe', lambda e, hs=hs, u=u, phh=phh: e.matmul(phh[hs, u * 128:u * 128 + wq], lhsT=AW[hs, u * 128 + 64:(u + 1) * 128],
                                                                               rhs=RH2[hs, u * 128:u * 128 + wq], start=True, stop=True),
                                 reads=[('AW', grp), ('RH2', grp)], writes=[phr], signal=False)
                    S.signal_last('pe')
                    S.op('act', lambda e, pq=pq: e.activation(out=v3(PHI[:]), in_=xv(pq, 0), func=AF.Copy), reads=[pqr], writes=[('PHI', grp)])
                    S.op('dve', lambda e, phh=phh: e.tensor_tensor(out=v3(HH[:]), in0=xv(phh, 0), in1=v3(KH[:]), op=ALU.add), reads=[phr, ('KH', grp)], writes=[('HH', grp)])
                    if full:
                        S.op('dve', lambda e, pq=pq, grp=grp: e.tensor_tensor(
                            out=v3(QQ[:]), in0=xv(pq, 1), in1=OP1[:, grp * 512:(grp + 1) * 512].rearrange("p (u w k) -> p u w k", w=2, k=64)[:, :, 1, :], op=ALU.add),
                            reads=[pqr, ('OP1', slot)], writes=[('QQ', grp)])
                        S.op('dve', lambda e, phh=phh: e.tensor_tensor(out=v3(GG[:]), in0=xv(phh, 1), in1=v3(NRK[:]), op=ALU.add), reads=[phr, ('NRK', grp)], writes=[('GG', grp)])
                    yield
                    for u in range(4):
                        c = grp * 4 + u
                        if full:
                            for h in range(2):
                                hs = slice(h * 64, (h + 1) * 64)
                                S.op('pe', lambda e, hs=hs, u=u, c=c, py=py, hp=hp: e.matmul(py[hs, c * 64:(c + 1) * 64], lhsT=STB[hp][hs, :], rhs=QQ[hs, u * 64:(u + 1) * 64], start=True, stop=False),
                                     reads=[('STB', hp), ('QQ', grp)], writes=[pyr], signal=False)
                                S.op('pe', lambda e, hs=hs, u=u, c=c, py=py: e.matmul(py[hs, c * 64:(c + 1) * 64], lhsT=VT[hs, u * 64:(u + 1) * 64], rhs=GG[hs, u * 64:(u + 1) * 64], start=False, stop=True),
                                     reads=[('VT', grp), ('GG', grp)], writes=[pyr], signal=False)
                            S.signal_last('pe')
                        pss, psr = bank()
                        for h in range(2):
                            hs = slice(h * 64, (h + 1) * 64)
                            S.op('pe', lambda e, hs=hs, u=u, pss=pss, hp=hp: e.matmul(pss[hs, 0:64], lhsT=PHI[hs, u * 64:(u + 1) * 64], rhs=STB[hp][hs, :], start=True, stop=False),
                                 reads=[('PHI', grp), ('STB', hp)], writes=[psr], signal=False)
                            S.op('pe', lambda e, hs=hs, u=u, pss=pss: e.matmul(pss[hs, 0:64], lhsT=HH[hs, u * 64:(u + 1) * 64], rhs=VT[hs, u * 64:(u + 1) * 64], start=False, stop=True),
                                 reads=[('HH', grp), ('VT', grp)], writes=[psr], signal=False)
                        S.signal_last('pe')
                        S.op('dve', lambda e, c=c, pss=pss, hp=hp: e.scalar_tensor_tensor(out=ST[hp][:], in0=ST[hp][:], scalar=PCt[:, c:c + 1], in1=pss[:, 0:64], op0=ALU.mult, op1=ALU.add),
                             reads=[('ST', hp), ('PCt', slot), psr], writes=[('ST', hp)])
                        S.op('act', lambda e, hp=hp: e.activation(out=STB[hp][:], in_=ST[hp][:], func=AF.Copy), reads=[('ST', hp)], writes=[('STB', hp)])

                def gn_stage(hp, ti, slot, py, pyr):
                    hc = hp
                    tc0 = ti * 512
                    OP1, OP2, TS, PCt, RKP = OPS[slot]
                    S.op('act', lambda e, py=py: e.activation(out=YT[:], in_=py[:, :], func=AF.Copy), reads=[pyr], writes=['YT'])
                    if hp == 0 and ti == 0:
                        dump(7, YT, 'YT')
                    S.op('dve', lambda e: e.tensor_copy(out=YTb[:], in_=YT[:]), reads=['YT'], writes=['YTb'])
                    S.op('act', lambda e: e.activation(out=Ysq[:], in_=YT[:], func=AF.Square), reads=['YT'], writes=['Ysq'])
                    pm, pmr = bank()
                    pv, pvr = bank()
                    S.op('pe', lambda e, pm=pm: e.matmul(pm[:, :], lhsT=blk_b[:], rhs=YTb[:], start=True, stop=True), reads=['cst', 'YTb'], writes=[pmr])
                    S.op('pe', lambda e, pv=pv: e.matmul(pv[:, :], lhsT=blk_b[:], rhs=Ysq[:], start=True, stop=True), reads=['cst', 'Ysq'], writes=[pvr])
                    S.op('act', lambda e, pm=pm: e.activation(out=MU_[:], in_=pm[:, :], func=AF.Copy, scale=1.0 / 64), reads=[pmr], writes=['IC'])
                    S.op('dve', lambda e: e.tensor_tensor(out=VAR[:], in0=MU_[:], in1=MU_[:], op=ALU.mult), reads=['IC'], writes=['LG'])
                    S.op('dve', lambda e, pv=pv: e.scalar_tensor_tensor(out=VAR[:], in0=pv[:, :], scalar=1.0 / 64, in1=VAR[:], op0=ALU.mult, op1=ALU.subtract),
                         reads=[pvr, 'LG'], writes=['LG'])
                    S.op('act', lambda e: e.activation(out=VAR[:], in_=VAR[:], func=AF.Ln, bias=epsc[:, 1:2], scale=1.0), reads=['LG', 'epsc'], writes=['LG'])
                    S.op('act', lambda e: e.activation(out=VAR[:], in_=VAR[:], func=AF.Exp, scale=-0.5), reads=['LG'], writes=['LG'])
                    S.op('dve', lambda e: e.tensor_tensor(out=YT[:], in0=YT[:], in1=MU_[:], op=ALU.subtract), reads=['YT', 'IC'], writes=['YT'])
                    S.op('dve', lambda e: e.tensor_tensor(out=YT[:], in0=YT[:], in1=VAR[:], op=ALU.mult), reads=['YT', 'LG'], writes=['YT'])
                    S.op('dve', lambda e, hc=hc: e.tensor_scalar(out=YT[:], in0=YT[:], scalar1=prm[:, PC_LNW + hc:PC_LNW + hc + 1], scalar2=prm[:, PC_LNB + hc:PC_LNB + hc + 1],
                                                                 op0=ALU.mult, op1=ALU.add), reads=['YT', 'prm'], writes=['YT'])
                    pbn, pbnr = bank()
                    S.op('pe', lambda e, pbn=pbn: e.matmul(pbn[:, :], lhsT=blk_b[:], rhs=RKP[:], start=True, stop=True), reads=['cst', ('RKP', slot)], writes=[pbnr])
                    S.op('dve', lambda e, pbn=pbn: e.tensor_tensor(out=MU_[:], in0=pbn[:, :], in1=TS[:, 3, :], op=ALU.mult), reads=[pbnr, ('TS', slot)], writes=['IC'])
                    S.op('dve', lambda e: e.tensor_tensor(out=YT[:], in0=YT[:], in1=MU_[:], op=ALU.add), reads=['YT', 'IC'], writes=['YT'])
                    if hp == 0 and ti == 0:
                        dump(8, YT, 'YT')
                    pgt, pgr = bank()
                    S.op('pe', lambda e, pgt=pgt, hp=hp, tc0=tc0: e.matmul(pgt[:, :], lhsT=WGU[:, hp * 128:(hp + 1) * 128], rhs=SG[:, tc0:tc0 + 512], start=True, stop=True),
                         reads=['WGU', ('SG', ti)], writes=[pgr])
                    S.op('dve', lambda e, pgt=pgt, hp=hp, tc0=tc0: e.tensor_tensor(out=YG[:, hp, tc0:tc0 + 512], in0=YT[:], in1=pgt[:, :], op=ALU.mult),
                         reads=['YT', pgr], writes=['YG'])

                def drive(gl):
                    gl = list(gl)
                    while gl:
                        for g_ in list(gl):
                            try:
                                next(g_)
                            except StopIteration:
                                gl.remove(g_)

                tiles16 = [(hp_, ti_) for hp_ in range(4) for ti_ in range(4)]
                drive([prep_gen(tiles16[0][0], tiles16[0][1], 0)])
                for k_, (hp, ti) in enumerate(tiles16):
                    py, pyr = (PS[7], ('ps', 7)) if full else (None, None)
                    gl = [grp_gen(0, k_ % 2, hp, py, pyr), grp_gen(1, k_ % 2, hp, py, pyr)]
                    if k_ + 1 < len(tiles16):
                        gl.append(prep_gen(tiles16[k_ + 1][0], tiles16[k_ + 1][1], (k_ + 1) % 2))
                    drive(gl)
                    if full:
                        gn_stage(hp, ti, k_ % 2, py, pyr)
                S.barrier()
            if not full:
                outer.close()
                return
            ck(10)
            ph2 = outer
            YA = sb(ph2, "YA", [128, 4, NT], BF16)
            MG = sb(ph2, "MG", [128, 8, NT], BF16)
            sub = ExitStack()
            OPEN.append(sub)
            wc = [sb(sub, "wc%d" % i, [128, 8, 3, 128], BF16) for i in range(2)]
            Bc, Cc, Uc = f32c = [sb(sub, "cb%d" % i, [128, 512], F32) for i in range(3)]
            CU = [sb(sub, "CU%d" % i, [128, 516], F32) for i in range(2)]
            cacc = sb(sub, "cacc", [128, 512], F32)
            for i in range(4):
                ws = i % 2
                for q in range(3):
                    col = q * 512 + i * 128
                    S.dma('pool', lambda e, ws=ws, q=q, col=col: e.dma_start(out=wc[ws][:, :, q, :], in_=winv[:, :, col:col + 128]),
                          writes=[('wc', ws)], sem='wc%d' % ws)
                cw = PC_CONV + i * 3
                for ti in range(4):
                    c0 = ti * 512
                    cu = CU[ti % 2]
                    cup = CU[(ti + 1) % 2]
                    pbs = []
                    for q in range(3):
                        pb, pr = bank()
                        for dc in range(8):
                            S.op('pe', lambda e, ws=ws, q=q, dc=dc, c0=c0, pb=pb: e.matmul(pb[:, :], lhsT=wc[ws][:, dc, q, :], rhs=xn[:, dc, c0:c0 + 512], start=(dc == 0), stop=(dc == 7)),
                                 reads=[('wc', ws), ('xn', c0)], writes=[pr], signal=(dc == 7))
                        pbs.append((pb, pr))
                    S.op('act', lambda e, pb=pbs[0][0]: e.activation(out=Bc[:], in_=pb[:, :], func=AF.Copy), reads=[pbs[0][1]], writes=['Bc'])
                    S.op('act', lambda e, pb=pbs[1][0]: e.activation(out=Cc[:], in_=pb[:, :], func=AF.Copy), reads=[pbs[1][1]], writes=['Cc'])
                    S.op('dve', lambda e, pb=pbs[2][0], cu=cu: e.tensor_tensor(out=cu[:, 4:516], in0=Cc[:], in1=pb[:, :], op=ALU.mult),
                         reads=['Cc', pbs[2][1]], writes=[('CU', ti % 2)])
                    if ti == 0:
                        hb = []
                        for q in (1, 2):
                            pb, pr = bank()
                            for dc in range(8):
                                S.op('pe', lambda e, ws=ws, q=q, dc=dc, pb=pb: e.matmul(pb[:, 0:2], lhsT=wc[ws][:, dc, q, :], rhs=xn[:, dc, NT:NTH], start=(dc == 0), stop=(dc == 7)),
                                     reads=[('wc', ws), ('xn', NT)], writes=[pr], signal=(dc == 7))
                            hb.append((pb, pr))
                        S.op('act', lambda e, pb=hb[0][0]: e.activation(out=Cc[:, 0:2], in_=pb[:, 0:2], func=AF.Copy), reads=[hb[0][1], ('CU', 0)], writes=['Cc'])
                        S.op('dve', lambda e, pb=hb[1][0], cu=cu: e.tensor_tensor(out=cu[:, 2:4], in0=Cc[:, 0:2], in1=pb[:, 0:2], op=ALU.mult),
                             reads=['Cc', hb[1][1]], writes=[('CU', 0)])
                    else:
                        S.op('dve', lambda e, cu=cu, cup=cup: e.tensor_copy(out=cu[:, 2:4], in_=cup[:, 514:516]), reads=[('CU', (ti + 1) % 2)], writes=[('CU', ti % 2)])
                    S.op('dve', lambda e, cu=cu, cw=cw: e.tensor_scalar(out=cacc[:], in0=cu[:, 2:514], scalar1=prm[:, cw:cw + 1], scalar2=None, op0=ALU.mult),
                         reads=[('CU', ti % 2), 'prm'], writes=['cacc'])
                    S.op('dve', lambda e, cu=cu, cw=cw: e.scalar_tensor_tensor(out=cacc[:], in0=cu[:, 3:515], scalar=prm[:, cw + 1:cw + 2], in1=cacc[:], op0=ALU.mult, op1=ALU.add),
                         reads=[('CU', ti % 2), 'prm', 'cacc'], writes=['cacc'])
                    S.op('dve', lambda e, cu=cu, cw=cw: e.scalar_tensor_tensor(out=cacc[:], in0=cu[:, 4:516], scalar=prm[:, cw + 2:cw + 3], in1=cacc[:], op0=ALU.mult, op1=ALU.add),
                         reads=[('CU', ti % 2), 'prm', 'cacc'], writes=['cacc'])
                    if i == 0 and ti == 0:
                        dump(10, cacc, 'cacc'); dump(11, Bc, 'Bc')
                    S.op('dve', lambda e, i=i, c0=c0: e.tensor_tensor(out=YA[:, i, c0:c0 + 512], in0=Bc[:], in1=cacc[:], op=ALU.mult),
                         reads=['Bc', 'cacc'], writes=['YA'])
            S.barrier()
            sub.close()
            sub = ExitStack()
            OPEN.append(sub)
            WOA = sb(sub, "WOA", [128, 4, D], BF16)
            WOB = sb(sub, "WOB", [128, 4, D], BF16)
            S.dma('pool', lambda e: e.dma_start(out=WOA[:], in_=woa.rearrange("(j p) d -> p j d", p=128)), writes=['WOA'], sem='woa')
            S.dma('pool', lambda e: e.dma_start(out=WOB[:], in_=wob.rearrange("(j p) d -> p j d", p=128)), writes=['WOB'], sem='wob')
            wgc = [sb(sub, "wgc%d" % i, [128, 8, 2, 128], BF16) for i in range(2)]
            ga, gb_, m1 = [sb(sub, "gm%d" % i, [128, 512], F32) for i in range(3)]
            for dcx in range(8):
                ws = dcx % 2
                for q in range(2):
                    col = 3328 + q * 1024 + dcx * 128
                    S.dma('pool', lambda e, ws=ws, q=q, col=col: e.dma_start(out=wgc[ws][:, :, q, :], in_=winv[:, :, col:col + 128]),
                          writes=[('wgc', ws)], sem='wgc%d' % ws)
                for ti in range(4):
                    c0 = ti * 512
                    pgs = []
                    for q in range(2):
                        pb, pr = bank()
                        for dc in range(8):
                            S.op('pe', lambda e, ws=ws, q=q, dc=dc, c0=c0, pb=pb: e.matmul(pb[:, :], lhsT=wgc[ws][:, dc, q, :], rhs=xn[:, dc, c0:c0 + 512], start=(dc == 0), stop=(dc == 7)),
                                 reads=[('wgc', ws), ('xn', c0)], writes=[pr], signal=(dc == 7))
                        pgs.append((pb, pr))
                    S.op('act', lambda e, pb=pgs[0][0]: e.activation(out=ga[:], in_=pb[:, :], func=AF.Sigmoid), reads=[pgs[0][1]], writes=['ga'])
                    S.op('act', lambda e, pb=pgs[1][0]: e.activation(out=gb_[:], in_=pb[:, :], func=AF.Sigmoid), reads=[pgs[1][1]], writes=['gb'])
                    pya, pyar = bank()
                    pyb, pybr = bank()
                    for j in range(4):
                        S.op('pe', lambda e, j=j, dcx=dcx, c0=c0, pya=pya: e.matmul(pya[:, :], lhsT=WOA[:, j, dcx * 128:(dcx + 1) * 128], rhs=YA[:, j, c0:c0 + 512], start=(j == 0), stop=(j == 3)),
                             reads=['WOA', 'YA'], writes=[pyar], signal=(j == 3))
                    for j in range(4):
                        S.op('pe', lambda e, j=j, dcx=dcx, c0=c0, pyb=pyb: e.matmul(pyb[:, :], lhsT=WOB[:, j, dcx * 128:(dcx + 1) * 128], rhs=YG[:, j, c0:c0 + 512], start=(j == 0), stop=(j == 3)),
                             reads=['WOB', 'YG'], writes=[pybr], signal=(j == 3))
                    S.op('dve', lambda e, pya=pya: e.tensor_tensor(out=m1[:], in0=ga[:], in1=pya[:, :], op=ALU.mult), reads=['ga', pyar], writes=['m1'])
                    S.op('dve', lambda e, pyb=pyb: e.tensor_tensor(out=gb_[:], in0=gb_[:], in1=pyb[:, :], op=ALU.mult), reads=['gb', pybr], writes=['gb'])
                    if dcx == 0 and ti == 0:
                        dump(12, m1, 'm1'); dump(13, gb_, 'gb')
                    S.op('dve', lambda e, dcx=dcx, c0=c0: e.tensor_tensor(out=MG[:, dcx, c0:c0 + 512], in0=m1[:], in1=gb_[:], op=ALU.add), reads=['m1', 'gb'], writes=['MG'])
            S.barrier()
            sub.close()
            sub = ExitStack()
            OPEN.append(sub)
            WO = sb(sub, "WO", [128, 8, D], BF16)
            S.dma('pool', lambda e: e.dma_start(out=WO[:], in_=wo.rearrange("(j p) d -> p j d", p=128)), writes=['WO'], sem='wo')
            for dcx in range(8):
                for ti in range(4):
                    c0 = ti * 512
                    pb, pr = bank()
                    for j in range(8):
                        S.op('pe', lambda e, j=j, dcx=dcx, c0=c0, pb=pb: e.matmul(pb[:, :], lhsT=WO[:, j, dcx * 128:(dcx + 1) * 128], rhs=MG[:, j, c0:c0 + 512], start=(j == 0), stop=(j == 7)),
                             reads=['WO', 'MG'], writes=[pr], signal=(j == 7))
                    S.op('dve', lambda e, dcx=dcx, c0=c0, pb=pb: e.tensor_tensor(out=xT[:, dcx, c0:c0 + 512], in0=xT[:, dcx, c0:c0 + 512], in1=pb[:, :], op=ALU.add),
                         reads=[pr, ('xT', dcx, c0)], writes=[('xT', dcx, c0)])
            dump(9, None, ('xT', 0, 0), ap=xT[:, 0, 0:512])
            S.barrier()
            sub.close()
            outer.close()

        try:
            for seg in range(NSEG - nseg, NSEG):
                full = (seg == NSEG - 1)
                load_segment(seg)
                if do_ffn1:
                    ffn("f1", PC_F1N, w1g, w1u, w1d, TILES[:4])
                    S.op('dve', lambda e: e.tensor_copy(out=XH[:], in_=xT[:, :, NT - 2:NT]), reads=[('xT', dc, 1536) for dc in range(8)], writes=['XH'])
                if do_mix:
                    mixer(full)
            if do_ffn2:
                ffn("f2", PC_F2N, w2g, w2u, w2d, TILES[:4])
        except _Stop:
            pass
        S.enabled = True
        S.barrier()
        store_output()
        S.emit()
    return nc


def make_consts():
    c = np.zeros((128, 128 * 3 + 64 + 256 * 4 + 512), np.float32)
    c[:, 0:128] = np.eye(128)
    c[:, 128:256] = 1.0
    blk = np.zeros((128, 128), np.float32)
    blk[0:64, 0:64] = 1.0
    blk[64:128, 64:128] = 1.0
    c[:, 256:384] = blk
    i64 = np.eye(64, dtype=np.float32)
    id2 = np.concatenate([i64, i64], 0)
    c[:, 384:448] = id2
    c[:, 448:704] = np.tile(id2, (1, 4))
    su = np.triu(np.ones((64, 64), np.float32), 1)
    iu = np.triu(np.ones((64, 64), np.float32), 0)
    sl = su.T
    for k, m in enumerate((su, iu, sl)):
        c[:, 704 + k * 256:704 + (k + 1) * 256] = np.tile(np.concatenate([m, m], 0), (1, 4))
    rm = np.ones((128, 512), np.float32)
    rm[:, 0::64] = 0.0
    c[:, 1472:1984] = rm
    return c


def make_prm(i):
    p = np.zeros((128, PC_N), np.float32)

    def cols(v, n):
        return np.ascontiguousarray(np.asarray(v, np.float32).reshape(n, 128).T)
    p[:, PC_F1N:PC_F1N + 8] = cols(i["ffn1_norm"][0], 8)
    p[:, PC_MXN:PC_MXN + 8] = cols(i["mix_norm"][0], 8)
    p[:, PC_F2N:PC_F2N + 8] = cols(i["ffn2_norm"][0], 8)
    p[:, PC_FIN:PC_FIN + 8] = cols(i["final_norm"], 8)
    cw = np.asarray(i["conv_w"][0], np.float32)
    for ch in range(4):
        for j in range(3):
            p[:, PC_CONV + ch * 3 + j] = cw[j, ch * 128:(ch + 1) * 128]
    p[:, PC_MU:PC_MU + 14] = cols(i["mu_b"][0], 14)
    p[:, PC_W0:PC_W0 + 4] = cols(i["w0"][0], 4)
    p[:, PC_A0:PC_A0 + 4] = cols(i["a0"][0], 4)
    p[:, PC_KK:PC_KK + 4] = cols(i["k_k"][0], 4)
    p[:, PC_KA:PC_KA + 4] = cols(i["k_a"][0], 4)
    p[:, PC_RK:PC_RK + 4] = cols(np.asarray(i["r_k"][0]).reshape(512), 4)
    p[:, PC_LNW:PC_LNW + 4] = cols(i["ln_x_w"][0], 4)
    p[:, PC_LNB:PC_LNB + 4] = cols(i["ln_x_b"][0], 4)
    return p


def make_in_maps(inputs):
    i = {k: np.asarray(v) for k, v in inputs.items()}
    x = np.asarray(i["x"], np.float32)
    shared = {
        "prm": make_prm(i), "cst": make_consts(),
        "w1g": np.ascontiguousarray(i["ffn1_w_gate"][0]), "w1u": np.ascontiguousarray(i["ffn1_w_up"][0]), "w1d": np.ascontiguousarray(i["ffn1_w_down"][0]),
        "w2g": np.ascontiguousarray(i["ffn2_w_gate"][0]), "w2u": np.ascontiguousarray(i["ffn2_w_up"][0]), "w2d": np.ascontiguousarray(i["ffn2_w_down"][0]),
        "win": np.ascontiguousarray(i["w_in"][0]), "woa": np.ascontiguousarray(i["w_out_a"][0]), "wob": np.ascontiguousarray(i["w_out_b"][0]),
        "wo": np.ascontiguousarray(i["w_o"][0]), "wdu": np.ascontiguousarray(i["w_decay_up"][0]), "wiu": np.ascontiguousarray(i["w_iclr_up"][0]),
        "wgu": np.ascontiguousarray(i["w_gate_up"][0]),
    }
    maps = []
    for core in range(8):
        b, qq = core // 4, core % 4
        x4 = np.zeros((NSEG, NTH, D), np.float32)
        for s in range(NSEG):
            src = qq - (NSEG - 1 - s)
            if src < 0:
                continue
            x4[s, 0:NT] = x[b, src * NT:(src + 1) * NT]
            if src > 0:
                x4[s, NT:NTH] = x[b, src * NT - 2:src * NT]
        m = dict(shared)
        m["x4"] = x4
        maps.append(m)
    return maps


_NC = {}


def kernel(**inputs):
    if "nc" not in _NC:
        _NC["nc"] = build_nc()
    maps = make_in_maps(inputs)
    res = run_bass_kernel_spmd(_NC["nc"], maps, core_ids=list(range(8)))
    out = np.zeros((2, 8192, D), np.float32)
    for core in range(8):
        b, qq = core // 4, core % 4
        out[b, qq * NT:(qq + 1) * NT] = res.results[core]["out"]
    return out
```

```python
import numpy as np
from contextlib import ExitStack
import concourse.bass as bass
import concourse.mybir as mybir
from concourse.bass_utils import run_bass_kernel_spmd

F32 = mybir.dt.float32
BF16 = mybir.dt.bfloat16
AF = mybir.ActivationFunctionType
ALU = mybir.AluOpType

NT = 2048
NTH = 2050
D = 1024
FF = 2816
NSEG = 4
TILES = [(0, 512), (512, 512), (1024, 512), (1536, 512), (2048, 2)]
RMS_EPS = 1e-6
GN_EPS = 64e-5
DECAY_C = -0.6065306597126334

PC_F1N, PC_MXN, PC_F2N, PC_FIN, PC_CONV, PC_MU, PC_W0, PC_A0, PC_KK, PC_KA, PC_RK, PC_LNW, PC_LNB, PC_N = \
    0, 8, 16, 24, 32, 44, 58, 62, 66, 70, 74, 78, 82, 86


DBG = [0]
DUMP = [False]
OPEN = []
SREF = []


class _Stop(Exception):
    pass


DBGN = [0]


def ck(n):
    if DBG[0] == n and SREF:
        if DBGN[0] > 0:
            DBGN[0] -= 1
            return
        SREF[0].enabled = False


class Sched:
    ENG = ['pe', 'act', 'dve', 'pool', 'sp']

    def __init__(self, nc, es):
        self.nc = nc
        self.es = es
        self.ops = {e: [] for e in self.ENG}
        self.sig = {e: 0 for e in self.ENG}
        self.sem = {e: es.enter_context(nc.semaphore("s_" + e)) for e in self.ENG}
        self.seen = {e: {} for e in self.ENG}
        self.last_w = {}
        self.readers = {}
        self.dma_sem = {}
        self.dma_cnt = {}
        self.enabled = True
        SREF.clear()
        SREF.append(self)

    def _deps(self, eng, reads, writes):
        need = {}

        def add(k, v):
            if need.get(k, 0) < v:
                need[k] = v
        for r in reads:
            t = self.last_w.get(r)
            if t is not None:
                add(*t)
        for w in writes:
            t = self.last_w.get(w)
            if t is not None:
                add(*t)
            for k, v in self.readers.get(w, {}).items():
                add(k, v)
        waits = []
        for k, v in need.items():
            if k == 'pe' and eng == 'pe':
                continue
            if self.seen[eng].get(k, 0) >= v:
                continue
            self.seen[eng][k] = v
            waits.append((k, v))
        return waits

    def _commit(self, tok, reads, writes):
        for w in writes:
            self.last_w[w] = tok
            self.readers[w] = {}
        for r in reads:
            d = self.readers.setdefault(r, {})
            if d.get(tok[0], 0) < tok[1]:
                d[tok[0]] = tok[1]

    @staticmethod
    def _is_psum(r):
        return r == 'pst' or (isinstance(r, tuple) and r[0] == 'ps')

    def op(self, eng, fn, reads=(), writes=(), signal=True):
        if not self.enabled:
            return None
        writes = list(writes) + [r for r in reads if self._is_psum(r)]
        waits = self._deps(eng, reads, writes)
        if signal:
            self.sig[eng] += 1
            tok = (eng, self.sig[eng])
        else:
            tok = (eng, self.sig[eng] + 1)
        self.ops[eng].append([waits, fn, (eng, 1) if signal else None])
        self._commit(tok, reads, writes)
        return tok

    def signal_last(self, eng):
        if not self.enabled:
            return
        o = self.ops[eng][-1]
        if o[2] is None:
            o[2] = (eng, 1)
            self.sig[eng] += 1

    def dma(self, eng, fn, reads=(), writes=(), sem=None):
        if not self.enabled:
            return None
        if sem not in self.dma_sem:
            self.dma_sem[sem] = self.es.enter_context(self.nc.semaphore("d_" + str(sem)))
            self.dma_cnt[sem] = 0
        waits = self._deps(eng, reads, writes)
        self.dma_cnt[sem] += 16
        key = ('dma', sem)
        tok = (key, self.dma_cnt[sem])
        self.ops[eng].append([waits, fn, (key, 16)])
        self._commit(tok, reads, writes)
        return tok

    def wait_all(self, eng, toks):
        if not self.enabled:
            return
        waits = []
        for k, v in toks:
            if self.seen[eng].get(k, 0) >= v:
                continue
            self.seen[eng][k] = v
            waits.append((k, v))
        self.ops[eng].append([waits, None, None])

    def barrier(self):
        toks = [(e, self.sig[e]) for e in self.ENG if self.sig[e] > 0]
        toks += [(('dma', s), c) for s, c in self.dma_cnt.items() if c > 0]
        for e in self.ENG:
            self.wait_all(e, toks)
        self.last_w = {}
        self.readers = {}

    def _semh(self, k):
        if isinstance(k, tuple):
            return self.dma_sem[k[1]]
        return self.sem[k]

    def emit(self):
        nc = self.nc
        with nc.Block() as block:
            def replay(name):
                def f(eng):
                    for waits, fn, inc in self.ops[name]:
                        for k, v in waits:
                            eng.wait_ge(self._semh(k), v)
                        if fn is None:
                            continue
                        ins = fn(eng)
                        if inc is not None:
                            ins.then_inc(self._semh(inc[0]), inc[1])
                return f
            block.tensor(replay('pe'))
            block.scalar(replay('act'))
            block.vector(replay('dve'))
            block.gpsimd(replay('pool'))
            block.sync(replay('sp'))


def build_nc(nseg=NSEG, do_ffn1=True, do_mix=True, do_ffn2=True):
    nc = bass.Bass("TRN2", target_bir_lowering=False)

    def din(name, shape):
        return nc.dram_tensor(name, list(shape), F32, kind="ExternalInput").ap()
    x4 = din("x4", [NSEG, NTH, D])
    prm_d = din("prm", [128, PC_N])
    cst_d = din("cst", [128, 128 * 3 + 64 + 256 * 4 + 512])
    w1g, w1u, w1d = din("w1g", [D, FF]), din("w1u", [D, FF]), din("w1d", [FF, D])
    w2g, w2u, w2d = din("w2g", [D, FF]), din("w2u", [D, FF]), din("w2d", [FF, D])
    win = din("win", [D, 5376])
    woa, wob, wo = din("woa", [512, D]), din("wob", [512, D]), din("wo", [D, D])
    wdu, wiu, wgu = din("wdu", [64, 512]), din("wiu", [64, 512]), din("wgu", [128, 512])
    out_d = nc.dram_tensor("out", [NT, D], F32, kind="ExternalOutput").ap()
    dbg_d = nc.dram_tensor("dbg", [16, 128, 512], F32, kind="ExternalOutput").ap() if DUMP[0] else None

    with ExitStack() as es:
        S = Sched(nc, es)

        uid = [0]

        def sb(stack, n, s, d):
            uid[0] += 1
            return stack.enter_context(nc.sbuf_tensor("%s_%d" % (n, uid[0]), s, d))
        xT = sb(es, "xT", [128, 8, NTH], F32)
        xn = sb(es, "xn", [128, 8, NTH], BF16)
        prm = sb(es, "prm_sb", [128, PC_N], F32)
        omka = sb(es, "omka", [128, 4], F32)
        XH = sb(es, "XH", [128, 8, 2], F32)
        epsc = sb(es, "epsc", [128, 2], F32)
        ident = sb(es, "ident", [128, 128], F32)
        ones_b = sb(es, "ones_b", [128, 128], BF16)
        blk_b = sb(es, "blk_b", [128, 128], BF16)
        id2 = sb(es, "id2", [128, 64], BF16)
        id2x4 = sb(es, "id2x4", [128, 256], BF16)
        msu = sb(es, "msu", [128, 256], BF16)
        miu = sb(es, "miu", [128, 256], BF16)
        msl = sb(es, "msl", [128, 256], BF16)
        rmask = sb(es, "rmask", [128, 512], F32)
        ST = [sb(es, "ST%d" % i, [128, 64], F32) for i in range(4)]
        STB = [sb(es, "STB%d" % i, [128, 64], BF16) for i in range(4)]
        PS = [es.enter_context(nc.psum_tensor("ps%d" % i, [128, 512], F32)) for i in range(8)]
        psi = [0]
        nrot = [7]

        def bank():
            b = psi[0] % nrot[0]
            psi[0] += 1
            return PS[b], ('ps', b)

        S.dma('sp', lambda e: e.dma_start(out=prm[:], in_=prm_d), writes=['prm'], sem='c_prm')
        S.dma('sp', lambda e: e.dma_start(out=ident[:], in_=cst_d[:, 0:128]), writes=['ident'], sem='c_ident')
        S.dma('sp', lambda e: e.dma_start(out=rmask[:], in_=cst_d[:, 1472:1984]), writes=['rmask'], sem='c_rmask')
        o = 128
        for t, n in ((ones_b, 128), (blk_b, 128), (id2, 64), (id2x4, 256), (msu, 256), (miu, 256), (msl, 256)):
            S.dma('pool', lambda e, t=t, o=o, n=n: e.dma_start(out=t[:], in_=cst_d[:, o:o + n]), writes=['cst'], sem='c1')
            o += n
        S.op('dve', lambda e: e.tensor_scalar(out=omka[:], in0=prm[:, PC_KA:PC_KA + 4], scalar1=-1.0, scalar2=1.0,
                                              op0=ALU.mult, op1=ALU.add), reads=['prm'], writes=['omka'])
        S.op('dve', lambda e: e.memset(epsc[:, 0:1], RMS_EPS), writes=['epsc'])
        S.op('dve', lambda e: e.tensor_scalar(out=XH[:].rearrange("p a b -> p (a b)"), in0=ident[:, 0:16], scalar1=0.0, scalar2=None, op0=ALU.mult),
             reads=['ident'], writes=['XH'])
        S.op('dve', lambda e: e.memset(epsc[:, 1:2], GN_EPS), writes=['epsc'])
        for i in range(4):
            S.op('dve', lambda e, i=i: e.tensor_scalar(out=ST[i][:], in0=ident[:, 0:64], scalar1=0.0, scalar2=None, op0=ALU.mult),
                 reads=['ident'], writes=[('ST', i)])
            S.op('dve', lambda e, i=i: e.tensor_scalar(out=STB[i][:], in0=ident[:, 0:64], scalar1=0.0, scalar2=None, op0=ALU.mult),
                 reads=['ident'], writes=[('STB', i)])

        def dump(idx, tile_, res, ap=None):
            if DUMP[0]:
                src = ap if ap is not None else tile_[:]
                S.dma('sp', lambda e: e.dma_start(out=dbg_d[idx], in_=src), reads=[res], sem='dbg')

        def rmsnorm_to(dst_fn, gcol, sqs, rss, tiles):
            nb_ = len(sqs)
            st = {}

            def stage_a(tix):
                c0, n = tiles[tix]
                kk_ = tix % nb_
                tmp_sq, nsq = sqs[kk_], ('nsq', kk_)
                for dc in range(8):
                    S.op('act', lambda e, dc=dc, c0=c0, n=n, tmp_sq=tmp_sq: e.activation(out=tmp_sq[:, dc, 0:n], in_=xT[:, dc, c0:c0 + n], func=AF.Square),
                         reads=[('xT', dc, c0)], writes=[nsq])
                pb, pr = bank()
                for dc in range(8):
                    S.op('pe', lambda e, dc=dc, n=n, pb=pb, tmp_sq=tmp_sq: e.matmul(pb[:, 0:n], lhsT=ones_b[:], rhs=tmp_sq[:, dc, 0:n], start=(dc == 0), stop=(dc == 7)),
                         reads=[nsq, 'cst'], writes=[pr], signal=(dc == 7))
                st[tix] = (pb, pr)

            def stage_b(tix):
                c0, n = tiles[tix]
                kk_ = tix % nb_
                tmp_rs, nrs = rss[kk_], ('nrs', kk_)
                pb, pr = st.pop(tix)
                S.op('act', lambda e, n=n, pb=pb, tmp_rs=tmp_rs: e.activation(out=tmp_rs[:, 0:n], in_=pb[:, 0:n], func=AF.Ln, bias=epsc[:, 0:1], scale=1.0 / D),
                     reads=[pr, 'epsc'], writes=[nrs])
                S.op('act', lambda e, n=n, tmp_rs=tmp_rs: e.activation(out=tmp_rs[:, 0:n], in_=tmp_rs[:, 0:n], func=AF.Exp, scale=-0.5), reads=[nrs], writes=[nrs])
                for dc in range(8):
                    oap, ores = dst_fn(dc, c0, n)
                    S.op('dve', lambda e, dc=dc, c0=c0, n=n, oap=oap, tmp_rs=tmp_rs: e.scalar_tensor_tensor(
                        out=oap, in0=xT[:, dc, c0:c0 + n], scalar=prm[:, gcol + dc:gcol + dc + 1], in1=tmp_rs[:, 0:n],
                        op0=ALU.mult, op1=ALU.mult), reads=[('xT', dc, c0), nrs, 'prm'], writes=[ores])

            ahead = 1 if nb_ > 1 else 0
            for tix in range(min(ahead, len(tiles))):
                stage_a(tix)
            for tix in range(len(tiles)):
                if tix + ahead < len(tiles) and ahead:
                    stage_a(tix + ahead)
                elif not ahead:
                    stage_a(tix)
                stage_b(tix)

        def xn_dst(dc, c0, n):
            return xn[:, dc, c0:c0 + n], ('xn', c0)

        def ffn(tag, gcol, wg, wu, wd, tiles):
            with ExitStack() as ph:
                sq = [sb(ph, tag + "sq%d" % i, [128, 8, 512], BF16) for i in range(2)]
                rs = [sb(ph, tag + "rs%d" % i, [128, 512], F32) for i in range(2)]
                sg = [sb(ph, tag + "sg%d" % i, [128, 512], F32) for i in range(2)]
                act = [sb(ph, tag + "act%d" % i, [128, 4, NTH], BF16) for i in range(2)]
                wgb = [sb(ph, tag + "wg%d" % i, [128, 8, 512], BF16) for i in range(2)]
                wub = [sb(ph, tag + "wu%d" % i, [128, 8, 512], BF16) for i in range(2)]
                wdb = [sb(ph, tag + "wd%d" % i, [128, 4, D], BF16) for i in range(2)]
                rmsnorm_to(xn_dst, gcol, sq, rs, tiles)
                wgv = wg.rearrange("(dc p) f -> p dc f", p=128)
                wuv = wu.rearrange("(dc p) f -> p dc f", p=128)
                wdv = wd.rearrange("(j p) d -> p j d", p=128)
                ngr = 6
                sgi = 0
                for g in range(ngr):
                    s = g % 2
                    nf = 4 if g < 5 else 2
                    f0 = g * 512
                    S.dma('pool', lambda e, s=s, f0=f0, nf=nf: e.dma_start(out=wgb[s][:, :, 0:nf * 128], in_=wgv[:, :, f0:f0 + nf * 128]),
                          writes=[(tag, 'wg', s)], sem=tag + 'wg%d' % s)
                    S.dma('pool', lambda e, s=s, f0=f0, nf=nf: e.dma_start(out=wub[s][:, :, 0:nf * 128], in_=wuv[:, :, f0:f0 + nf * 128]),
                          writes=[(tag, 'wu', s)], sem=tag + 'wu%d' % s)
                    S.dma('pool', lambda e, s=s, g=g, nf=nf: e.dma_start(out=wdb[s][:, 0:nf, :], in_=wdv[:, g * 4:g * 4 + nf, :]),
                          writes=[(tag, 'wd', s)], sem=tag + 'wd%d' % s)
                    for j in range(nf):
                        for (c0, n) in tiles:
                            pg, rg = bank()
                            pu, ru = bank()
                            for dc in range(8):
                                S.op('pe', lambda e, s=s, j=j, dc=dc, c0=c0, n=n, pg=pg: e.matmul(
                                    pg[:, 0:n], lhsT=wgb[s][:, dc, j * 128:(j + 1) * 128], rhs=xn[:, dc, c0:c0 + n], start=(dc == 0), stop=(dc == 7)),
                                    reads=[(tag, 'wg', s), ('xn', c0)], writes=[rg], signal=(dc == 7))
                            for dc in range(8):
                                S.op('pe', lambda e, s=s, j=j, dc=dc, c0=c0, n=n, pu=pu: e.matmul(
                                    pu[:, 0:n], lhsT=wub[s][:, dc, j * 128:(j + 1) * 128], rhs=xn[:, dc, c0:c0 + n], start=(dc == 0), stop=(dc == 7)),
                                    reads=[(tag, 'wu', s), ('xn', c0)], writes=[ru], signal=(dc == 7))
                            k = sgi % 2
                            sgi += 1
                            S.op('act', lambda e, k=k, n=n, pg=pg: e.activation(out=sg[k][:, 0:n], in_=pg[:, 0:n], func=AF.Silu),
                                 reads=[rg], writes=[(tag, 'sg', k)])
                            S.op('dve', lambda e, k=k, s=s, j=j, c0=c0, n=n, pu=pu: e.tensor_tensor(
                                out=act[s][:, j, c0:c0 + n], in0=sg[k][:, 0:n], in1=pu[:, 0:n], op=ALU.mult),
                                reads=[(tag, 'sg', k), ru], writes=[(tag, 'act', s, c0)])
                    for dc in range(8):
                        for (c0, n) in tiles:
                            pd, rd = bank()
                            for j in range(nf):
                                S.op('pe', lambda e, s=s, j=j, dc=dc, c0=c0, n=n, pd=pd, nf=nf: e.matmul(
                                    pd[:, 0:n], lhsT=wdb[s][:, j, dc * 128:(dc + 1) * 128], rhs=act[s][:, j, c0:c0 + n], start=(j == 0), stop=(j == nf - 1)),
                                    reads=[(tag, 'wd', s), (tag, 'act', s, c0)], writes=[rd], signal=(j == nf - 1))
                            S.op('dve', lambda e, dc=dc, c0=c0, n=n, pd=pd: e.scalar_tensor_tensor(
                                out=xT[:, dc, c0:c0 + n], in0=pd[:, 0:n], scalar=0.5, in1=xT[:, dc, c0:c0 + n], op0=ALU.mult, op1=ALU.add),
                                reads=[rd, ('xT', dc, c0)], writes=[('xT', dc, c0)])
                S.barrier()

        def load_segment(seg):
            with ExitStack() as ph:
                xtok = [sb(ph, "xtok%d" % i, [128, 4, D], F32) for i in range(2)]
                for ti in range(4):
                    s = ti % 2
                    S.dma('sp', lambda e, s=s, ti=ti: e.dma_start(out=xtok[s][:], in_=x4[seg, ti * 512:(ti + 1) * 512, :].rearrange("(n p) d -> p n d", p=128)),
                          writes=[('xtok', s)], sem='xt%d' % s)
                    for dc in range(8):
                        pb, pr = bank()
                        for n4 in range(4):
                            S.op('pe', lambda e, s=s, n4=n4, dc=dc, pb=pb: e.transpose(pb[:, n4 * 128:(n4 + 1) * 128], xtok[s][:, n4, dc * 128:(dc + 1) * 128], ident[:]),
                                 reads=[('xtok', s), 'ident'], writes=[pr], signal=(n4 == 3))
                        eng = 'dve' if dc % 2 == 0 else 'act'
                        if eng == 'dve':
                            S.op('dve', lambda e, dc=dc, ti=ti, pb=pb: e.tensor_copy(out=xT[:, dc, ti * 512:(ti + 1) * 512], in_=pb[:, :]),
                                 reads=[pr], writes=[('xT', dc, ti * 512)])
                        else:
                            S.op('act', lambda e, dc=dc, ti=ti, pb=pb: e.activation(out=xT[:, dc, ti * 512:(ti + 1) * 512], in_=pb[:, :], func=AF.Copy),
                                 reads=[pr], writes=[('xT', dc, ti * 512)])
                S.op('dve', lambda e: e.tensor_copy(out=xT[:, :, NT:NTH], in_=XH[:]), reads=['XH'], writes=[('xT', dc, NT) for dc in range(8)])
                S.barrier()

        def store_output():
            with ExitStack() as ph:
                sq = [sb(ph, "osq%d" % i, [128, 8, 512], BF16) for i in range(2)]
                rs = [sb(ph, "ors%d" % i, [128, 512], F32) for i in range(2)]
                otok = [sb(ph, "otok%d" % i, [128, D], F32) for i in range(2)]

                def dst(dc, c0, n):
                    return xT[:, dc, c0:c0 + n], ('xT', dc, c0)
                rmsnorm_to(dst, PC_FIN, sq, rs, TILES[:4])
                toks = []
                for tk in range(16):
                    s = tk % 2
                    for half in range(2):
                        pb, pr = bank()
                        for q in range(4):
                            dc = half * 4 + q
                            S.op('pe', lambda e, dc=dc, q=q, tk=tk, pb=pb: e.transpose(pb[:, q * 128:(q + 1) * 128], xT[:, dc, tk * 128:(tk + 1) * 128], ident[:]),
                                 reads=[('xT', dc, (tk // 4) * 512), 'ident'], writes=[pr], signal=(q == 3))
                        if half == 0:
                            S.op('dve', lambda e, s=s, pb=pb: e.tensor_copy(out=otok[s][:, 0:512], in_=pb[:, :]), reads=[pr], writes=[('otok', s)])
                        else:
                            S.op('act', lambda e, s=s, pb=pb: e.activation(out=otok[s][:, 512:1024], in_=pb[:, :], func=AF.Copy), reads=[pr], writes=[('otok', s)])
                    toks.append(S.dma('sp', lambda e, s=s, tk=tk: e.dma_start(out=out_d[tk * 128:(tk + 1) * 128, :], in_=otok[s][:]),
                                      reads=[('otok', s)], sem='out%d' % s))
                S.wait_all('sp', toks[-2:])
                S.barrier()

        winv = win.rearrange("(dc p) f -> p dc f", p=128)

        def mixer(full):
            if full:
                dump(14, None, ('xT', 0, 0), ap=xT[:, 0, 0:512])
            with ExitStack() as ph:
                sq = [sb(ph, "msq%d" % i, [128, 8, 512], BF16) for i in range(2)]
                rs = [sb(ph, "mrs%d" % i, [128, 512], F32) for i in range(2)]
                rmsnorm_to(xn_dst, PC_MXN, sq, rs, TILES)
                S.barrier()
            outer = ExitStack()
            OPEN.append(outer)
            if full:
                YG = sb(outer, "YG", [128, 4, NT], BF16)
            with ExitStack() as ph:
                LU = sb(ph, "LU", [128, 512], BF16)
                S.dma('pool', lambda e: e.dma_start(out=LU[0:64, :], in_=wdu), writes=['LU'], sem='lu')
                S.dma('pool', lambda e: e.dma_start(out=LU[64:128, :], in_=wiu), writes=['LU'], sem='lu')
                wrkv = [sb(ph, "wrkv%d" % i, [128, 8, 3, 128], BF16) for i in range(1 if full else 2)]
                LW = sb(ph, "LW", [128, NT], BF16)
                Pb = [sb(ph, "Pb%d" % i, [128, 516], F32) for i in range(1 if full else 2)]
                dtmp = sb(ph, "dtmp", [128, 512], F32)
                if full:
                    WGU = sb(ph, "WGU", [128, 512], BF16)
                    S.dma('pool', lambda e: e.dma_start(out=WGU[:], in_=wgu), writes=['WGU'], sem='wgu')
                    SG = sb(ph, "SG", [128, NT], BF16)
                LASTT = sb(ph, "lastt", [128, 16], F32)
                LAST = {k_: LASTT[:, 2 * i_:2 * i_ + 2] for i_, k_ in enumerate(('l0', 'l1', 'r', 'k', 'v'))}
                pbi = [0]

                def proj_mix(wfn, wres, mucol, out_ap, out_res, ti, key=None):
                    c0 = ti * 512
                    cur = Pb[pbi[0] % len(Pb)]
                    cr = ('Pb', pbi[0] % len(Pb))
                    pbi[0] += 1
                    pb, pr = bank()
                    for dc in range(8):
                        S.op('pe', lambda e, dc=dc, c0=c0, pb=pb: e.matmul(pb[:, :], lhsT=wfn(dc), rhs=xn[:, dc, c0:c0 + 512], start=(dc == 0), stop=(dc == 7)),
                             reads=[wres, ('xn', c0)], writes=[pr], signal=(dc == 7))
                    S.op('act', lambda e, cur=cur, pb=pb: e.activation(out=cur[:, 4:516], in_=pb[:, :], func=AF.Copy), reads=[pr], writes=[cr])
                    if ti == 0:
                        ph_, phr = bank()
                        for dc in range(8):
                            S.op('pe', lambda e, dc=dc, ph_=ph_: e.matmul(ph_[:, 0:2], lhsT=wfn(dc), rhs=xn[:, dc, NT:NTH], start=(dc == 0), stop=(dc == 7)),
                                 reads=[wres, ('xn', NT)], writes=[phr], signal=(dc == 7))
                        S.op('dve', lambda e, cur=cur, ph_=ph_: e.tensor_copy(out=cur[:, 2:4], in_=ph_[:, 0:2]), reads=[phr], writes=[cr])
                    else:
                        S.op('dve', lambda e, cur=cur, key=key: e.tensor_copy(out=cur[:, 2:4], in_=LAST[key]), reads=[('last', key)], writes=[cr])
                    S.op('dve', lambda e, cur=cur, key=key: e.tensor_copy(out=LAST[key], in_=cur[:, 514:516]), reads=[cr], writes=[('last', key)])
                    S.op('dve', lambda e, cur=cur: e.tensor_tensor(out=dtmp[:], in0=cur[:, 3:515], in1=cur[:, 4:516], op=ALU.subtract),
                         reads=[cr], writes=['dtmp'])
                    S.op('dve', lambda e, cur=cur: e.scalar_tensor_tensor(out=out_ap, in0=dtmp[:], scalar=prm[:, mucol:mucol + 1], in1=cur[:, 4:516],
                                                                           op0=ALU.mult, op1=ALU.add), reads=['dtmp', cr, 'prm'], writes=[out_res])

                lora_ph = ExitStack()
                wl = sb(lora_ph, "wl", [128, 8, 256], BF16)
                S.dma('pool', lambda e: e.dma_start(out=wl[:], in_=winv[:, :, 3072:3328]), writes=['wl'], sem='wl')
                ltmp = sb(lora_ph, "ltmp", [128, 512], F32)
                for ti in range(4):
                    proj_mix(lambda dc: wl[:, dc, 0:128], 'wl', PC_MU + 12, ltmp[:], 'ltmp', ti, key='l0')
                    S.op('act', lambda e, ti=ti: e.activation(out=LW[0:64, ti * 512:(ti + 1) * 512], in_=ltmp[0:64, :], func=AF.Tanh),
                         reads=['ltmp'], writes=[('LW', ti)])
                    S.op('dve', lambda e, ti=ti: e.tensor_copy(out=LW[64:128, ti * 512:(ti + 1) * 512], in_=ltmp[64:128, :]),
                         reads=['ltmp'], writes=[('LW', ti)])
                if full:
                    for ti in range(4):
                        proj_mix(lambda dc: wl[:, dc, 128:256], 'wl', PC_MU + 13, ltmp[:], 'ltmp', ti, key='l1')
                        S.op('act', lambda e, ti=ti: e.activation(out=SG[:, ti * 512:(ti + 1) * 512], in_=ltmp[:], func=AF.Sigmoid),
                             reads=['ltmp'], writes=[('SG', ti)])

                ck(2)
                S.barrier()
                lora_ph.close()
                def f32t(n):
                    return sb(ph, n, [128, 512], F32)
                Rm, Km, Vm = f32t("Rm"), f32t("Km"), f32t("Vm")
                LG, IC, KKt, K2, Bt = f32t("LG"), f32t("IC"), f32t("KKt"), f32t("K2"), f32t("Bt")
                Li, E1, E2, T1 = f32t("Li"), f32t("E1"), f32t("E2"), f32t("T1")
                sqb = sb(ph, "sqb", [128, 512], BF16)
                nslot = 2
                nrot[0] = 7 if full else 8
                OPS = [(sb(ph, "OP1s%d" % i, [128, 1024], BF16),
                        sb(ph, "OP2s%d" % i, [128, 1024], BF16),
                        sb(ph, "TSs%d" % i, [128, 4, 512], BF16),
                        sb(ph, "PCts%d" % i, [128, 8], F32),
                        sb(ph, "RKPs%d" % i, [128, 512], BF16) if full else None) for i in range(nslot)]
                def pair(n, shape, dt):
                    return [sb(ph, "%s_g%d" % (n, g), shape, dt) for g in range(2)]
                G_RH1 = pair("RH1", [128, 512], BF16)
                G_RH2 = pair("RH2", [128, 512], BF16)
                G_KH = pair("KH", [128, 256], BF16)
                G_VT = pair("VT", [128, 256], BF16)
                G_NRK = pair("NRK", [128, 256], BF16)
                G_XA = [[sb(ph, "XA%d_g%d" % (i, g), [128, 512], BF16) for i in range(2)] for g in range(2)]
                G_XT = [[sb(ph, "XT%d_g%d" % (i, g), [128, 256], BF16) for i in range(2)] for g in range(2)]
                G_TT = pair("TT", [128, 256], BF16)
                G_AW = pair("AW", [128, 512], BF16)
                G_PHI = pair("PHI", [128, 256], BF16)
                G_QQ = pair("QQ", [128, 256], BF16)
                G_HH = pair("HH", [128, 256], BF16)
                G_GG = pair("GG", [128, 256], BF16)
                if full:
                    YT = f32t("YT")
                    YTb = sb(ph, "YTb", [128, 512], BF16)
                    Ysq = sb(ph, "Ysq", [128, 512], BF16)
                    MU_ = IC
                    VAR = LG

                if DBG[0] == -1:
                    print('SBUF remaining after RWKV allocs (full=%s): %d B' % (full, nc.sbuf_bytes_remaining))
                def v3(ap, k=64):
                    return ap.rearrange("p (c k) -> p c k", k=k)

                def opv(t, which):
                    return t[:, :].rearrange("p (c w k) -> p c w k", w=2, k=64)[:, :, which, :]

                def prep_gen(hp, ti, slot):
                    ws = hp % len(wrkv)
                    hc = hp
                    tc0 = ti * 512
                    OP1, OP2, TS, PCt, RKP = OPS[slot]
                    if ti == 0:
                        for q in range(3):
                            col = 1536 + q * 512 + hp * 128
                            S.dma('pool', lambda e, ws=ws, q=q, col=col: e.dma_start(out=wrkv[ws][:, :, q, :], in_=winv[:, :, col:col + 128]),
                                  writes=[('wrkv', ws)], sem='wrkv%d' % ws)
                    if full:
                        proj_mix(lambda dc, ws=ws: wrkv[ws][:, dc, 0, :], ('wrkv', ws), PC_MU + 0 + hp, Rm[:], 'Rm', ti, key='r')
                        yield
                    proj_mix(lambda dc, ws=ws: wrkv[ws][:, dc, 1, :], ('wrkv', ws), PC_MU + 4 + hp, Km[:], 'Km', ti, key='k')
                    yield
                    proj_mix(lambda dc, ws=ws: wrkv[ws][:, dc, 2, :], ('wrkv', ws), PC_MU + 8 + hp, Vm[:], 'Vm', ti, key='v')
                    yield
                    if full and hp == 0 and ti == 0:
                        dump(0, Rm, 'Rm'); dump(1, Km, 'Km'); dump(2, Vm, 'Vm')
                    yield
                    pz, pzr = bank()
                    S.op('pe', lambda e, pz=pz, hp=hp, tc0=tc0: e.matmul(pz[:, :], lhsT=LU[0:64, hp * 128:(hp + 1) * 128], rhs=LW[0:64, tc0:tc0 + 512], start=True, stop=True),
                         reads=['LU', ('LW', ti)], writes=[pzr])
                    S.op('act', lambda e, pz=pz, hc=hc: e.activation(out=LG[:], in_=pz[:, :], func=AF.Sigmoid, bias=prm[:, PC_W0 + hc:PC_W0 + hc + 1], scale=1.0),
                         reads=[pzr, 'prm'], writes=['LG'])
                    pi, pir = bank()
                    S.op('pe', lambda e, pi=pi, hp=hp, tc0=tc0: e.matmul(pi[:, :], lhsT=LU[64:128, hp * 128:(hp + 1) * 128], rhs=LW[64:128, tc0:tc0 + 512], start=True, stop=True),
                         reads=['LU', ('LW', ti)], writes=[pir])
                    S.op('act', lambda e, pi=pi, hc=hc: e.activation(out=IC[:], in_=pi[:, :], func=AF.Sigmoid, bias=prm[:, PC_A0 + hc:PC_A0 + hc + 1], scale=1.0),
                         reads=[pir, 'prm'], writes=['IC'])
                    S.op('dve', lambda e: e.tensor_scalar(out=LG[:], in0=LG[:], scalar1=DECAY_C, scalar2=None, op0=ALU.mult), reads=['LG'], writes=['LG'])
                    yield
                    S.op('dve', lambda e, hc=hc: e.tensor_scalar(out=KKt[:], in0=Km[:], scalar1=prm[:, PC_KK + hc:PC_KK + hc + 1], scalar2=None, op0=ALU.mult),
                         reads=['Km', 'prm'], writes=['KKt'])
                    S.op('act', lambda e: e.activation(out=sqb[:], in_=KKt[:], func=AF.Square), reads=['KKt'], writes=['sqb'])
                    pn, pnr = bank()
                    S.op('pe', lambda e, pn=pn: e.matmul(pn[:, :], lhsT=blk_b[:], rhs=sqb[:], start=True, stop=True), reads=['cst', 'sqb'], writes=[pnr])
                    S.op('dve', lambda e, pn=pn: e.tensor_scalar(out=T1[:], in0=pn[:, :], scalar1=1e-18, scalar2=None, op0=ALU.max), reads=[pnr], writes=['T1'])
                    S.op('act', lambda e: e.activation(out=T1[:], in_=T1[:], func=AF.Ln), reads=['T1'], writes=['T1'])
                    S.op('act', lambda e: e.activation(out=T1[:], in_=T1[:], func=AF.Exp, scale=-0.5), reads=['T1'], writes=['T1'])
                    S.op('dve', lambda e: e.tensor_tensor(out=KKt[:], in0=KKt[:], in1=T1[:], op=ALU.mult), reads=['KKt', 'T1'], writes=['KKt'])
                    yield
                    S.op('dve', lambda e, hc=hc: e.tensor_scalar(out=T1[:], in0=IC[:], scalar1=prm[:, PC_KA + hc:PC_KA + hc + 1], scalar2=omka[:, hc:hc + 1],
                                                                 op0=ALU.mult, op1=ALU.add), reads=['IC', 'prm', 'omka'], writes=['T1'])
                    S.op('dve', lambda e: e.tensor_tensor(out=K2[:], in0=Km[:], in1=T1[:], op=ALU.mult), reads=['Km', 'T1'], writes=['K2'])
                    S.op('dve', lambda e: e.tensor_tensor(out=Bt[:], in0=KKt[:], in1=IC[:], op=ALU.mult), reads=['KKt', 'IC'], writes=['Bt'])
                    if full and hp == 0 and ti == 0:
                        dump(3, LG, 'LG'); dump(4, IC, 'IC'); dump(5, KKt, 'KKt'); dump(6, K2, 'K2')
                    yield
                    S.op('dve', lambda e: e.tensor_tensor_scan(out=Li[:], data0=rmask[:], data1=LG[:], initial=0.0, op0=ALU.mult, op1=ALU.add),
                         reads=['rmask', 'LG'], writes=['Li'])
                    yield
                    if full:
                        S.op('act', lambda e: e.activation(out=E1[:], in_=Li[:], func=AF.Exp), reads=['Li'], writes=['E1'])
                        S.op('dve', lambda e: e.tensor_tensor(out=opv(OP1, 1), in0=v3(Rm[:]), in1=v3(E1[:]), op=ALU.mult), reads=['Rm', 'E1'], writes=[('OP1', slot)])
                    yield
                    S.op('act', lambda e: e.activation(out=E2[:], in_=Li[:], func=AF.Exp, scale=-1.0), reads=['Li'], writes=['E2'])
                    S.op('dve', lambda e: e.tensor_tensor(out=opv(OP2, 0), in0=v3(Bt[:]), in1=v3(E2[:]), op=ALU.mult), reads=['Bt', 'E2'], writes=[('OP2', slot)])
                    S.op('dve', lambda e: e.tensor_tensor(out=opv(OP2, 1), in0=v3(K2[:]), in1=v3(E2[:]), op=ALU.mult), reads=['K2', 'E2'], writes=[('OP2', slot)])
                    yield
                    S.op('dve', lambda e: e.tensor_tensor(out=T1[:], in0=Li[:], in1=LG[:], op=ALU.subtract), reads=['Li', 'LG'], writes=['T1'])
                    S.op('act', lambda e: e.activation(out=E1[:], in_=T1[:], func=AF.Exp), reads=['T1'], writes=['E1'])
                    S.op('dve', lambda e: e.scalar_tensor_tensor(out=TS[:, 0, :], in0=KKt[:], scalar=-1.0, in1=E1[:], op0=ALU.mult, op1=ALU.mult),
                         reads=['KKt', 'E1'], writes=[('TS', slot)])
                    S.op('dve', lambda e: e.tensor_copy(out=opv(OP1, 0), in_=v3(TS[:, 0, :])), reads=[('TS', slot)], writes=[('OP1', slot)])
                    yield
                    S.op('dve', lambda e: e.tensor_tensor(out=v3(T1[:]), in0=v3(Li[:])[:, :, 63:64].to_broadcast([128, 8, 64]), in1=v3(Li[:]), op=ALU.subtract),
                         reads=['Li'], writes=['T1'])
                    S.op('act', lambda e: e.activation(out=E2[:], in_=T1[:], func=AF.Exp), reads=['T1'], writes=['E2'])
                    S.op('act', lambda e: e.activation(out=PCt[:], in_=v3(Li[:])[:, :, 63], func=AF.Exp), reads=['Li'], writes=[('PCt', slot)])
                    S.op('dve', lambda e: e.tensor_tensor(out=TS[:, 1, :], in0=Bt[:], in1=E2[:], op=ALU.mult), reads=['Bt', 'E2'], writes=[('TS', slot)])
                    S.op('dve', lambda e: e.tensor_tensor(out=TS[:, 2, :], in0=K2[:], in1=E2[:], op=ALU.mult), reads=['K2', 'E2'], writes=[('TS', slot)])
                    S.op('act', lambda e: e.activation(out=TS[:, 3, :], in_=Vm[:], func=AF.Copy), reads=['Vm'], writes=[('TS', slot)])
                    if full:
                        S.op('dve', lambda e, hc=hc: e.scalar_tensor_tensor(out=RKP[:], in0=Rm[:], scalar=prm[:, PC_RK + hc:PC_RK + hc + 1], in1=K2[:], op0=ALU.mult, op1=ALU.mult),
                             reads=['Rm', 'K2', 'prm'], writes=[('RKP', slot)])

                def grp_gen(grp, slot, hp, py, pyr):
                    OP1, OP2, TS, PCt, RKP = OPS[slot]
                    RH1, RH2, KH, VT, NRK = G_RH1[grp], G_RH2[grp], G_KH[grp], G_VT[grp], G_NRK[grp]
                    XA, XTt, TT, AW = G_XA[grp], G_XT[grp], G_TT[grp], G_AW[grp]
                    PHI, QQ, HH, GG = G_PHI[grp], G_QQ[grp], G_HH[grp], G_GG[grp]
                    pt = [bank(), bank()]
                    for u in range(4):
                        c = grp * 4 + u
                        for src in range(4):
                            for h in range(2):
                                hs = slice(h * 64, (h + 1) * 64)
                                S.op('pe', lambda e, hs=hs, u=u, src=src, c=c, ptb=pt[src // 2][0]: e.matmul(
                                    ptb[hs, ((src % 2) * 4 + u) * 64:((src % 2) * 4 + u + 1) * 64], lhsT=TS[hs, src, c * 64:(c + 1) * 64], rhs=id2[hs, :], start=True, stop=True),
                                    reads=[('TS', slot), 'cst'], writes=[pt[src // 2][1]], signal=False)
                    S.signal_last('pe')
                    pv0 = pt[0][0][:, :].rearrange("p (s u k) -> p s u k", u=4, k=64)
                    pv1 = pt[1][0][:, :].rearrange("p (s u k) -> p s u k", u=4, k=64)
                    S.op('act', lambda e: e.activation(out=RH1[:, :].rearrange("p (u w k) -> p u w k", w=2, k=64)[:, :, 0, :], in_=pv0[:, 0, :, :], func=AF.Copy),
                         reads=[pt[0][1]], writes=[('RH1', grp)])
                    S.op('act', lambda e: e.activation(out=RH2[:, :].rearrange("p (u w k) -> p u w k", w=2, k=64)[:, :, 0, :], in_=pv0[:, 1, :, :], func=AF.Copy),
                         reads=[pt[0][1]], writes=[('RH2', grp)])
                    S.op('act', lambda e: e.activation(out=v3(KH[:]), in_=pv1[:, 0, :, :], func=AF.Copy), reads=[pt[1][1]], writes=[('KH', grp)])
                    S.op('act', lambda e: e.activation(out=v3(VT[:]), in_=pv1[:, 1, :, :], func=AF.Copy), reads=[pt[1][1]], writes=[('VT', grp)])
                    yield
                    wq = 128 if full else 64
                    px, pxr = bank()
                    pyy, pyyr = bank()
                    if full:
                        pzz, pzzr = bank()
                    for u in range(4):
                        c = grp * 4 + u
                        for h in range(2):
                            hs = slice(h * 64, (h + 1) * 64)
                            S.op('pe', lambda e, hs=hs, u=u, c=c, px=px: e.matmul(px[hs, u * 128:u * 128 + wq], lhsT=OP2[hs, c * 128:c * 128 + 64],
                                                                                 rhs=OP1[hs, c * 128:c * 128 + wq], start=True, stop=True),
                                 reads=[('OP1', slot), ('OP2', slot)], writes=[pxr], signal=False)
                            S.op('pe', lambda e, hs=hs, u=u, c=c, pyy=pyy: e.matmul(pyy[hs, u * 128:(u + 1) * 128], lhsT=OP1[hs, c * 128:c * 128 + 64],
                                                                                   rhs=OP2[hs, c * 128:(c + 1) * 128], start=True, stop=True),
                                 reads=[('OP1', slot), ('OP2', slot)], writes=[pyyr], signal=False)
                            if full:
                                S.op('pe', lambda e, hs=hs, u=u, c=c, pzz=pzz: e.matmul(pzz[hs, u * 64:(u + 1) * 64], lhsT=OP2[hs, c * 128 + 64:(c + 1) * 128],
                                                                                       rhs=OP1[hs, c * 128 + 64:(c + 1) * 128], start=True, stop=True),
                                     reads=[('OP1', slot), ('OP2', slot)], writes=[pzzr], signal=False)
                    S.signal_last('pe')

                    def xv(t, which):
                        return t[:, :].rearrange("p (u w k) -> p u w k", w=2, k=64)[:, :, which, :]
                    xa, xb = XA[0], XA[1]
                    xta, xtb = XTt[0], XTt[1]
                    S.op('dve', lambda e, px=px, xa=xa: e.tensor_tensor(out=xv(xa, 0), in0=xv(px, 0), in1=v3(msu[:]), op=ALU.mult),
                         reads=[pxr, 'cst'], writes=[('XA', grp, 0)])
                    S.op('dve', lambda e, xa=xa: e.tensor_copy(out=xv(xa, 1), in_=v3(id2x4[:])), reads=['cst'], writes=[('XA', grp, 0)])
                    if full:
                        S.op('dve', lambda e, px=px: e.tensor_tensor(out=xv(RH2, 1), in0=xv(px, 1), in1=v3(miu[:]), op=ALU.mult),
                             reads=[pxr, 'cst'], writes=[('RH2', grp)])
                    S.op('dve', lambda e, pyy=pyy, xta=xta: e.tensor_tensor(out=v3(xta[:]), in0=xv(pyy, 0), in1=v3(msl[:]), op=ALU.mult),
                         reads=[pyyr, 'cst'], writes=[('XT', grp, 0)])
                    S.op('dve', lambda e, pyy=pyy: e.tensor_tensor(out=xv(RH1, 1), in0=xv(pyy, 1), in1=v3(msl[:]), op=ALU.mult),
                         reads=[pyyr, 'cst'], writes=[('RH1', grp)])
                    if full:
                        S.op('dve', lambda e, pzz=pzz: e.tensor_tensor(out=v3(NRK[:]), in0=v3(pzz[:, 0:256]), in1=v3(miu[:]), op=ALU.mult),
                             reads=[pzzr, 'cst'], writes=[('NRK', grp)])
                    yield
                    cur = 0
                    for lvl in range(5):
                        xc, xn_ = XA[cur], XA[1 - cur]
                        tcur, tn_ = XTt[cur], XTt[1 - cur]
                        pa, par = bank()
                        pbb, pbr = bank()
                        for u in range(4):
                            for h in range(2):
                                hs = slice(h * 64, (h + 1) * 64)
                                S.op('pe', lambda e, hs=hs, u=u, pa=pa, xc=xc, tcur=tcur: e.matmul(
                                    pa[hs, u * 128:(u + 1) * 128], lhsT=tcur[hs, u * 64:(u + 1) * 64], rhs=xc[hs, u * 128:(u + 1) * 128], start=True, stop=True),
                                    reads=[('XA', grp, cur), ('XT', grp, cur)], writes=[par], signal=False)
                                S.op('pe', lambda e, hs=hs, u=u, pbb=pbb, xc=xc, tcur=tcur: e.matmul(
                                    pbb[hs, u * 64:(u + 1) * 64], lhsT=xc[hs, u * 128:u * 128 + 64], rhs=tcur[hs, u * 64:(u + 1) * 64], start=True, stop=True),
                                    reads=[('XA', grp, cur), ('XT', grp, cur)], writes=[pbr], signal=False)
                        S.signal_last('pe')
                        S.op('act', lambda e, pa=pa, xn_=xn_: e.activation(out=xv(xn_, 0), in_=xv(pa, 0), func=AF.Copy), reads=[par], writes=[('XA', grp, 1 - cur)])
                        S.op('dve', lambda e, pa=pa, xn_=xn_, xc=xc: e.tensor_tensor(out=xv(xn_, 1), in0=xv(pa, 1), in1=xv(xc, 1), op=ALU.add),
                             reads=[par, ('XA', grp, cur)], writes=[('XA', grp, 1 - cur)])
                        S.op('act', lambda e, pbb=pbb, tn_=tn_: e.activation(out=tn_[:, :], in_=pbb[:, 0:256], func=AF.Copy), reads=[pbr], writes=[('XT', grp, 1 - cur)])
                        cur = 1 - cur
                        yield
                    xc, tcur = XA[cur], XTt[cur]
                    pa, par = bank()
                    for u in range(4):
                        for h in range(2):
                            hs = slice(h * 64, (h + 1) * 64)
                            S.op('pe', lambda e, hs=hs, u=u, pa=pa, xc=xc, tcur=tcur: e.matmul(
                                pa[hs, u * 64:(u + 1) * 64], lhsT=tcur[hs, u * 64:(u + 1) * 64], rhs=xc[hs, u * 128 + 64:(u + 1) * 128], start=True, stop=True),
                                reads=[('XA', grp, cur), ('XT', grp, cur)], writes=[par], signal=False)
                    S.signal_last('pe')
                    S.op('dve', lambda e, pa=pa, xc=xc: e.tensor_tensor(out=v3(TT[:]), in0=v3(pa[:, 0:256]), in1=xv(xc, 1), op=ALU.add),
                         reads=[par, ('XA', grp, cur)], writes=[('TT', grp)])
                    yield
                    pa, par = bank()
                    for u in range(4):
                        for h in range(2):
                            hs = slice(h * 64, (h + 1) * 64)
                            S.op('pe', lambda e, hs=hs, u=u, pa=pa: e.matmul(pa[hs, u * 128:(u + 1) * 128], lhsT=TT[hs, u * 64:(u + 1) * 64],
                                                                             rhs=RH1[hs, u * 128:(u + 1) * 128], start=True, stop=True),
                                 reads=[('TT', grp), ('RH1', grp)], writes=[par], signal=False)
                    S.signal_last('pe')
                    S.op('act', lambda e, pa=pa: e.activation(out=AW[:, :], in_=pa[:, :], func=AF.Copy), reads=[par], writes=[('AW', grp)])
                    pq, pqr = bank()
                    phh, phr = bank()
                    for u in range(4):
                        for h in range(2):
                            hs = slice(h * 64, (h + 1) * 64)
                            S.op('pe', lambda e, hs=hs, u=u, pq=pq: e.matmul(pq[hs, u * 128:u * 128 + wq], lhsT=AW[hs, u * 128:u * 128 + 64],
                                                                             rhs=RH2[hs, u * 128:u * 128 + wq], start=True, stop=True),
                                 reads=[('AW', grp), ('RH2', grp)], writes=[pqr], signal=False)
                            S.op('pe', lambda e, hs=hs, u=u, phh=phh: e.matmul(phh[hs, u * 128:u * 128 + wq], lhsT=AW[hs, u * 128 + 64:(u + 1) * 128],
                                                                               rhs=RH2[hs, u * 128:u * 128 + wq], start=True, stop=True),
                                 reads=[('AW', grp), ('RH2', grp)], writes=[phr], signal=False)
                    S.signal_last('pe')
                    S.op('act', lambda e, pq=pq: e.activation(out=v3(PHI[:]), in_=xv(pq, 0), func=AF.Copy), reads=[pqr], writes=[('PHI', grp)])
                    S.op('dve', lambda e, phh=phh: e.tensor_tensor(out=v3(HH[:]), in0=xv(phh, 0), in1=v3(KH[:]), op=ALU.add), reads=[phr, ('KH', grp)], writes=[('HH', grp)])
                    if full:
                        S.op('dve', lambda e, pq=pq, grp=grp: e.tensor_tensor(
                            out=v3(QQ[:]), in0=xv(pq, 1), in1=OP1[:, grp * 512:(grp + 1) * 512].rearrange("p (u w k) -> p u w k", w=2, k=64)[:, :, 1, :], op=ALU.add),
                            reads=[pqr, ('OP1', slot)], writes=[('QQ', grp)])
                        S.op('dve', lambda e, phh=phh: e.tensor_tensor(out=v3(GG[:]), in0=xv(phh, 1), in1=v3(NRK[:]), op=ALU.add), reads=[phr, ('NRK', grp)], writes=[('GG', grp)])
                    yield
                    for u in range(4):
                        c = grp * 4 + u
                        if full:
                            for h in range(2):
                                hs = slice(h * 64, (h + 1) * 64)
                                S.op('pe', lambda e, hs=hs, u=u, c=c, py=py, hp=hp: e.matmul(py[hs, c * 64:(c + 1) * 64], lhsT=STB[hp][hs, :], rhs=QQ[hs, u * 64:(u + 1) * 64], start=True, stop=False),
                                     reads=[('STB', hp), ('QQ', grp)], writes=[pyr], signal=False)
                                S.op('pe', lambda e, hs=hs, u=u, c=c, py=py: e.matmul(py[hs, c * 64:(c + 1) * 64], lhsT=VT[hs, u * 64:(u + 1) * 64], rhs=GG[hs, u * 64:(u + 1) * 64], start=False, stop=True),
                                     reads=[('VT', grp), ('GG', grp)], writes=[pyr], signal=False)
                            S.signal_last('pe')
                        pss, psr = bank()
                        for h in range(2):
                            hs = slice(h * 64, (h + 1) * 64)
                            S.op('pe', lambda e, hs=hs, u=u, pss=pss, hp=hp: e.matmul(pss[hs, 0:64], lhsT=PHI[hs, u * 64:(u + 1) * 64], rhs=STB[hp][hs, :], start=True, stop=False),
                                 reads=[('PHI', grp), ('STB', hp)], writes=[psr], signal=False)
                            S.op('pe', lambda e, hs=hs, u=u, pss=pss: e.matmul(pss[hs, 0:64], lhsT=HH[hs, u * 64:(u + 1) * 64], rhs=VT[hs, u * 64:(u + 1) * 64], start=False, stop=True),
                                 reads=[('HH', grp), ('VT', grp)], writes=[psr], signal=False)
                        S.signal_last('pe')
                        S.op('dve', lambda e, c=c, pss=pss, hp=hp: e.scalar_tensor_tensor(out=STB[hp][:], in0=ST[hp][:], scalar=PCt[:, c:c + 1], in1=pss[:, 0:64], op0=ALU.mult, op1=ALU.add),
                             reads=[('ST', hp), ('PCt', slot), psr], writes=[('STB', hp)])
                        S.op('dve', lambda e, c=c, pss=pss, hp=hp: e.scalar_tensor_tensor(out=ST[hp][:], in0=ST[hp][:], scalar=PCt[:, c:c + 1], in1=pss[:, 0:64], op0=ALU.mult, op1=ALU.add),
                             reads=[('ST', hp), ('PCt', slot), psr], writes=[('ST', hp)])

                def gn_stage(hp, ti, slot, py, pyr):
                    hc = hp
                    tc0 = ti * 512
                    OP1, OP2, TS, PCt, RKP = OPS[slot]
                    S.op('act', lambda e, py=py: e.activation(out=YT[:], in_=py[:, :], func=AF.Copy), reads=[pyr], writes=['YT'])
                    if hp == 0 and ti == 0:
                        dump(7, YT, 'YT')
                    S.op('dve', lambda e: e.tensor_copy(out=YTb[:], in_=YT[:]), reads=['YT'], writes=['YTb'])
                    S.op('act', lambda e: e.activation(out=Ysq[:], in_=YT[:], func=AF.Square), reads=['YT'], writes=['Ysq'])
                    pm, pmr = bank()
                    pv, pvr = bank()
                    S.op('pe', lambda e, pm=pm: e.matmul(pm[:, :], lhsT=blk_b[:], rhs=YTb[:], start=True, stop=True), reads=['cst', 'YTb'], writes=[pmr])
                    S.op('pe', lambda e, pv=pv: e.matmul(pv[:, :], lhsT=blk_b[:], rhs=Ysq[:], start=True, stop=True), reads=['cst', 'Ysq'], writes=[pvr])
                    S.op('act', lambda e, pm=pm: e.activation(out=MU_[:], in_=pm[:, :], func=AF.Copy, scale=1.0 / 64), reads=[pmr], writes=['IC'])
                    S.op('dve', lambda e: e.tensor_tensor(out=VAR[:], in0=MU_[:], in1=MU_[:], op=ALU.mult), reads=['IC'], writes=['LG'])
                    S.op('dve', lambda e, pv=pv: e.scalar_tensor_tensor(out=VAR[:], in0=pv[:, :], scalar=1.0 / 64, in1=VAR[:], op0=ALU.mult, op1=ALU.subtract),
                         reads=[pvr, 'LG'], writes=['LG'])
                    S.op('act', lambda e: e.activation(out=VAR[:], in_=VAR[:], func=AF.Ln, bias=epsc[:, 1:2], scale=1.0), reads=['LG', 'epsc'], writes=['LG'])
                    S.op('act', lambda e: e.activation(out=VAR[:], in_=VAR[:], func=AF.Exp, scale=-0.5), reads=['LG'], writes=['LG'])
                    S.op('dve', lambda e: e.tensor_tensor(out=YT[:], in0=YT[:], in1=MU_[:], op=ALU.subtract), reads=['YT', 'IC'], writes=['YT'])
                    S.op('dve', lambda e: e.tensor_tensor(out=YT[:], in0=YT[:], in1=VAR[:], op=ALU.mult), reads=['YT', 'LG'], writes=['YT'])
                    S.op('dve', lambda e, hc=hc: e.tensor_scalar(out=YT[:], in0=YT[:], scalar1=prm[:, PC_LNW + hc:PC_LNW + hc + 1], scalar2=prm[:, PC_LNB + hc:PC_LNB + hc + 1],
                                                                 op0=ALU.mult, op1=ALU.add), reads=['YT', 'prm'], writes=['YT'])
                    pbn, pbnr = bank()
                    S.op('pe', lambda e, pbn=pbn: e.matmul(pbn[:, :], lhsT=blk_b[:], rhs=RKP[:], start=True, stop=True), reads=['cst', ('RKP', slot)], writes=[pbnr])
                    S.op('dve', lambda e, pbn=pbn: e.tensor_tensor(out=MU_[:], in0=pbn[:, :], in1=TS[:, 3, :], op=ALU.mult), reads=[pbnr, ('TS', slot)], writes=['IC'])
                    S.op('dve', lambda e: e.tensor_tensor(out=YT[:], in0=YT[:], in1=MU_[:], op=ALU.add), reads=['YT', 'IC'], writes=['YT'])
                    if hp == 0 and ti == 0:
                        dump(8, YT, 'YT')
                    pgt, pgr = bank()
                    S.op('pe', lambda e, pgt=pgt, hp=hp, tc0=tc0: e.matmul(pgt[:, :], lhsT=WGU[:, hp * 128:(hp + 1) * 128], rhs=SG[:, tc0:tc0 + 512], start=True, stop=True),
                         reads=['WGU', ('SG', ti)], writes=[pgr])
                    S.op('dve', lambda e, pgt=pgt, hp=hp, tc0=tc0: e.tensor_tensor(out=YG[:, hp, tc0:tc0 + 512], in0=YT[:], in1=pgt[:, :], op=ALU.mult),
                         reads=['YT', pgr], writes=['YG'])

                def drive(gl):
                    gl = list(gl)
                    while gl:
                        for g_ in list(gl):
                            try:
                                next(g_)
                            except StopIteration:
                                gl.remove(g_)

                tiles16 = [(hp_, ti_) for hp_ in range(4) for ti_ in range(4)]
                drive([prep_gen(tiles16[0][0], tiles16[0][1], 0)])
                for k_, (hp, ti) in enumerate(tiles16):
                    py, pyr = (PS[7], ('ps', 7)) if full else (None, None)
                    gl = [grp_gen(0, k_ % 2, hp, py, pyr), grp_gen(1, k_ % 2, hp, py, pyr)]
                    if k_ + 1 < len(tiles16):
                        gl.append(prep_gen(tiles16[k_ + 1][0], tiles16[k_ + 1][1], (k_ + 1) % 2))
                    drive(gl)
                    if full:
                        gn_stage(hp, ti, k_ % 2, py, pyr)
                S.barrier()
            if not full:
                outer.close()
                return
            ck(10)
            ph2 = outer
            YA = sb(ph2, "YA", [128, 4, NT], BF16)
            MG = sb(ph2, "MG", [128, 8, NT], BF16)
            sub = ExitStack()
            OPEN.append(sub)
            wc = [sb(sub, "wc%d" % i, [128, 8, 3, 128], BF16) for i in range(2)]
            Bc, Cc, Uc = f32c = [sb(sub, "cb%d" % i, [128, 512], F32) for i in range(3)]
            CU = [sb(sub, "CU%d" % i, [128, 516], F32) for i in range(2)]
            cacc = sb(sub, "cacc", [128, 512], F32)
            for i in range(4):
                ws = i % 2
                for q in range(3):
                    col = q * 512 + i * 128
                    S.dma('pool', lambda e, ws=ws, q=q, col=col: e.dma_start(out=wc[ws][:, :, q, :], in_=winv[:, :, col:col + 128]),
                          writes=[('wc', ws)], sem='wc%d' % ws)
                cw = PC_CONV + i * 3
                for ti in range(4):
                    c0 = ti * 512
                    cu = CU[ti % 2]
                    cup = CU[(ti + 1) % 2]
                    pbs = []
                    for q in range(3):
                        pb, pr = bank()
                        for dc in range(8):
                            S.op('pe', lambda e, ws=ws, q=q, dc=dc, c0=c0, pb=pb: e.matmul(pb[:, :], lhsT=wc[ws][:, dc, q, :], rhs=xn[:, dc, c0:c0 + 512], start=(dc == 0), stop=(dc == 7)),
                                 reads=[('wc', ws), ('xn', c0)], writes=[pr], signal=(dc == 7))
                        pbs.append((pb, pr))
                    S.op('act', lambda e, pb=pbs[0][0]: e.activation(out=Bc[:], in_=pb[:, :], func=AF.Copy), reads=[pbs[0][1]], writes=['Bc'])
                    S.op('act', lambda e, pb=pbs[1][0]: e.activation(out=Cc[:], in_=pb[:, :], func=AF.Copy), reads=[pbs[1][1]], writes=['Cc'])
                    S.op('dve', lambda e, pb=pbs[2][0], cu=cu: e.tensor_tensor(out=cu[:, 4:516], in0=Cc[:], in1=pb[:, :], op=ALU.mult),
                         reads=['Cc', pbs[2][1]], writes=[('CU', ti % 2)])
                    if ti == 0:
                        hb = []
                        for q in (1, 2):
                            pb, pr = bank()
                            for dc in range(8):
                                S.op('pe', lambda e, ws=ws, q=q, dc=dc, pb=pb: e.matmul(pb[:, 0:2], lhsT=wc[ws][:, dc, q, :], rhs=xn[:, dc, NT:NTH], start=(dc == 0), stop=(dc == 7)),
                                     reads=[('wc', ws), ('xn', NT)], writes=[pr], signal=(dc == 7))
                            hb.append((pb, pr))
                        S.op('act', lambda e, pb=hb[0][0]: e.activation(out=Cc[:, 0:2], in_=pb[:, 0:2], func=AF.Copy), reads=[hb[0][1], ('CU', 0)], writes=['Cc'])
                        S.op('dve', lambda e, pb=hb[1][0], cu=cu: e.tensor_tensor(out=cu[:, 2:4], in0=Cc[:, 0:2], in1=pb[:, 0:2], op=ALU.mult),
                             reads=['Cc', hb[1][1]], writes=[('CU', 0)])
                    else:
                        S.op('dve', lambda e, cu=cu, cup=cup: e.tensor_copy(out=cu[:, 2:4], in_=cup[:, 514:516]), reads=[('CU', (ti + 1) % 2)], writes=[('CU', ti % 2)])
                    S.op('dve', lambda e, cu=cu, cw=cw: e.tensor_scalar(out=cacc[:], in0=cu[:, 2:514], scalar1=prm[:, cw:cw + 1], scalar2=None, op0=ALU.mult),
                         reads=[('CU', ti % 2), 'prm'], writes=['cacc'])
                    S.op('dve', lambda e, cu=cu, cw=cw: e.scalar_tensor_tensor(out=cacc[:], in0=cu[:, 3:515], scalar=prm[:, cw + 1:cw + 2], in1=cacc[:], op0=ALU.mult, op1=ALU.add),
                         reads=[('CU', ti % 2), 'prm', 'cacc'], writes=['cacc'])
                    S.op('dve', lambda e, cu=cu, cw=cw: e.scalar_tensor_tensor(out=cacc[:], in0=cu[:, 4:516], scalar=prm[:, cw + 2:cw + 3], in1=cacc[:], op0=ALU.mult, op1=ALU.add),
                         reads=[('CU', ti % 2), 'prm', 'cacc'], writes=['cacc'])
                    if i == 0 and ti == 0:
                        dump(10, cacc, 'cacc'); dump(11, Bc, 'Bc')
                    S.op('dve', lambda e, i=i, c0=c0: e.tensor_tensor(out=YA[:, i, c0:c0 + 512], in0=Bc[:], in1=cacc[:], op=ALU.mult),
                         reads=['Bc', 'cacc'], writes=['YA'])
            S.barrier()
            sub.close()
            sub = ExitStack()
            OPEN.append(sub)
            WOA = sb(sub, "WOA", [128, 4, D], BF16)
            WOB = sb(sub, "WOB", [128, 4, D], BF16)
            S.dma('pool', lambda e: e.dma_start(out=WOA[:], in_=woa.rearrange("(j p) d -> p j d", p=128)), writes=['WOA'], sem='woa')
            S.dma('pool', lambda e: e.dma_start(out=WOB[:], in_=wob.rearrange("(j p) d -> p j d", p=128)), writes=['WOB'], sem='wob')
            wgc = [sb(sub, "wgc%d" % i, [128, 8, 2, 128], BF16) for i in range(2)]
            ga, gb_, m1 = [sb(sub, "gm%d" % i, [128, 512], F32) for i in range(3)]
            for dcx in range(8):
                ws = dcx % 2
                for q in range(2):
                    col = 3328 + q * 1024 + dcx * 128
                    S.dma('pool', lambda e, ws=ws, q=q, col=col: e.dma_start(out=wgc[ws][:, :, q, :], in_=winv[:, :, col:col + 128]),
                          writes=[('wgc', ws)], sem='wgc%d' % ws)
                for ti in range(4):
                    c0 = ti * 512
                    pgs = []
                    for q in range(2):
                        pb, pr = bank()
                        for dc in range(8):
                            S.op('pe', lambda e, ws=ws, q=q, dc=dc, c0=c0, pb=pb: e.matmul(pb[:, :], lhsT=wgc[ws][:, dc, q, :], rhs=xn[:, dc, c0:c0 + 512], start=(dc == 0), stop=(dc == 7)),
                                 reads=[('wgc', ws), ('xn', c0)], writes=[pr], signal=(dc == 7))
                        pgs.append((pb, pr))
                    S.op('act', lambda e, pb=pgs[0][0]: e.activation(out=ga[:], in_=pb[:, :], func=AF.Sigmoid), reads=[pgs[0][1]], writes=['ga'])
                    S.op('act', lambda e, pb=pgs[1][0]: e.activation(out=gb_[:], in_=pb[:, :], func=AF.Sigmoid), reads=[pgs[1][1]], writes=['gb'])
                    pya, pyar = bank()
                    pyb, pybr = bank()
                    for j in range(4):
                        S.op('pe', lambda e, j=j, dcx=dcx, c0=c0, pya=pya: e.matmul(pya[:, :], lhsT=WOA[:, j, dcx * 128:(dcx + 1) * 128], rhs=YA[:, j, c0:c0 + 512], start=(j == 0), stop=(j == 3)),
                             reads=['WOA', 'YA'], writes=[pyar], signal=(j == 3))
                    for j in range(4):
                        S.op('pe', lambda e, j=j, dcx=dcx, c0=c0, pyb=pyb: e.matmul(pyb[:, :], lhsT=WOB[:, j, dcx * 128:(dcx + 1) * 128], rhs=YG[:, j, c0:c0 + 512], start=(j == 0), stop=(j == 3)),
                             reads=['WOB', 'YG'], writes=[pybr], signal=(j == 3))
                    S.op('dve', lambda e, pya=pya: e.tensor_tensor(out=m1[:], in0=ga[:], in1=pya[:, :], op=ALU.mult), reads=['ga', pyar], writes=['m1'])
                    S.op('dve', lambda e, pyb=pyb: e.tensor_tensor(out=gb_[:], in0=gb_[:], in1=pyb[:, :], op=ALU.mult), reads=['gb', pybr], writes=['gb'])
                    if dcx == 0 and ti == 0:
                        dump(12, m1, 'm1'); dump(13, gb_, 'gb')
                    S.op('dve', lambda e, dcx=dcx, c0=c0: e.tensor_tensor(out=MG[:, dcx, c0:c0 + 512], in0=m1[:], in1=gb_[:], op=ALU.add), reads=['m1', 'gb'], writes=['MG'])
            S.barrier()
            sub.close()
            sub = ExitStack()
            OPEN.append(sub)
            WO = sb(sub, "WO", [128, 8, D], BF16)
            S.dma('pool', lambda e: e.dma_start(out=WO[:], in_=wo.rearrange("(j p) d -> p j d", p=128)), writes=['WO'], sem='wo')
            for dcx in range(8):
                for ti in range(4):
                    c0 = ti * 512
                    pb, pr = bank()
                    for j in range(8):
                        S.op('pe', lambda e, j=j, dcx=dcx, c0=c0, pb=pb: e.matmul(pb[:, :], lhsT=WO[:, j, dcx * 128:(dcx + 1) * 128], rhs=MG[:, j, c0:c0 + 512], start=(j == 0), stop=(j == 7)),
                             reads=['WO', 'MG'], writes=[pr], signal=(j == 7))
                    S.op('dve', lambda e, dcx=dcx, c0=c0, pb=pb: e.tensor_tensor(out=xT[:, dcx, c0:c0 + 512], in0=xT[:, dcx, c0:c0 + 512], in1=pb[:, :], op=ALU.add),
                         reads=[pr, ('xT', dcx, c0)], writes=[('xT', dcx, c0)])
            dump(9, None, ('xT', 0, 0), ap=xT[:, 0, 0:512])
            S.barrier()
            sub.close()
            outer.close()

        try:
            for seg in range(NSEG - nseg, NSEG):
                full = (seg == NSEG - 1)
                load_segment(seg)
                if do_ffn1:
                    ffn("f1", PC_F1N, w1g, w1u, w1d, TILES[:4])
                    S.op('dve', lambda e: e.tensor_copy(out=XH[:], in_=xT[:, :, NT - 2:NT]), reads=[('xT', dc, 1536) for dc in range(8)], writes=['XH'])
                if do_mix:
                    mixer(full)
            if do_ffn2:
                ffn("f2", PC_F2N, w2g, w2u, w2d, TILES[:4])
        except _Stop:
            pass
        S.enabled = True
        S.barrier()
        store_output()
        S.emit()
    return nc


def make_consts():
    c = np.zeros((128, 128 * 3 + 64 + 256 * 4 + 512), np.float32)
    c[:, 0:128] = np.eye(128)
    c[:, 128:256] = 1.0
    blk = np.zeros((128, 128), np.float32)
    blk[0:64, 0:64] = 1.0
    blk[64:128, 64:128] = 1.0
    c[:, 256:384] = blk
    i64 = np.eye(64, dtype=np.float32)
    id2 = np.concatenate([i64, i64], 0)
    c[:, 384:448] = id2
    c[:, 448:704] = np.tile(id2, (1, 4))
    su = np.triu(np.ones((64, 64), np.float32), 1)
    iu = np.triu(np.ones((64, 64), np.float32), 0)
    sl = su.T
    for k, m in enumerate((su, iu, sl)):
        c[:, 704 + k * 256:704 + (k + 1) * 256] = np.tile(np.concatenate([m, m], 0), (1, 4))
    rm = np.ones((128, 512), np.float32)
    rm[:, 0::64] = 0.0
    c[:, 1472:1984] = rm
    return c


def make_prm(i):
    p = np.zeros((128, PC_N), np.float32)

    def cols(v, n):
        return np.ascontiguousarray(np.asarray(v, np.float32).reshape(n, 128).T)
    p[:, PC_F1N:PC_F1N + 8] = cols(i["ffn1_norm"][0], 8)
    p[:, PC_MXN:PC_MXN + 8] = cols(i["mix_norm"][0], 8)
    p[:, PC_F2N:PC_F2N + 8] = cols(i["ffn2_norm"][0], 8)
    p[:, PC_FIN:PC_FIN + 8] = cols(i["final_norm"], 8)
    cw = np.asarray(i["conv_w"][0], np.float32)
    for ch in range(4):
        for j in range(3):
            p[:, PC_CONV + ch * 3 + j] = cw[j, ch * 128:(ch + 1) * 128]
    p[:, PC_MU:PC_MU + 14] = cols(i["mu_b"][0], 14)
    p[:, PC_W0:PC_W0 + 4] = cols(i["w0"][0], 4)
    p[:, PC_A0:PC_A0 + 4] = cols(i["a0"][0], 4)
    p[:, PC_KK:PC_KK + 4] = cols(i["k_k"][0], 4)
    p[:, PC_KA:PC_KA + 4] = cols(i["k_a"][0], 4)
    p[:, PC_RK:PC_RK + 4] = cols(np.asarray(i["r_k"][0]).reshape(512), 4)
    p[:, PC_LNW:PC_LNW + 4] = cols(i["ln_x_w"][0], 4)
    p[:, PC_LNB:PC_LNB + 4] = cols(i["ln_x_b"][0], 4)
    return p


def make_in_maps(inputs):
    i = {k: np.asarray(v) for k, v in inputs.items()}
    x = np.asarray(i["x"], np.float32)
    shared = {
        "prm": make_prm(i), "cst": make_consts(),
        "w1g": np.ascontiguousarray(i["ffn1_w_gate"][0]), "w1u": np.ascontiguousarray(i["ffn1_w_up"][0]), "w1d": np.ascontiguousarray(i["ffn1_w_down"][0]),
        "w2g": np.ascontiguousarray(i["ffn2_w_gate"][0]), "w2u": np.ascontiguousarray(i["ffn2_w_up"][0]), "w2d": np.ascontiguousarray(i["ffn2_w_down"][0]),
        "win": np.ascontiguousarray(i["w_in"][0]), "woa": np.ascontiguousarray(i["w_out_a"][0]), "wob": np.ascontiguousarray(i["w_out_b"][0]),
        "wo": np.ascontiguousarray(i["w_o"][0]), "wdu": np.ascontiguousarray(i["w_decay_up"][0]), "wiu": np.ascontiguousarray(i["w_iclr_up"][0]),
        "wgu": np.ascontiguousarray(i["w_gate_up"][0]),
    }
    maps = []
    for core in range(8):
        b, qq = core // 4, core % 4
        x4 = np.zeros((NSEG, NTH, D), np.float32)
        for s in range(NSEG):
            src = qq - (NSEG - 1 - s)
            if src < 0:
                continue
            x4[s, 0:NT] = x[b, src * NT:(src + 1) * NT]
            if src > 0:
                x4[s, NT:NTH] = x[b, src * NT - 2:src * NT]
        m = dict(shared)
        m["x4"] = x4
        maps.append(m)
    return maps


_NC = {}


def kernel(**inputs):
    if "nc" not in _NC:
        _NC["nc"] = build_nc()
    maps = make_in_maps(inputs)
    res = run_bass_kernel_spmd(_NC["nc"], maps, core_ids=list(range(8)))
    out = np.zeros((2, 8192, D), np.float32)
    for core in range(8):
        b, qq = core // 4, core % 4
        out[b, qq * NT:(qq + 1) * NT] = res.results[core]["out"]
    return out
```
